# Optimizing a Trainium2 kernel written in Bass

```python
import math
import jax, jax.numpy as jnp
from jax import lax
import numpy as np

D_MODEL = 2048
BATCH = 4
SEQ = 8192
DEPTH = 1

CTX_LEN = 256
GRID_W = 64

ATTN_HEADS = 8
ATTN_HEAD_DIM = 64
ATTN_VALUE_DIM = 2 * ATTN_HEAD_DIM
ATTN_WIDTH = ATTN_HEADS * ATTN_VALUE_DIM
QK_WIDTH = 2 * ATTN_HEADS * ATTN_HEAD_DIM
ROPE_THETA = 10000.0
Q_BLOCK = 128

POOL_WINDOWS = (2, 4, 8, 16)
POOL_GROUPS = len(POOL_WINDOWS)
POOL_WIDTH = D_MODEL // 2
POOL_GROUP_DIM = POOL_WIDTH // POOL_GROUPS

N_BRANCHES = 2
D_FF = 4 * D_MODEL
EPS = 1e-6

Q_OFF = 0
K_OFF = Q_OFF + QK_WIDTH
V_OFF = K_OFF + QK_WIDTH
P_OFF = V_OFF + ATTN_WIDTH
G_OFF = P_OFF + POOL_WIDTH
IN_WIDTH = G_OFF + N_BRANCHES * D_MODEL

kernel_name = "hybrid_diffattn_pool_dit_layer"


def rmsnorm(x, w):
    xf = x.astype(jnp.float32)
    y = xf * lax.rsqrt(jnp.mean(xf * xf, axis=-1, keepdims=True) + EPS)
    return (y * w.astype(jnp.float32)).astype(x.dtype)


def modulate(x, w, shift, scale):
    return rmsnorm(x, w) * (1 + scale) + shift


def split_qk(t):
    B, L, _ = t.shape
    return t.reshape(B, L, 2 * ATTN_HEADS, ATTN_HEAD_DIM).transpose(0, 2, 1, 3)


def split_v(t):
    B, L, _ = t.shape
    return t.reshape(B, L, ATTN_HEADS, ATTN_VALUE_DIM).transpose(0, 2, 1, 3)


def axial_rope(x, row, col):
    half = x.shape[-1] // 2
    inv_freq = ROPE_THETA ** (-jnp.arange(0, half, 2, dtype=jnp.float32) / half)

    def rot(xa, pos):
        ang = pos.astype(jnp.float32)[:, None] * inv_freq[None, :]
        cos, sin = jnp.cos(ang), jnp.sin(ang)
        x1, x2 = xa[..., : half // 2], xa[..., half // 2:]
        return jnp.concatenate([x1 * cos - x2 * sin, x1 * sin + x2 * cos], axis=-1)

    xf = x.astype(jnp.float32)
    out = jnp.concatenate([rot(xf[..., :half], row), rot(xf[..., half:], col)], axis=-1)
    return out.astype(x.dtype)


def diff_attention(q, k, v, lam):
    B, _, Lq, dh = q.shape
    Lk = k.shape[2]
    nblk = Lq // Q_BLOCK
    qb = q.reshape(B, 2 * ATTN_HEADS, nblk, Q_BLOCK, dh).transpose(2, 0, 1, 3, 4)
    scale = dh ** -0.5

    def block(qi):
        s = jnp.einsum('bhqd,bhkd->bhqk', qi, k, preferred_element_type=jnp.float32) * scale
        p = jax.nn.softmax(s, axis=-1).reshape(B, ATTN_HEADS, 2, Q_BLOCK, Lk)
        a = (p[:, :, 0] - lam * p[:, :, 1]).astype(v.dtype)
        return jnp.einsum('bhqk,bhkd->bqhd', a, v)

    o = lax.map(block, qb)
    return o.transpose(1, 0, 2, 3, 4).reshape(B, Lq, ATTN_HEADS, ATTN_VALUE_DIM)


def multiscale_pool(u, pool_w, pool_scale):
    B, L, _ = u.shape
    uf = u.astype(jnp.float32)
    cs = jnp.pad(jnp.cumsum(uf, axis=1), ((0, 0), (1, 0), (0, 0)))
    t = jnp.arange(L)
    outs = []
    for g, w in enumerate(POOL_WINDOWS):
        sl = slice(g * POOL_GROUP_DIM, (g + 1) * POOL_GROUP_DIM)
        lo = jnp.clip(t - w // 2, 0, L)
        hi = jnp.clip(t + w - w // 2, 0, L)
        csg = cs[..., sl]
        mean = (csg[:, hi] - csg[:, lo]) / (hi - lo).astype(jnp.float32)[None, :, None]
        outs.append(mean - uf[..., sl])
    d = jnp.stack(outs, axis=2).astype(u.dtype)
    y = jnp.einsum('blgc,gcd->blgd', d, pool_w).reshape(B, L, POOL_WIDTH)
    return y * pool_scale


def token_mixer(q, k_all, v_all, pool_in, gate_logits, lam, lam_init, subln_w,
                pool_w, pool_scale, w_a_up, w_b_up, w_o):
    B, L = pool_in.shape[:2]
    heads = diff_attention(q, k_all, v_all, lam)
    heads = rmsnorm(heads, subln_w) * (1.0 - lam_init)
    y_a = heads.reshape(B, L, ATTN_WIDTH) @ w_a_up
    y_b = multiscale_pool(pool_in, pool_w, pool_scale) @ w_b_up
    g = jax.nn.sigmoid(gate_logits.astype(jnp.float32)).astype(y_a.dtype)
    g_a, g_b = g[..., :D_MODEL], g[..., D_MODEL:]
    return (g_a * y_a + g_b * y_b) @ w_o


def sq_relu_mlp(h, w1, w2):
    return jnp.square(jax.nn.relu(h @ w1)) @ w2


def setup_inputs(seed: int = 0) -> dict:
    key = jax.random.key(seed)
    ks = jax.random.split(key, 24)
    f32 = jnp.float32

    def nrm(k, shape, scale):
        return jax.random.normal(k, shape, f32) * scale

    return {
        "x": nrm(ks[0], (BATCH, SEQ, D_MODEL), 1.0),
        "c": nrm(ks[1], (BATCH, D_MODEL), 1.0),
        "ctx": nrm(ks[2], (BATCH, CTX_LEN, D_MODEL), 1.0),
        "c_ctx": nrm(ks[3], (D_MODEL,), 1.0),
        "w_mod": nrm(ks[4], (DEPTH, D_MODEL, 6 * D_MODEL), D_MODEL ** -0.5),
        "b_mod": nrm(ks[5], (DEPTH, 6 * D_MODEL), 0.02),
        "norm_attn_w": 1.0 + nrm(ks[6], (DEPTH, D_MODEL), 0.1),
        "w_in": nrm(ks[7], (DEPTH, D_MODEL, IN_WIDTH), D_MODEL ** -0.5),
        "q_norm_w": 1.0 + nrm(ks[8], (DEPTH, ATTN_HEAD_DIM), 0.1),
        "k_norm_w": 1.0 + nrm(ks[9], (DEPTH, ATTN_HEAD_DIM), 0.1),
        "lambda_q1": nrm(ks[10], (DEPTH, ATTN_HEAD_DIM), 0.1),
        "lambda_k1": nrm(ks[11], (DEPTH, ATTN_HEAD_DIM), 0.1),
        "lambda_q2": nrm(ks[12], (DEPTH, ATTN_HEAD_DIM), 0.1),
        "lambda_k2": nrm(ks[13], (DEPTH, ATTN_HEAD_DIM), 0.1),
        "subln_w": 1.0 + nrm(ks[14], (DEPTH, ATTN_VALUE_DIM), 0.1),
        "pool_w": nrm(ks[15], (DEPTH, POOL_GROUPS, POOL_GROUP_DIM, POOL_GROUP_DIM), POOL_GROUP_DIM ** -0.5),
        "pool_scale": 1.0 + nrm(ks[16], (DEPTH, POOL_WIDTH), 0.1),
        "w_a_up": nrm(ks[17], (DEPTH, ATTN_WIDTH, D_MODEL), ATTN_WIDTH ** -0.5),
        "w_b_up": nrm(ks[18], (DEPTH, POOL_WIDTH, D_MODEL), POOL_WIDTH ** -0.5),
        "w_o": nrm(ks[19], (DEPTH, D_MODEL, D_MODEL), D_MODEL ** -0.5),
        "norm_mlp_w": 1.0 + nrm(ks[20], (DEPTH, D_MODEL), 0.1),
        "w_ff1": nrm(ks[21], (DEPTH, D_MODEL, D_FF), D_MODEL ** -0.5),
        "w_ff2": nrm(ks[22], (DEPTH, D_FF, D_MODEL), D_FF ** -0.5),
    }


def reference(x, c, ctx, c_ctx, w_mod, b_mod, norm_attn_w, w_in, q_norm_w, k_norm_w,
              lambda_q1, lambda_k1, lambda_q2, lambda_k2, subln_w, pool_w, pool_scale,
              w_a_up, w_b_up, w_o, norm_mlp_w, w_ff1, w_ff2):
    L = x.shape[1]
    rows = L // GRID_W
    row = jnp.repeat(jnp.arange(rows), GRID_W)
    col = jnp.tile(jnp.arange(GRID_W), rows)

    x_ctx = ctx
    for l in range(DEPTH):
        last = l == DEPTH - 1
        lam_init = 0.8 - 0.6 * math.exp(-0.3 * l)
        lam = (jnp.exp(jnp.sum(lambda_q1[l].astype(jnp.float32) * lambda_k1[l].astype(jnp.float32)))
               - jnp.exp(jnp.sum(lambda_q2[l].astype(jnp.float32) * lambda_k2[l].astype(jnp.float32)))
               + lam_init)

        mod = (jax.nn.silu(c) @ w_mod[l] + b_mod[l])[:, None, :]
        sa, ca, ga, sm, cm, gm = jnp.split(mod, 6, axis=-1)
        mod_c = jax.nn.silu(c_ctx) @ w_mod[l] + b_mod[l]
        sa_c, ca_c, ga_c, sm_c, cm_c, gm_c = jnp.split(mod_c, 6, axis=-1)

        h_c = modulate(x_ctx, norm_attn_w[l], sa_c, ca_c)
        if last:
            p_c_kv = h_c @ w_in[l][:, K_OFF:P_OFF]
        else:
            p_c = h_c @ w_in[l]
            p_c_kv = p_c[..., K_OFF:P_OFF]
        k_c = rmsnorm(split_qk(p_c_kv[..., :QK_WIDTH]), k_norm_w[l])
        v_c = split_v(p_c_kv[..., QK_WIDTH:])

        h = modulate(x, norm_attn_w[l], sa, ca)
        p = h @ w_in[l]
        q = axial_rope(rmsnorm(split_qk(p[..., Q_OFF:K_OFF]), q_norm_w[l]), row, col)
        k = axial_rope(rmsnorm(split_qk(p[..., K_OFF:V_OFF]), k_norm_w[l]), row, col)
        v = split_v(p[..., V_OFF:P_OFF])
        k_all = jnp.concatenate([k, k_c], axis=2)
        v_all = jnp.concatenate([v, v_c], axis=2)
        mix = token_mixer(q, k_all, v_all, p[..., P_OFF:G_OFF], p[..., G_OFF:], lam, lam_init,
                          subln_w[l], pool_w[l], pool_scale[l], w_a_up[l], w_b_up[l], w_o[l])
        x = x + ga * mix
        x = x + gm * sq_relu_mlp(modulate(x, norm_mlp_w[l], sm, cm), w_ff1[l], w_ff2[l])

        if not last:
            q_c = rmsnorm(split_qk(p_c[..., Q_OFF:K_OFF]), q_norm_w[l])
            mix_c = token_mixer(q_c, k_c, v_c, p_c[..., P_OFF:G_OFF], p_c[..., G_OFF:], lam, lam_init,
                                subln_w[l], pool_w[l], pool_scale[l], w_a_up[l], w_b_up[l], w_o[l])
            x_ctx = x_ctx + ga_c * mix_c
            x_ctx = x_ctx + gm_c * sq_relu_mlp(modulate(x_ctx, norm_mlp_w[l], sm_c, cm_c),
                                               w_ff1[l], w_ff2[l])
    return x
```

```python
import os
import numpy as np
import ml_dtypes
from contextlib import ExitStack
import concourse.bass as bass
import concourse.mybir as mybir
from concourse.bass_utils import run_bass_kernel_spmd

F32 = mybir.dt.float32
BF16 = mybir.dt.bfloat16
U8 = mybir.dt.uint8
AF = mybir.ActivationFunctionType
ALU = mybir.AluOpType
AX = mybir.AxisListType

ENGS = ("pe", "act", "dve", "pool", "sp")
SEM_ROT = 30000
D = 2048
KD = 16
EPS = 1e-6
GRID_W = 64
POOL_WINDOWS = (2, 4, 8, 16)


class Op:
    __slots__ = ("eng", "fn", "deps", "sig", "sem", "val", "is_dma")

    def __init__(self, eng, fn, deps, is_dma=False):
        self.eng = eng
        self.fn = fn
        self.deps = deps
        self.sig = False
        self.sem = None
        self.val = None
        self.is_dma = is_dma


class Buf:
    __slots__ = ("name", "w", "r", "rd", "excl")

    def __init__(self, name="", excl=False):
        self.name = name
        self.excl = excl
        self.w = []
        self.r = {}
        self.rd = []


class Sched:
    def __init__(self, nc, stack, n_dma_sems=8):
        self.nc = nc
        self.stack = stack
        self.ops = {e: [] for e in ENGS}
        self.nsem = 0
        self.dma_pool = {q: {"sems": [], "cnt": [], "last": [], "i": 0, "n": n_dma_sems} for q in ("sp", "pool")}

    def new_sem(self, name):
        self.nsem += 1
        return self.stack.enter_context(self.nc.semaphore(f"{name}{self.nsem}"))

    def _deps(self, eng, is_dma, reads, writes, extra):
        deps = list(extra)
        for b in reads:
            deps += b.w
        for b in writes:
            deps += b.w
            deps += list(b.r.values())
            deps += b.rd
        return [d for d in deps if d is not None]

    def _update(self, o, reads, writes, accumulate=False):
        for b in reads:
            if o.is_dma:
                b.rd.append(o)
            else:
                b.r[o.eng] = o
        for b in writes:
            if accumulate:
                b.w = b.w + [o]
            else:
                b.w = [o]
            b.r = {}
            b.rd = []

    @staticmethod
    def _split(reads, writes):
        ex = [b for b in reads if b.excl]
        if ex:
            reads = [b for b in reads if not b.excl]
            writes = list(writes) + ex
        return reads, writes

    def op(self, eng, fn, reads=(), writes=(), deps=()):
        reads, writes = self._split(reads, writes)
        o = Op(eng, fn, self._deps(eng, False, reads, writes, deps))
        self.ops[eng].append(o)
        self._update(o, reads, writes)
        return o

    def pe(self, fn, reads=(), writes=(), deps=()):
        return self.op("pe", fn, reads, writes, deps)

    def act(self, fn, reads=(), writes=(), deps=()):
        return self.op("act", fn, reads, writes, deps)

    def dve(self, fn, reads=(), writes=(), deps=()):
        return self.op("dve", fn, reads, writes, deps)

    def dma(self, q, out, in_, reads=(), writes=(), deps=(), accumulate=False, **kw):
        P = self.dma_pool[q]
        if len(P["sems"]) < P["n"]:
            P["sems"].append(self.new_sem(f"d{q}"))
            P["cnt"].append(0)
            P["last"].append(None)
            i = len(P["sems"]) - 1
        else:
            i = P["i"] % P["n"]
        P["i"] += 1
        if P["cnt"][i] + 16 > SEM_ROT:
            P["sems"][i] = self.new_sem(f"d{q}")
            P["cnt"][i] = 0
        reads, writes = self._split(reads, writes)
        d = self._deps(q, True, reads, writes, deps)
        if accumulate:
            prevw = set(id(x) for b in writes for x in b.w if x.is_dma)
            d = [x for x in d if id(x) not in prevw]
        if P["last"][i] is not None:
            d.append(P["last"][i])
        o = Op(q, lambda e, out=out, in_=in_, kw=kw: e.dma_start(out=out, in_=in_, **kw), d, is_dma=True)
        P["cnt"][i] += 16
        o.sem = P["sems"][i]
        o.val = P["cnt"][i]
        o.sig = True
        P["last"][i] = o
        self.ops[q].append(o)
        self._update(o, reads, writes, accumulate)
        return o

    def barrier(self):
        toks = []
        for e in ENGS:
            for o in reversed(self.ops[e]):
                if not o.is_dma and o.fn is not None:
                    toks.append(o)
                    break
        for q in self.dma_pool:
            toks += [o for o in self.dma_pool[q]["last"] if o is not None]
        for e in ENGS:
            o = Op(e, None, list(toks))
            self.ops[e].append(o)

    def emit(self, block):
        for e in ENGS:
            for o in self.ops[e]:
                for d in o.deps:
                    if not d.is_dma and not (d.eng == "pe" and o.eng == "pe" and not o.is_dma):
                        d.sig = True
        for e in ENGS:
            sem = None
            cnt = 0
            for o in self.ops[e]:
                if o.is_dma or not o.sig:
                    continue
                if sem is None or cnt >= SEM_ROT:
                    sem = self.new_sem(f"c{e}")
                    cnt = 0
                cnt += 1
                o.sem = sem
                o.val = cnt
        stats = {}

        def run(eng_name, e):
            waited = {}
            nw = 0
            for o in self.ops[eng_name]:
                for d in o.deps:
                    if d.eng == "pe" and eng_name == "pe" and not d.is_dma and not o.is_dma:
                        continue
                    k = id(d.sem)
                    if waited.get(k, 0) >= d.val:
                        continue
                    waited[k] = d.val
                    e.wait_ge(d.sem, d.val)
                    nw += 1
                if o.fn is None:
                    continue
                ins = o.fn(e)
                if o.sig:
                    ins.then_inc(o.sem, 16 if o.is_dma else 1)
            stats[eng_name] = (len(self.ops[eng_name]), nw)

        @block.tensor
        def _(e):
            run("pe", e)

        @block.scalar
        def _(e):
            run("act", e)

        @block.vector
        def _(e):
            run("dve", e)

        @block.gpsimd
        def _(e):
            run("pool", e)

        @block.sync
        def _(e):
            run("sp", e)

        return stats


class Arena:
    def __init__(self, tensor, nbytes):
        self.t = tensor
        self.n = nbytes
        self.off = 0
        self.marks = []
        self.peak = 0

    def mark(self):
        self.marks.append(self.off)

    def release(self):
        self.off = self.marks.pop()

    def alloc(self, free_shape, dt):
        esz = {F32: 4, BF16: 2, U8: 1}[dt]
        n = int(np.prod(free_shape)) * esz
        assert self.off + n <= self.n, f"arena overflow {self.off}+{n}>{self.n}"
        a = self.t[:, self.off:self.off + n]
        if dt != U8:
            a = a.bitcast(dt)
        self.off += (n + 63) // 64 * 64
        self.peak = max(self.peak, self.off)
        if len(free_shape) > 1:
            names = " ".join(f"d{i}" for i in range(len(free_shape)))
            kw = {f"d{i}": int(free_shape[i]) for i in range(len(free_shape))}
            a = a.rearrange(f"p ({names}) -> p {names}", **kw)
        return a


class _Stop(Exception):
    pass


def build(L, C, debug=False, stop=99):
    T = L // 2
    NT = T // 128
    NO = (L - T) // 128
    NCT = C // 128
    NKT = NT + NO + NCT
    LK = NKT * 128
    NS = T // 512
    NQ = T // 512
    nc = bass.Bass("TRN2", target_bir_lowering=False)

    def din(name, shape, dt=F32):
        return nc.dram_tensor(name, list(shape), dt, kind="ExternalInput").ap()

    xo = din("xo", [T, D]); xr = din("xr", [L - T, D]); xh = din("xh", [256, D]); ctx = din("ctx", [C, D])
    cvec = din("cvec", [D]); cctx = din("cctx", [D])
    w_mod = din("w_mod", [D, 6 * D]); b_mod = din("b_mod", [6 * D]); naw_d = din("norm_attn_w", [D])
    w_in = din("w_in", [D, 8192]); qnw = din("q_norm_w", [64]); knw = din("k_norm_w", [64])
    lq1 = din("lambda_q1", [64]); lk1 = din("lambda_k1", [64]); lq2 = din("lambda_q2", [64]); lk2 = din("lambda_k2", [64])
    subln = din("subln_w", [128]); pool_w = din("pool_w", [4, 256, 256]); pool_scale = din("pool_scale", [1024])
    w_a_up = din("w_a_up", [1024, D]); w_b_up = din("w_b_up", [1024, D]); w_o = din("w_o", [D, D])
    nmw_d = din("norm_mlp_w", [D]); w_ff1 = din("w_ff1", [D, 4 * D]); w_ff2 = din("w_ff2", [4 * D, D])
    ident_d = din("ident", [128, 128], BF16); rope_d = din("rope", [NKT, 128, 128]); bands_d = din("bands", [128, 36 * 128], BF16)
    y = nc.dram_tensor("y", [T, D], F32, kind="ExternalOutput").ap()
    sk = dict(kind="ExternalOutput") if debug else {}
    KTs = nc.dram_tensor("KTs", [8, 128, LK], BF16, **sk).ap()
    Vs = nc.dram_tensor("Vs", [LK, 1024], BF16, **sk).ap()
    QTs = nc.dram_tensor("QTs", [8, 128, T], BF16, **sk).ap()
    Us = nc.dram_tensor("Us", [(NT + 2) * 128, 1024], BF16, **sk).ap()
    HTs = nc.dram_tensor("HTs", [8, 128, T], BF16, **sk).ap()
    KTs_b = [Buf() for _ in range(NKT)]; Vs_b = [Buf() for _ in range(NKT)]
    QTs_b = [Buf() for _ in range(NT)]; Us_b = [Buf() for _ in range(NT + 2)]
    HTs_b = [[Buf() for _ in range(NQ)] for _ in range(8)]
    dbg = {}

    with ExitStack() as st:
        ARENA_BYTES = 200 * 1024
        arena_t = st.enter_context(nc.sbuf_tensor("arena", [128, ARENA_BYTES], U8))
        psA = st.enter_context(nc.psum_tensor("psA", [128, 2048], F32))
        psB = st.enter_context(nc.psum_tensor("psB", [128, 2048], F32))
        S = Sched(nc, st)
        A = Arena(arena_t, ARENA_BYTES)
        PA = [Buf(f"PA{i}", excl=True) for i in range(4)]
        PB = [Buf(f"PB{i}", excl=True) for i in range(4)]

        def pa(i):
            return psA[:, i * 512:(i + 1) * 512]

        def pb(i):
            return psB[:, i * 512:(i + 1) * 512]


        ident = A.alloc([128], BF16); ident_b = Buf()
        bands = A.alloc([36, 128], BF16); bands_b = Buf()
        A1 = A.alloc([16], F32); B1 = A.alloc([16], F32); A1c = A.alloc([16], F32); B1c = A.alloc([16], F32)
        A2 = A.alloc([16], F32); B2 = A.alloc([16], F32)
        modv_b = Buf()
        GArow = A.alloc([D], F32); GMrow = A.alloc([D], F32); grow_b = Buf()
        wq_bc = A.alloc([64], F32); wk_bc = A.alloc([64], F32); wqk_b = Buf()
        swb = A.alloc([128], F32); swb_b = Buf()
        pscale = A.alloc([8], F32); pscale_b = Buf()
        neglam = A.alloc([1], F32); nshift = A.alloc([1], F32); epsc = A.alloc([1], F32); misc_b = Buf()

        S.dma("sp", ident, ident_d, writes=[ident_b])
        S.dma("sp", bands.rearrange("p a b -> p (a b)"), bands_d, writes=[bands_b])
        S.dve(lambda e: e.memset(epsc, EPS), writes=[misc_b])

        try:
            A.mark()
            naw = A.alloc([16], F32); nmw = A.alloc([16], F32); bcol = A.alloc([6, 16], F32)
            c0 = A.alloc([16], F32); c1 = A.alloc([16], F32)
            scb = A.alloc([16, 2], BF16); rep = A.alloc([16, 128], BF16)
            lam4 = A.alloc([4, 64], F32); lamp = A.alloc([2, 64], F32); lams = A.alloc([2], F32); lame = A.alloc([2], F32)
            sbc = A.alloc([128], F32); wsq = A.alloc([2, 64], F32); wmx = A.alloc([2], F32)
            cols = A.alloc([4, 16, 2], F32)
            brow = A.alloc([2, D], F32)
            tmp16 = A.alloc([16], F32)
            p0_b = Buf()
            NCD = dict(allow_slow_non_contiguous=True)
            S.dma("sp", naw, naw_d.rearrange("(k p) -> p k", p=128), writes=[p0_b], accumulate=True, **NCD)
            S.dma("sp", nmw, nmw_d.rearrange("(k p) -> p k", p=128), writes=[p0_b], accumulate=True, **NCD)
            for j in range(6):
                S.dma("sp", bcol[:, j, :], b_mod[j * D:(j + 1) * D].rearrange("(k p) -> p k", p=128), writes=[p0_b], accumulate=True, **NCD)
            S.dma("sp", c0, cvec.rearrange("(k p) -> p k", p=128), writes=[p0_b], accumulate=True, **NCD)
            S.dma("sp", c1, cctx.rearrange("(k p) -> p k", p=128), writes=[p0_b], accumulate=True, **NCD)
            S.dma("sp", pscale, pool_scale.rearrange("(k p) -> p k", p=128), writes=[pscale_b], **NCD)
            S.dma("sp", wq_bc, qnw.partition_broadcast(128), writes=[wqk_b], accumulate=True)
            S.dma("sp", wk_bc, knw.partition_broadcast(128), writes=[wqk_b], accumulate=True)
            for i, lv in enumerate((lq1, lk1, lq2, lk2)):
                S.dma("sp", lam4[:, i, :], lv.partition_broadcast(128), writes=[p0_b], accumulate=True)
            S.dma("sp", sbc, subln.partition_broadcast(128), writes=[p0_b], accumulate=True)
            S.dma("sp", brow[:, 0, :], b_mod[2 * D:3 * D].partition_broadcast(128), writes=[p0_b], accumulate=True)
            S.dma("sp", brow[:, 1, :], b_mod[5 * D:6 * D].partition_broadcast(128), writes=[p0_b], accumulate=True)
            sc_b = Buf()
            S.act(lambda e: e.activation(out=scb[:, :, 0], in_=c0, func=AF.Silu), reads=[p0_b], writes=[sc_b])
            S.act(lambda e: e.activation(out=scb[:, :, 1], in_=c1, func=AF.Silu), reads=[p0_b, sc_b], writes=[sc_b])
            rep_b = Buf()
            S.dve(lambda e: e.tensor_copy(out=rep, in_=scb[:, :, 0:1].to_broadcast([128, 16, 128])), reads=[sc_b], writes=[rep_b])
            S.dve(lambda e: e.tensor_tensor(out=lamp[:, 0, :], in0=lam4[:, 0, :], in1=lam4[:, 1, :], op=ALU.mult), reads=[p0_b], writes=[misc_b])
            S.dve(lambda e: e.tensor_tensor(out=lamp[:, 1, :], in0=lam4[:, 2, :], in1=lam4[:, 3, :], op=ALU.mult), reads=[misc_b], writes=[misc_b])
            S.dve(lambda e: e.tensor_reduce(out=lams, in_=lamp, axis=AX.X, op=ALU.add), reads=[misc_b], writes=[misc_b])
            S.act(lambda e: e.activation(out=lame, in_=lams, func=AF.Exp), reads=[misc_b], writes=[misc_b])
            S.dve(lambda e: e.tensor_tensor(out=neglam, in0=lame[:, 1:2], in1=lame[:, 0:1], op=ALU.subtract), reads=[misc_b], writes=[misc_b])
            S.dve(lambda e: e.tensor_scalar(out=neglam, in0=neglam, scalar1=-0.2, scalar2=None, op0=ALU.add), reads=[misc_b], writes=[misc_b])
            S.dve(lambda e: e.tensor_tensor(out=wsq[:, 0, :], in0=wq_bc, in1=wq_bc, op=ALU.mult), reads=[wqk_b, misc_b], writes=[misc_b])
            S.dve(lambda e: e.tensor_tensor(out=wsq[:, 1, :], in0=wk_bc, in1=wk_bc, op=ALU.mult), reads=[wqk_b, misc_b], writes=[misc_b])
            S.dve(lambda e: e.tensor_reduce(out=wmx[:, 0:1], in_=wsq.rearrange("p a b -> p (a b)"), axis=AX.X, op=ALU.max), reads=[misc_b], writes=[misc_b])
            S.dve(lambda e: e.tensor_scalar(out=nshift, in0=wmx[:, 0:1], scalar1=-8.0, scalar2=None, op0=ALU.mult), reads=[misc_b], writes=[misc_b])
            S.dve(lambda e: e.tensor_scalar(out=swb, in0=sbc, scalar1=0.8, scalar2=None, op0=ALU.mult), reads=[p0_b], writes=[swb_b])

            wst = [A.alloc([16, 512], BF16) for _ in range(2)]
            wst_b = [Buf() for _ in range(2)]
            colmap = {0: 0, 1: 1, 3: 2, 4: 3}
            n = 0
            for j in range(6):
                for tq in range(4):
                    wb = wst[n % 2]; wbb = wst_b[n % 2]; n += 1
                    c0_ = j * D + tq * 512
                    S.dma("pool", wb, w_mod[:, c0_:c0_ + 512].rearrange("(k p) n -> p k n", p=128), writes=[wbb])
                    if j in colmap:
                        jj = colmap[j]
                        for m in range(4):
                            mm_ = tq * 4 + m
                            o_ap = psB[:, (jj * 16 + mm_) * 2:(jj * 16 + mm_) * 2 + 2]
                            for kc in range(KD):
                                S.pe(lambda e, o_ap=o_ap, wb=wb, m=m, kc=kc: e.matmul(o_ap, lhsT=wb[:, kc, m * 128:(m + 1) * 128], rhs=scb[:, kc, :], start=(kc == 0), stop=(kc == KD - 1)),
                                     reads=[wbb, sc_b], writes=[PB[0]])
                    else:
                        bank = PA[tq]
                        for kc in range(KD):
                            S.pe(lambda e, tq=tq, wb=wb, kc=kc: e.matmul(pa(tq), lhsT=rep[:, kc, :], rhs=wb[:, kc, :], start=(kc == 0), stop=(kc == KD - 1)),
                                 reads=[wbb, rep_b], writes=[bank])
                        row = GArow if j == 2 else GMrow
                        bi = 0 if j == 2 else 1
                        S.dve(lambda e, row=row, tq=tq, bi=bi: e.tensor_tensor(out=row[:, tq * 512:(tq + 1) * 512], in0=pa(tq), in1=brow[:, bi, tq * 512:(tq + 1) * 512], op=ALU.add),
                              reads=[bank, p0_b], writes=[grow_b])
            S.dve(lambda e: e.tensor_copy(out=cols.rearrange("p a b c -> p (a b c)"), in_=psB[:, 0:128]), reads=[PB[0]], writes=[modv_b])
            def mk_scale(dst, jj, which, bj, nw_):
                S.dve(lambda e: e.tensor_tensor(out=tmp16, in0=cols[:, jj, :, which], in1=bcol[:, bj, :], op=ALU.add), reads=[modv_b, p0_b], writes=[modv_b])
                S.dve(lambda e: e.scalar_tensor_tensor(out=dst, in0=tmp16, scalar=1.0, in1=nw_, op0=ALU.add, op1=ALU.mult), reads=[modv_b], writes=[modv_b])

            def mk_shift(dst, jj, which, bj):
                S.dve(lambda e: e.tensor_tensor(out=dst, in0=cols[:, jj, :, which], in1=bcol[:, bj, :], op=ALU.add), reads=[modv_b, p0_b], writes=[modv_b])

            mk_scale(A1, 1, 0, 1, naw); mk_shift(B1, 0, 0, 0)
            mk_scale(A1c, 1, 1, 1, naw); mk_shift(B1c, 0, 1, 0)
            mk_scale(A2, 3, 0, 4, nmw); mk_shift(B2, 2, 0, 3)
            A.release()
            S.barrier()
            if stop == 0:
                raise _Stop()

            A.mark()
            xbuf = [A.alloc([D], F32) for _ in range(2)]; xbuf_b = [Buf() for _ in range(2)]
            sqj = A.alloc([D], BF16); sqj_b = Buf()
            xs = [A.alloc([D], BF16) for _ in range(2)]; xs_b = [Buf() for _ in range(2)]
            st8 = [A.alloc([8], F32) for _ in range(2)]; st8_b = [Buf() for _ in range(2)]
            prep_n = [0]

            def prep(src_ap, Av, Bv, hT, hT_b, xt=None, xt_b=None):
                i = prep_n[0] % 2; prep_n[0] += 1
                if xt is None:
                    xt, xt_b = xbuf[i], xbuf_b[i]
                    S.dma("sp", xt, src_ap, writes=[xt_b])
                s8, s8b = st8[i], st8_b[i]
                xsi, xsi_b, sqj_, sqj_b_ = xs[i], xs_b[i], sqj, sqj_b
                PL = int(os.environ.get("PL", "9"))
                S.dve(lambda e: e.memset(s8[:, 0:1], 0.0), writes=[s8b])
                if PL < 1: return
                S.act(lambda e: e.activation(out=sqj_, in_=xt, func=AF.Square, accum_out=s8[:, 0:1]), reads=[xt_b], writes=[sqj_b_, s8b])
                if PL < 2: return
                S.act(lambda e: e.activation(out=s8[:, 1:2], in_=s8[:, 0:1], func=AF.Ln, scale=1.0 / D, bias=epsc), reads=[s8b, misc_b], writes=[s8b])
                S.act(lambda e: e.activation(out=s8[:, 2:3], in_=s8[:, 1:2], func=AF.Exp, scale=-0.5), reads=[s8b], writes=[s8b])
                S.act(lambda e: e.activation(out=xsi, in_=xt, func=AF.Copy, scale=s8[:, 2:3]), reads=[xt_b, s8b], writes=[xsi_b])
                if PL < 3: return
                for r in range(4):
                    for c in range(4):
                        kc = r * 4 + c
                        S.pe(lambda e, r=r, c=c, kc=kc: e.matmul(pb(r)[:, c * 128:(c + 1) * 128], lhsT=xsi[:, kc * 128:(kc + 1) * 128], rhs=ident, start=True, stop=True),
                             reads=[xsi_b, ident_b], writes=[PB[r]])
                    if PL < 4: continue
                    for c in range(4):
                        kc = r * 4 + c
                        S.act(lambda e, r=r, c=c, kc=kc: e.activation(out=hT[:, kc, :], in_=pb(r)[:, c * 128:(c + 1) * 128], func=AF.Identity, scale=Av[:, kc:kc + 1], bias=Bv[:, kc:kc + 1]),
                              reads=[PB[r], modv_b], writes=[hT_b])

            sq = [A.alloc([512], F32) for _ in range(2)]; qn = [A.alloc([512], F32) for _ in range(2)]
            ta = [A.alloc([512], F32) for _ in range(2)]; tb = [A.alloc([512], F32) for _ in range(2)]
            rs16 = [A.alloc([16], F32) for _ in range(2)]
            pp_b = [Buf() for _ in range(2)]
            pp_n = [0]

            def qk_post(bank, bank_b, dst, dst_b, wbc, rope_t, rope_b):
                i = pp_n[0] % 2; pp_n[0] += 1
                b = pp_b[i]
                v3 = lambda a: a.rearrange("p (s d) -> p s d", s=8)
                v5 = lambda a: a.rearrange("p (s h x i) -> p s h x i", s=8, h=2, x=2, i=16)
                cos4 = rope_t[:, 0:64]; sin4 = rope_t[:, 64:128]
                sin5 = sin4.rearrange("p (h x i) -> p h x i", h=2, x=2, i=16)
                S.act(lambda e: e.activation(out=sq[i], in_=bank, func=AF.Square), reads=[bank_b], writes=[b])
                S.dve(lambda e: e.tensor_reduce(out=rs16[i][:, 0:8], in_=v3(sq[i]), axis=AX.X, op=ALU.add), reads=[b], writes=[b])
                S.act(lambda e: e.activation(out=rs16[i][:, 8:16], in_=rs16[i][:, 0:8], func=AF.Ln, scale=1.0 / 64, bias=epsc), reads=[b, misc_b], writes=[b])
                S.act(lambda e: e.activation(out=rs16[i][:, 0:8], in_=rs16[i][:, 8:16], func=AF.Exp, scale=-0.5), reads=[b], writes=[b])
                S.dve(lambda e: e.tensor_tensor(out=v3(qn[i]), in0=v3(bank), in1=rs16[i][:, 0:8].unsqueeze(2).to_broadcast([128, 8, 64]), op=ALU.mult), reads=[bank_b, b], writes=[b])
                S.dve(lambda e: e.tensor_tensor(out=v3(qn[i]), in0=v3(qn[i]), in1=wbc.unsqueeze(1).to_broadcast([128, 8, 64]), op=ALU.mult), reads=[b, wqk_b], writes=[b])
                S.dve(lambda e: e.tensor_tensor(out=v3(ta[i]), in0=v3(qn[i]), in1=cos4.unsqueeze(1).to_broadcast([128, 8, 64]), op=ALU.mult), reads=[b, rope_b], writes=[b])
                S.dve(lambda e: e.tensor_tensor(out=v5(tb[i])[:, :, :, 0, :], in0=v5(qn[i])[:, :, :, 1, :], in1=sin5[:, :, 0, :].unsqueeze(1).to_broadcast([128, 8, 2, 16]), op=ALU.mult), reads=[b, rope_b], writes=[b])
                S.dve(lambda e: e.tensor_tensor(out=v5(tb[i])[:, :, :, 1, :], in0=v5(qn[i])[:, :, :, 0, :], in1=sin5[:, :, 1, :].unsqueeze(1).to_broadcast([128, 8, 2, 16]), op=ALU.mult), reads=[b, rope_b], writes=[b])
                S.dve(lambda e: e.tensor_tensor(out=dst, in0=ta[i], in1=tb[i], op=ALU.add), reads=[b], writes=[dst_b])

            wres = A.alloc([16, 2048], BF16); wres_b = [Buf() for _ in range(4)]
            hTa = [A.alloc([16, 128], BF16) for _ in range(2)]; hTa_b = [Buf() for _ in range(2)]
            ropeb = [A.alloc([128], F32) for _ in range(2)]; ropeb_b = [Buf() for _ in range(2)]
            qkb = [A.alloc([1024], BF16) for _ in range(2)]; qkb_b = [Buf() for _ in range(2)]
            ktst = [A.alloc([8, 128], BF16) for _ in range(2)]; ktst_b = [Buf() for _ in range(2)]
            vst = [A.alloc([1024], BF16) for _ in range(2)]; vst_b = [Buf() for _ in range(2)]

            def load_wres(col0):
                for g in range(4):
                    S.dma("pool", wres[:, :, g * 512:(g + 1) * 512], w_in[:, col0 + g * 512:col0 + (g + 1) * 512].rearrange("(k p) n -> p k n", p=128), writes=[wres_b[g]])

            def project(hT, hT_b, g, bank_i):
                for kc in range(KD):
                    S.pe(lambda e, g=g, kc=kc, bank_i=bank_i: e.matmul(pa(bank_i), lhsT=hT[:, kc, :], rhs=wres[:, kc, g * 512:(g + 1) * 512], start=(kc == 0), stop=(kc == KD - 1)),
                         reads=[hT_b, wres_b[g]], writes=[PA[bank_i]])

            def qk_transpose_store(i, dstT, dstT_b, col0, q):
                for r in range(2):
                    for c in range(4):
                        hh = r * 4 + c
                        S.pe(lambda e, r=r, c=c, hh=hh: e.matmul(pb(2 + r)[:, c * 128:(c + 1) * 128], lhsT=qkb[i][:, hh * 128:(hh + 1) * 128], rhs=ident, start=True, stop=True),
                             reads=[qkb_b[i], ident_b], writes=[PB[2 + r]])
                    S.dve(lambda e, r=r: e.tensor_copy(out=ktst[i][:, r * 4:(r + 1) * 4, :].rearrange("p a b -> p (a b)"), in_=pb(2 + r)), reads=[PB[2 + r]], writes=[ktst_b[i]])
                S.dma("sp", dstT[:, :, col0:col0 + 128].rearrange("h p t -> p h t"), ktst[i], reads=[ktst_b[i]], writes=[dstT_b])

            load_wres(1024)
            for kt in range(NKT):
                i = kt % 2
                if kt < NT:
                    src = xo[kt * 128:(kt + 1) * 128, :]; Av, Bv = A1, B1
                elif kt < NT + NO:
                    src = xr[(kt - NT) * 128:(kt - NT + 1) * 128, :]; Av, Bv = A1, B1
                else:
                    src = ctx[(kt - NT - NO) * 128:(kt - NT - NO + 1) * 128, :]; Av, Bv = A1c, B1c
                S.dma("sp", ropeb[i], rope_d[kt], writes=[ropeb_b[i]])
                prep(src, Av, Bv, hTa[i], hTa_b[i])
                if stop == 10: raise _Stop()
                for g in range(4):
                    project(hTa[i], hTa_b[i], g, g)
                if stop == 11: raise _Stop()
                for g in range(2):
                    qk_post(pa(g), PA[g], qkb[i][:, g * 512:(g + 1) * 512], qkb_b[i], wk_bc, ropeb[i], ropeb_b[i])
                if stop == 12: raise _Stop()
                qk_transpose_store(i, KTs, KTs_b[kt], kt * 128, False)
                if stop == 13: raise _Stop()
                for g in range(2):
                    S.act(lambda e, g=g, i=i: e.activation(out=vst[i][:, g * 512:(g + 1) * 512], in_=pa(2 + g), func=AF.Copy), reads=[PA[2 + g]], writes=[vst_b[i]])
                S.dma("sp", Vs[kt * 128:(kt + 1) * 128, :], vst[i], reads=[vst_b[i]], writes=[Vs_b[kt]])

            for g in range(2):
                S.dma("pool", wres[:, :, g * 512:(g + 1) * 512], w_in[:, g * 512:(g + 1) * 512].rearrange("(k p) n -> p k n", p=128), writes=[wres_b[g]])
            for g in range(2):
                S.dma("pool", wres[:, :, (2 + g) * 512:(3 + g) * 512], w_in[:, 3072 + g * 512:3072 + (g + 1) * 512].rearrange("(k p) n -> p k n", p=128), writes=[wres_b[2 + g]])
            for tt in range(NT + 2):
                i = tt % 2
                if tt < NT:
                    src = xo[tt * 128:(tt + 1) * 128, :]; urow = (1 + tt) * 128
                elif tt == NT:
                    src = xh[0:128, :]; urow = 0
                else:
                    src = xh[128:256, :]; urow = (NT + 1) * 128
                prep(src, A1, B1, hTa[i], hTa_b[i])
                if tt < NT:
                    S.dma("sp", ropeb[i], rope_d[tt], writes=[ropeb_b[i]])
                    for g in range(2):
                        project(hTa[i], hTa_b[i], g, g)
                    for g in range(2):
                        qk_post(pa(g), PA[g], qkb[i][:, g * 512:(g + 1) * 512], qkb_b[i], wq_bc, ropeb[i], ropeb_b[i])
                    qk_transpose_store(i, QTs, QTs_b[tt], tt * 128, True)
                for g in range(2):
                    project(hTa[i], hTa_b[i], 2 + g, 2 + g)
                for g in range(2):
                    S.act(lambda e, g=g, i=i: e.activation(out=vst[i][:, g * 512:(g + 1) * 512], in_=pa(2 + g), func=AF.Copy), reads=[PA[2 + g]], writes=[vst_b[i]])
                S.dma("sp", Us[urow:urow + 128, :], vst[i], reads=[vst_b[i]], writes=[Us_b[urow // 128]])
            A.release()
            S.barrier()
            if stop == 1:
                raise _Stop()

            A.mark()
            QT = [A.alloc([T], BF16) for _ in range(2)]; QT_b = [Buf() for _ in range(2)]
            KT = [A.alloc([LK], BF16) for _ in range(2)]; KT_b = [Buf() for _ in range(2)]
            V1 = [A.alloc([NKT, 130], BF16) for _ in range(2)]; V1_b = [Buf() for _ in range(2)]
            Pb = [A.alloc([1024], BF16) for _ in range(3)]; Pb_b = [Buf() for _ in range(3)]
            Osb = A.alloc([1536], F32); Osb_b = Buf()
            rz = A.alloc([16], F32); dd = A.alloc([128], F32); dj = A.alloc([128], F32); ss4 = A.alloc([8], F32)
            ep_b = Buf()
            hn = A.alloc([512], BF16); hn_b = Buf()
            hst = [A.alloc([512], BF16) for _ in range(2)]; hst_b = [Buf() for _ in range(2)]
            for i in range(2):
                S.dve(lambda e, i=i: e.memset(V1[i][:, :, 128:130], 1.0), writes=[V1_b[i]])

            def oreg(r):
                bank = r // 3
                off = bank * 512 + (r % 3) * 160
                return bank, off

            pcount = 0
            scount = 0
            for h in range(8):
                hb = h % 2
                S.dma("sp", QT[hb], QTs[h], reads=QTs_b, writes=[QT_b[hb]])
                S.dma("sp", KT[hb], KTs[h], reads=KTs_b, writes=[KT_b[hb]])
                S.dma("sp", V1[hb][:, :, 0:128], Vs[:, h * 128:(h + 1) * 128].rearrange("(k p) v -> p k v", p=128), reads=Vs_b, writes=[V1_b[hb]], accumulate=True)
                for qi in range(NQ):
                    for kt in range(NKT):
                        sb = scount % 2; scount += 1
                        S.pe(lambda e, sb=sb, hb=hb, kt=kt, qi=qi: e.matmul(psA[:, sb * 1024:sb * 1024 + 512], lhsT=KT[hb][0:64, kt * 128:(kt + 1) * 128], rhs=QT[hb][0:64, qi * 512:(qi + 1) * 512], start=True, stop=True),
                             reads=[KT_b[hb], QT_b[hb]], writes=[PA[2 * sb]])
                        S.pe(lambda e, sb=sb, hb=hb, kt=kt, qi=qi: e.matmul(psA[:, sb * 1024 + 512:sb * 1024 + 1024], lhsT=KT[hb][64:128, kt * 128:(kt + 1) * 128], rhs=QT[hb][64:128, qi * 512:(qi + 1) * 512], start=True, stop=True),
                             reads=[KT_b[hb], QT_b[hb]], writes=[PA[2 * sb + 1]])
                        pi = pcount % 3; pcount += 1
                        S.act(lambda e, sb=sb, pi=pi: e.activation(out=Pb[pi], in_=psA[:, sb * 1024:(sb + 1) * 1024], func=AF.Exp, scale=0.125, bias=nshift),
                              reads=[PA[2 * sb], PA[2 * sb + 1], misc_b], writes=[Pb_b[pi]])
                        for sub in range(2):
                            for qs in range(4):
                                bank, off = oreg(sub * 4 + qs)
                                S.pe(lambda e, pi=pi, sub=sub, qs=qs, off=off, hb=hb, kt=kt: e.matmul(psB[:, off:off + 129], lhsT=Pb[pi][:, sub * 512 + qs * 128:sub * 512 + (qs + 1) * 128], rhs=V1[hb][:, kt, 0:129], start=(kt == 0 and (sub * 4 + qs) % 3 == 0), stop=(kt == NKT - 1), skip_group_check=True),
                                     reads=[Pb_b[pi], V1_b[hb]], writes=[PB[bank]])
                    S.dve(lambda e: e.tensor_copy(out=Osb, in_=psB[:, 0:1536]), reads=PB[0:3], writes=[Osb_b])
                    for sub in range(2):
                        for qs in range(4):
                            r = sub * 4 + qs
                            _, off = oreg(r)
                            S.dve(lambda e, r=r, off=off: e.reciprocal(out=rz[:, r:r + 1], in_=Osb[:, off + 128:off + 129]), reads=[Osb_b], writes=[ep_b])
                    S.dve(lambda e: e.tensor_scalar(out=rz[:, 8:12], in0=rz[:, 4:8], scalar1=neglam, scalar2=None, op0=ALU.mult), reads=[ep_b, misc_b], writes=[ep_b])
                    for qs in range(4):
                        _, off0 = oreg(qs)
                        _, off1 = oreg(4 + qs)
                        S.dve(lambda e, qs=qs, off0=off0: e.tensor_scalar(out=dd, in0=Osb[:, off0:off0 + 128], scalar1=rz[:, qs:qs + 1], scalar2=None, op0=ALU.mult), reads=[Osb_b, ep_b], writes=[ep_b])
                        S.dve(lambda e, qs=qs, off1=off1: e.scalar_tensor_tensor(out=dd, in0=Osb[:, off1:off1 + 128], scalar=rz[:, 8 + qs:9 + qs], in1=dd, op0=ALU.mult, op1=ALU.add), reads=[Osb_b, ep_b], writes=[ep_b])
                        S.dve(lambda e: e.scalar_tensor_tensor(out=dj, in0=dd, scalar=1.0, in1=dd, op0=ALU.mult, op1=ALU.mult, accum_out=ss4[:, 0:1]), reads=[ep_b], writes=[ep_b])
                        S.act(lambda e: e.activation(out=ss4[:, 1:2], in_=ss4[:, 0:1], func=AF.Ln, scale=1.0 / 128, bias=epsc), reads=[ep_b, misc_b], writes=[ep_b])
                        S.act(lambda e: e.activation(out=ss4[:, 2:3], in_=ss4[:, 1:2], func=AF.Exp, scale=-0.5), reads=[ep_b], writes=[ep_b])
                        S.dve(lambda e, qs=qs: e.scalar_tensor_tensor(out=hn[:, qs * 128:(qs + 1) * 128], in0=dd, scalar=ss4[:, 2:3], in1=swb, op0=ALU.mult, op1=ALU.mult), reads=[ep_b, swb_b], writes=[hn_b])
                    for qs in range(4):
                        S.pe(lambda e, qs=qs: e.matmul(pb(3)[:, qs * 128:(qs + 1) * 128], lhsT=hn[:, qs * 128:(qs + 1) * 128], rhs=ident, start=True, stop=True), reads=[hn_b, ident_b], writes=[PB[3]])
                    hi = (h * NQ + qi) % 2
                    S.dve(lambda e, hi=hi: e.tensor_copy(out=hst[hi], in_=pb(3)), reads=[PB[3]], writes=[hst_b[hi]])
                    S.dma("sp", HTs[h, :, qi * 512:(qi + 1) * 512], hst[hi], reads=[hst_b[hi]], writes=[HTs_b[h][qi]])
            A.release()
            S.barrier()
            if stop == 2:
                raise _Stop()

            A.mark()
            x1 = A.alloc([4, D], F32); x1_b = [Buf() for _ in range(4)]
            sqj = A.alloc([D], BF16); sqj_b = Buf()
            xs = [A.alloc([D], BF16) for _ in range(2)]; xs_b = [Buf() for _ in range(2)]
            st8 = [A.alloc([8], F32) for _ in range(2)]; st8_b = [Buf() for _ in range(2)]
            hT = A.alloc([16, 512], BF16); hT_b = [Buf() for _ in range(4)]
            pw = A.alloc([4, 2, 256], BF16); pw_b = Buf()
            wst = [A.alloc([16, 512], BF16) for _ in range(3)]; wst_b = [Buf() for _ in range(3)]
            A.mark()
            ut = A.alloc([6, 1024], BF16); ut_b = [Buf() for _ in range(6)]
            dT = A.alloc([8, 512], BF16); dT_b = Buf()
            hdT = A.alloc([8, 512], BF16); hdT_b = Buf()
            yT = A.alloc([8, 512], BF16); yT_b = Buf()
            mixT = A.alloc([16, 512], BF16); mixT_b = Buf()
            gsg = [A.alloc([512], F32) for _ in range(4)]; gsg_b = [Buf() for _ in range(4)]
            mix_end = A.off
            A.release()
            A.mark()
            hidT = A.alloc([32, 512], BF16); hidT_b = Buf()
            rl = [A.alloc([512], F32) for _ in range(2)]; rl_b = [Buf() for _ in range(2)]
            ffn_end = A.off
            A.release()
            A.off = max(mix_end, ffn_end)
            A.peak = max(A.peak, A.off)
            region_all = ut_b + [dT_b, hdT_b, yT_b, mixT_b] + gsg_b + [hidT_b] + rl_b
            dummy = A.alloc([1], F32)

            def region_switch():
                S.dve(lambda e: e.memset(dummy, 0.0), writes=region_all)

            for g in range(4):
                S.dma("pool", pw[:, g, :, :], pool_w[g].rearrange("(k p) n -> p k n", p=128), writes=[pw_b], accumulate=True)

            wn = [0]

            def wload(src_ap, kcn, extra_reads=()):
                i = wn[0] % 3; wn[0] += 1
                S.dma("pool", wst[i][:, 0:kcn, :], src_ap.rearrange("(k p) n -> p k n", p=128), writes=[wst_b[i]])
                return wst[i], wst_b[i]

            evn = [0]

            for s in range(NS):
                t0 = s * 512
                region_switch()
                ffn_guard = []
                S.dma("sp", hdT, HTs[:, :, t0:t0 + 512].rearrange("h p t -> p h t"), reads=[HTs_b[h][s] for h in range(8)] + ffn_guard, writes=[hdT_b])
                for j in range(6):
                    S.dma("sp", ut[:, j, :], Us[(s * 4 + j) * 128:(s * 4 + j + 1) * 128, :], reads=[Us_b[s * 4 + j]] + ffn_guard, writes=[ut_b[j]])
                for j in range(4):
                    S.dma("sp", x1[:, j, :], xo[t0 + j * 128:t0 + (j + 1) * 128, :], writes=[x1_b[j]])
                for j in range(4):
                    prep(None, A1, B1, hT[:, :, j * 128:(j + 1) * 128], hT_b[j], xt=x1[:, j, :], xt_b=x1_b[j])
                for c in range(8):
                    g = c // 2
                    bank_i = c % 4
                    for j in range(4):
                        gt = s * 4 + j
                        cls = 0 if gt == 0 else (2 if gt == NT - 1 else 1)
                        for n in range(3):
                            bidx = (cls * 4 + g) * 3 + n
                            S.pe(lambda e, c=c, j=j, n=n, bidx=bidx, bank_i=bank_i: e.matmul(pa(bank_i)[:, j * 128:(j + 1) * 128], lhsT=ut[:, j + n, c * 128:(c + 1) * 128], rhs=bands[:, bidx, :], start=(n == 0), stop=(n == 2)),
                                 reads=[ut_b[j + n], bands_b], writes=[PA[bank_i]])
                    S.act(lambda e, c=c, bank_i=bank_i: e.activation(out=dT[:, c, :], in_=pa(bank_i), func=AF.Copy), reads=[PA[bank_i]] + ffn_guard, writes=[dT_b])
                for oc8 in range(8):
                    g = oc8 // 2; oc = oc8 % 2
                    bank_i = oc8 % 4
                    for ic in range(2):
                        S.pe(lambda e, g=g, oc=oc, ic=ic, bank_i=bank_i: e.matmul(pa(bank_i), lhsT=pw[:, g, ic, oc * 128:(oc + 1) * 128], rhs=dT[:, g * 2 + ic, :], start=(ic == 0), stop=(ic == 1)),
                             reads=[pw_b, dT_b], writes=[PA[bank_i]])
                    S.act(lambda e, oc8=oc8, bank_i=bank_i: e.activation(out=yT[:, oc8, :], in_=pa(bank_i), func=AF.Copy, scale=pscale[:, oc8:oc8 + 1]), reads=[PA[bank_i], pscale_b] + ffn_guard, writes=[yT_b])
                for mg in range(4):
                    iab = wn[0] % 3; wn[0] += 1
                    wa, wa_b = wst[iab], wst_b[iab]
                    S.dma("pool", wa[:, 0:8, :], w_a_up[:, mg * 512:(mg + 1) * 512].rearrange("(k p) n -> p k n", p=128), writes=[wa_b])
                    S.dma("pool", wa[:, 8:16, :], w_b_up[:, mg * 512:(mg + 1) * 512].rearrange("(k p) n -> p k n", p=128), writes=[wa_b], accumulate=True)
                    wbu, wbu_b = wa, wa_b
                    wga, wga_b = wload(w_in[:, 4096 + mg * 512:4096 + (mg + 1) * 512], 16)
                    wgb, wgb_b = wload(w_in[:, 6144 + mg * 512:6144 + (mg + 1) * 512], 16)
                    for mi in range(4):
                        m = mg * 4 + mi
                        cs = slice(mi * 128, (mi + 1) * 128)
                        for kc in range(8):
                            S.pe(lambda e, kc=kc, cs=cs, wa=wa: e.matmul(pa(0), lhsT=wa[:, kc, cs], rhs=hdT[:, kc, :], start=(kc == 0), stop=(kc == 7)), reads=[wa_b, hdT_b], writes=[PA[0]])
                        for kc in range(8):
                            S.pe(lambda e, kc=kc, cs=cs, wbu=wbu: e.matmul(pa(1), lhsT=wbu[:, 8 + kc, cs], rhs=yT[:, kc, :], start=(kc == 0), stop=(kc == 7)), reads=[wbu_b, yT_b], writes=[PA[1]])
                        for kc in range(KD):
                            S.pe(lambda e, kc=kc, cs=cs, wga=wga: e.matmul(pa(2), lhsT=wga[:, kc, cs], rhs=hT[:, kc, :], start=(kc == 0), stop=(kc == KD - 1)), reads=[wga_b] + hT_b, writes=[PA[2]])
                        for kc in range(KD):
                            S.pe(lambda e, kc=kc, cs=cs, wgb=wgb: e.matmul(pa(3), lhsT=wgb[:, kc, cs], rhs=hT[:, kc, :], start=(kc == 0), stop=(kc == KD - 1)), reads=[wgb_b] + hT_b, writes=[PA[3]])
                        gi = (evn[0] % 2) * 2; evn[0] += 1
                        S.act(lambda e, gi=gi: e.activation(out=gsg[gi], in_=pa(2), func=AF.Sigmoid), reads=[PA[2]] + ffn_guard, writes=[gsg_b[gi]])
                        S.act(lambda e, gi=gi: e.activation(out=gsg[gi + 1], in_=pa(3), func=AF.Sigmoid), reads=[PA[3]] + ffn_guard, writes=[gsg_b[gi + 1]])
                        S.dve(lambda e, gi=gi: e.tensor_tensor(out=gsg[gi], in0=gsg[gi], in1=pa(0), op=ALU.mult), reads=[PA[0], gsg_b[gi]], writes=[gsg_b[gi]])
                        S.dve(lambda e, gi=gi: e.tensor_tensor(out=gsg[gi + 1], in0=gsg[gi + 1], in1=pa(1), op=ALU.mult), reads=[PA[1], gsg_b[gi + 1]], writes=[gsg_b[gi + 1]])
                        S.dve(lambda e, gi=gi, m=m: e.tensor_tensor(out=mixT[:, m, :], in0=gsg[gi], in1=gsg[gi + 1], op=ALU.add), reads=[gsg_b[gi], gsg_b[gi + 1]] + ffn_guard, writes=[mixT_b])
                for ng in range(4):
                    wo, wo_b = wload(w_o[:, ng * 512:(ng + 1) * 512], 16)
                    for j in range(4):
                        for kc in range(KD):
                            S.pe(lambda e, kc=kc, j=j, wo=wo: e.matmul(pa(j), lhsT=mixT[:, kc, j * 128:(j + 1) * 128], rhs=wo[:, kc, :], start=(kc == 0), stop=(kc == KD - 1)), reads=[mixT_b, wo_b], writes=[PA[j]])
                    for j in range(4):
                        ri = evn[0] % 2; evn[0] += 1
                        S.dve(lambda e, j=j, ng=ng, ri=ri: e.tensor_tensor(out=gsg[ri], in0=pa(j), in1=GArow[:, ng * 512:(ng + 1) * 512], op=ALU.mult), reads=[PA[j], grow_b], writes=[gsg_b[ri]])
                        S.dve(lambda e, j=j, ng=ng, ri=ri: e.tensor_tensor(out=x1[:, j, ng * 512:(ng + 1) * 512], in0=x1[:, j, ng * 512:(ng + 1) * 512], in1=gsg[ri], op=ALU.add), reads=[gsg_b[ri], x1_b[j]], writes=[x1_b[j]])
                for j in range(4):
                    prep(None, A2, B2, hT[:, :, j * 128:(j + 1) * 128], hT_b[j], xt=x1[:, j, :], xt_b=x1_b[j])
                region_switch()
                mix_guard = []
                for half in range(2):
                    for fg in range(8):
                        col0 = half * 4096 + fg * 512
                        w1, w1_b = wload(w_ff1[:, col0:col0 + 512], 16)
                        for mi in range(4):
                            jh = fg * 4 + mi
                            bank_i = mi
                            for kc in range(KD):
                                S.pe(lambda e, kc=kc, mi=mi, w1=w1, bank_i=bank_i: e.matmul(pa(bank_i), lhsT=w1[:, kc, mi * 128:(mi + 1) * 128], rhs=hT[:, kc, :], start=(kc == 0), stop=(kc == KD - 1)), reads=[w1_b] + hT_b, writes=[PA[bank_i]])
                            ri = evn[0] % 2; evn[0] += 1
                            S.act(lambda e, ri=ri, bank_i=bank_i: e.activation(out=rl[ri], in_=pa(bank_i), func=AF.Relu), reads=[PA[bank_i]] + mix_guard, writes=[rl_b[ri]])
                            S.dve(lambda e, ri=ri, jh=jh: e.tensor_tensor(out=hidT[:, jh, :], in0=rl[ri], in1=rl[ri], op=ALU.mult), reads=[rl_b[ri]] + mix_guard, writes=[hidT_b])
                    for ng in range(4):
                        for kg in range(2):
                            r0 = half * 4096 + kg * 2048
                            w2, w2_b = wload(w_ff2[r0:r0 + 2048, ng * 512:(ng + 1) * 512], 16)
                            for j in range(4):
                                for kc in range(KD):
                                    kk = kg * 16 + kc
                                    S.pe(lambda e, kc=kc, kk=kk, j=j, w2=w2, kg=kg: e.matmul(pa(j), lhsT=hidT[:, kk, j * 128:(j + 1) * 128], rhs=w2[:, kc, :], start=(kk == 0), stop=(kk == 31)), reads=[hidT_b, w2_b], writes=[PA[j]])
                        for j in range(4):
                            ri = evn[0] % 2; evn[0] += 1
                            S.dve(lambda e, j=j, ng=ng, ri=ri: e.tensor_tensor(out=rl[ri], in0=pa(j), in1=GMrow[:, ng * 512:(ng + 1) * 512], op=ALU.mult), reads=[PA[j], grow_b], writes=[rl_b[ri]])
                            S.dve(lambda e, j=j, ng=ng, ri=ri: e.tensor_tensor(out=x1[:, j, ng * 512:(ng + 1) * 512], in0=x1[:, j, ng * 512:(ng + 1) * 512], in1=rl[ri], op=ALU.add), reads=[rl_b[ri], x1_b[j]], writes=[x1_b[j]])
                for j in range(4):
                    S.dma("sp", y[t0 + j * 128:t0 + (j + 1) * 128, :], x1[:, j, :], reads=[x1_b[j]], writes=[Buf()])
            A.release()
        except _Stop:
            pass
        finals = [o for o in S.dma_pool["sp"]["last"] if o is not None]
        S.op("sp", lambda e: e.nop(), deps=finals)
        block = st.enter_context(nc.Block())
        stats = S.emit(block)
        if debug:
            print("ops/waits:", stats, "sems:", S.nsem, "arena peak:", A.peak)
    return nc


def rope_tables(tok_idx, n_ctx):
    half = 32
    inv_freq = (10000.0 ** (-np.arange(0, half, 2, dtype=np.float32) / half)).astype(np.float32)
    row = (tok_idx // GRID_W).astype(np.float32)
    col = (tok_idx % GRID_W).astype(np.float32)
    out = np.zeros((len(tok_idx) + n_ctx, 128), np.float32)
    for hh, pos in enumerate((row, col)):
        ang = (pos[:, None] * inv_freq[None, :]).astype(np.float32)
        c, s_ = np.cos(ang), np.sin(ang)
        for xi in range(2):
            out[:len(tok_idx), hh * 32 + xi * 16: hh * 32 + xi * 16 + 16] = c
            out[:len(tok_idx), 64 + hh * 32 + xi * 16: 64 + hh * 32 + xi * 16 + 16] = (-s_ if xi == 0 else s_)
    out[len(tok_idx):, 0:64] = 1.0
    return out


def band_tables(L, s):
    T = L // 2
    out = np.zeros((128, 3, 4, 3, 128), np.float32)
    tiles = {0: s * (T // 128), 1: s * (T // 128) + 1, 2: (s + 1) * (T // 128) - 1}
    for cls, gt in tiles.items():
        for g, w in enumerate(POOL_WINDOWS):
            for tl in range(128):
                t = gt * 128 + tl
                lo = min(max(t - w // 2, 0), L)
                hi = min(max(t + w - w // 2, 0), L)
                for tp in range(lo, hi):
                    n = tp // 128 - gt + 1
                    out[tp % 128, cls, g, n, tl] += 1.0 / (hi - lo)
                out[tl, cls, g, 1, tl] -= 1.0
    return out.reshape(128, 36 * 128).astype(ml_dtypes.bfloat16)


_NC_CACHE = {}


def run(inputs, L, C, B, debug=False, stop=99):
    key = (L, C)
    if key not in _NC_CACHE:
        _NC_CACHE[key] = build(L, C, debug=debug, stop=stop)
    nc = _NC_CACHE[key]
    T = L // 2
    f = lambda a: np.ascontiguousarray(np.asarray(a, dtype=np.float32))
    x = f(inputs["x"]); c = f(inputs["c"]); ctx = f(inputs["ctx"])
    shared = {"cctx": f(inputs["c_ctx"]), "ident": np.eye(128, dtype=np.float32).astype(ml_dtypes.bfloat16)}
    for k in ("w_mod", "b_mod", "norm_attn_w", "w_in", "q_norm_w", "k_norm_w", "lambda_q1", "lambda_k1", "lambda_q2",
              "lambda_k2", "subln_w", "pool_w", "pool_scale", "w_a_up", "w_b_up", "w_o", "norm_mlp_w", "w_ff1", "w_ff2"):
        shared[k] = f(inputs[k])[0]
    in_maps = []
    zeros128 = np.zeros((128, D), np.float32)
    for b in range(B):
        for s in range(2):
            own = slice(s * T, (s + 1) * T)
            oth = slice((1 - s) * T, (2 - s) * T)
            m = dict(shared)
            m["xo"] = x[b, own]
            m["xr"] = x[b, oth]
            before = x[b, s * T - 128:s * T] if s == 1 else zeros128
            after = x[b, (s + 1) * T:(s + 1) * T + 128] if s == 0 else zeros128
            m["xh"] = np.ascontiguousarray(np.concatenate([before, after], 0))
            m["ctx"] = ctx[b]
            m["cvec"] = c[b]
            tok = np.concatenate([np.arange(s * T, (s + 1) * T), np.arange((1 - s) * T, (2 - s) * T)])
            m["rope"] = rope_tables(tok, C).reshape(-1, 128, 128)
            m["bands"] = band_tables(L, s)
            in_maps.append(m)
    res = run_bass_kernel_spmd(nc, in_maps, core_ids=list(range(2 * B)))
    out = np.zeros((B, L, D), np.float32)
    for b in range(B):
        for s in range(2):
            out[b, s * T:(s + 1) * T] = res.results[b * 2 + s]["y"]
    if debug:
        return out, res.results
    return out


def kernel(**inputs):
    return run(inputs, 8192, 256, 4)
```

```python
import os
import numpy as np
import ml_dtypes
from contextlib import ExitStack
import concourse.bass as bass
import concourse.mybir as mybir
from concourse.bass_utils import run_bass_kernel_spmd

F32 = mybir.dt.float32
BF16 = mybir.dt.bfloat16
U8 = mybir.dt.uint8
AF = mybir.ActivationFunctionType
ALU = mybir.AluOpType
AX = mybir.AxisListType

ENGS = ("pe", "act", "dve", "pool", "sp")
SEM_ROT = 30000
D = 2048
KD = 16
EPS = 1e-6
GRID_W = 64
POOL_WINDOWS = (2, 4, 8, 16)


class Op:
    __slots__ = ("eng", "fn", "deps", "sig", "sem", "val", "is_dma")

    def __init__(self, eng, fn, deps, is_dma=False):
        self.eng = eng
        self.fn = fn
        self.deps = deps
        self.sig = False
        self.sem = None
        self.val = None
        self.is_dma = is_dma


class Buf:
    __slots__ = ("name", "w", "r", "rd", "excl")

    def __init__(self, name="", excl=False):
        self.name = name
        self.excl = excl
        self.w = []
        self.r = {}
        self.rd = []


class Sched:
    def __init__(self, nc, stack, n_dma_sems=8):
        self.nc = nc
        self.stack = stack
        self.ops = {e: [] for e in ENGS}
        self.nsem = 0
        self.dma_pool = {q: {"sems": [], "cnt": [], "last": [], "i": 0, "n": n_dma_sems} for q in ("sp", "pool")}

    def new_sem(self, name):
        self.nsem += 1
        return self.stack.enter_context(self.nc.semaphore(f"{name}{self.nsem}"))

    def _deps(self, eng, is_dma, reads, writes, extra):
        deps = list(extra)
        for b in reads:
            deps += b.w
        for b in writes:
            deps += b.w
            deps += list(b.r.values())
            deps += b.rd
        return [d for d in deps if d is not None]

    def _update(self, o, reads, writes, accumulate=False):
        for b in reads:
            if o.is_dma:
                b.rd.append(o)
            else:
                b.r[o.eng] = o
        for b in writes:
            if accumulate:
                b.w = b.w + [o]
            else:
                b.w = [o]
            b.r = {}
            b.rd = []

    @staticmethod
    def _split(reads, writes):
        ex = [b for b in reads if b.excl]
        if ex:
            reads = [b for b in reads if not b.excl]
            writes = list(writes) + ex
        return reads, writes

    def op(self, eng, fn, reads=(), writes=(), deps=()):
        reads, writes = self._split(reads, writes)
        o = Op(eng, fn, self._deps(eng, False, reads, writes, deps))
        self.ops[eng].append(o)
        self._update(o, reads, writes)
        return o

    def pe(self, fn, reads=(), writes=(), deps=()):
        return self.op("pe", fn, reads, writes, deps)

    def act(self, fn, reads=(), writes=(), deps=()):
        return self.op("act", fn, reads, writes, deps)

    def dve(self, fn, reads=(), writes=(), deps=()):
        return self.op("dve", fn, reads, writes, deps)

    def dma(self, q, out, in_, reads=(), writes=(), deps=(), accumulate=False, **kw):
        P = self.dma_pool[q]
        if len(P["sems"]) < P["n"]:
            P["sems"].append(self.new_sem(f"d{q}"))
            P["cnt"].append(0)
            P["last"].append(None)
            i = len(P["sems"]) - 1
        else:
            i = P["i"] % P["n"]
        P["i"] += 1
        if P["cnt"][i] + 16 > SEM_ROT:
            P["sems"][i] = self.new_sem(f"d{q}")
            P["cnt"][i] = 0
        reads, writes = self._split(reads, writes)
        d = self._deps(q, True, reads, writes, deps)
        if accumulate:
            prevw = set(id(x) for b in writes for x in b.w if x.is_dma)
            d = [x for x in d if id(x) not in prevw]
        if P["last"][i] is not None:
            d.append(P["last"][i])
        o = Op(q, lambda e, out=out, in_=in_, kw=kw: e.dma_start(out=out, in_=in_, **kw), d, is_dma=True)
        P["cnt"][i] += 16
        o.sem = P["sems"][i]
        o.val = P["cnt"][i]
        o.sig = True
        P["last"][i] = o
        self.ops[q].append(o)
        self._update(o, reads, writes, accumulate)
        return o

    def barrier(self):
        toks = []
        for e in ENGS:
            for o in reversed(self.ops[e]):
                if not o.is_dma and o.fn is not None:
                    toks.append(o)
                    break
        for q in self.dma_pool:
            toks += [o for o in self.dma_pool[q]["last"] if o is not None]
        for e in ENGS:
            o = Op(e, None, list(toks))
            self.ops[e].append(o)

    def emit(self, block):
        for e in ENGS:
            for o in self.ops[e]:
                for d in o.deps:
                    if not d.is_dma and not (d.eng == "pe" and o.eng == "pe" and not o.is_dma):
                        d.sig = True
        for e in ENGS:
            sem = None
            cnt = 0
            for o in self.ops[e]:
                if o.is_dma or not o.sig:
                    continue
                if sem is None or cnt >= SEM_ROT:
                    sem = self.new_sem(f"c{e}")
                    cnt = 0
                cnt += 1
                o.sem = sem
                o.val = cnt
        stats = {}

        def run(eng_name, e):
            waited = {}
            nw = 0
            for o in self.ops[eng_name]:
                for d in o.deps:
                    if d.eng == "pe" and eng_name == "pe" and not d.is_dma and not o.is_dma:
                        continue
                    k = id(d.sem)
                    if waited.get(k, 0) >= d.val:
                        continue
                    waited[k] = d.val
                    e.wait_ge(d.sem, d.val)
                    nw += 1
                if o.fn is None:
                    continue
                ins = o.fn(e)
                if o.sig:
                    ins.then_inc(o.sem, 16 if o.is_dma else 1)
            stats[eng_name] = (len(self.ops[eng_name]), nw)

        @block.tensor
        def _(e):
            run("pe", e)

        @block.scalar
        def _(e):
            run("act", e)

        @block.vector
        def _(e):
            run("dve", e)

        @block.gpsimd
        def _(e):
            run("pool", e)

        @block.sync
        def _(e):
            run("sp", e)

        return stats


class Arena:
    def __init__(self, tensor, nbytes):
        self.t = tensor
        self.n = nbytes
        self.off = 0
        self.marks = []
        self.peak = 0

    def mark(self):
        self.marks.append(self.off)

    def release(self):
        self.off = self.marks.pop()

    def alloc(self, free_shape, dt):
        esz = {F32: 4, BF16: 2, U8: 1}[dt]
        n = int(np.prod(free_shape)) * esz
        assert self.off + n <= self.n, f"arena overflow {self.off}+{n}>{self.n}"
        a = self.t[:, self.off:self.off + n]
        if dt != U8:
            a = a.bitcast(dt)
        self.off += (n + 63) // 64 * 64
        self.peak = max(self.peak, self.off)
        if len(free_shape) > 1:
            names = " ".join(f"d{i}" for i in range(len(free_shape)))
            kw = {f"d{i}": int(free_shape[i]) for i in range(len(free_shape))}
            a = a.rearrange(f"p ({names}) -> p {names}", **kw)
        return a


class _Stop(Exception):
    pass


def build(L, C, debug=False, stop=99):
    T = L // 2
    NT = T // 128
    NO = (L - T) // 128
    NCT = C // 128
    NKT = NT + NO + NCT
    LK = NKT * 128
    NS = T // 512
    NQ = T // 512
    nc = bass.Bass("TRN2", target_bir_lowering=False)

    def din(name, shape, dt=F32):
        return nc.dram_tensor(name, list(shape), dt, kind="ExternalInput").ap()

    xo = din("xo", [T, D]); xr = din("xr", [L - T, D]); xh = din("xh", [256, D]); ctx = din("ctx", [C, D])
    cvec = din("cvec", [D]); cctx = din("cctx", [D])
    w_mod = din("w_mod", [D, 6 * D]); b_mod = din("b_mod", [6 * D]); naw_d = din("norm_attn_w", [D])
    w_in = din("w_in", [D, 8192]); qnw = din("q_norm_w", [64]); knw = din("k_norm_w", [64])
    lq1 = din("lambda_q1", [64]); lk1 = din("lambda_k1", [64]); lq2 = din("lambda_q2", [64]); lk2 = din("lambda_k2", [64])
    subln = din("subln_w", [128]); pool_w = din("pool_w", [4, 256, 256]); pool_scale = din("pool_scale", [1024])
    w_a_up = din("w_a_up", [1024, D]); w_b_up = din("w_b_up", [1024, D]); w_o = din("w_o", [D, D])
    nmw_d = din("norm_mlp_w", [D]); w_ff1 = din("w_ff1", [D, 4 * D]); w_ff2 = din("w_ff2", [4 * D, D])
    ident_d = din("ident", [128, 128], BF16); rope_d = din("rope", [NKT, 128, 128]); bands_d = din("bands", [128, 36 * 128], BF16)
    y = nc.dram_tensor("y", [T, D], F32, kind="ExternalOutput").ap()
    sk = dict(kind="ExternalOutput") if debug else {}
    KTs = nc.dram_tensor("KTs", [8, 128, LK], BF16, **sk).ap()
    Vs = nc.dram_tensor("Vs", [LK, 1024], BF16, **sk).ap()
    QTs = nc.dram_tensor("QTs", [8, 128, T], BF16, **sk).ap()
    Us = nc.dram_tensor("Us", [(NT + 2) * 128, 1024], BF16, **sk).ap()
    HTs = nc.dram_tensor("HTs", [8, 128, T], BF16, **sk).ap()
    KTs_b = [Buf() for _ in range(NKT)]; Vs_b = [Buf() for _ in range(NKT)]
    QTs_b = [Buf() for _ in range(NT)]; Us_b = [Buf() for _ in range(NT + 2)]
    HTs_b = [[Buf() for _ in range(NQ)] for _ in range(8)]
    dbg = {}

    with ExitStack() as st:
        ARENA_BYTES = 200 * 1024
        arena_t = st.enter_context(nc.sbuf_tensor("arena", [128, ARENA_BYTES], U8))
        psA = st.enter_context(nc.psum_tensor("psA", [128, 2048], F32))
        psB = st.enter_context(nc.psum_tensor("psB", [128, 2048], F32))
        S = Sched(nc, st)
        A = Arena(arena_t, ARENA_BYTES)
        PA = [Buf(f"PA{i}", excl=True) for i in range(4)]
        PB = [Buf(f"PB{i}", excl=True) for i in range(4)]

        def pa(i):
            return psA[:, i * 512:(i + 1) * 512]

        def pb(i):
            return psB[:, i * 512:(i + 1) * 512]


        ident = A.alloc([128], BF16); ident_b = Buf()
        bands = A.alloc([36, 128], BF16); bands_b = Buf()
        A1 = A.alloc([16], F32); B1 = A.alloc([16], F32); A1c = A.alloc([16], F32); B1c = A.alloc([16], F32)
        A2 = A.alloc([16], F32); B2 = A.alloc([16], F32)
        modv_b = Buf()
        GArow = A.alloc([D], F32); GMrow = A.alloc([D], F32); grow_b = Buf()
        wq_bc = A.alloc([64], F32); wk_bc = A.alloc([64], F32); wqk_b = Buf()
        swb = A.alloc([128], F32); swb_b = Buf()
        pscale = A.alloc([8], F32); pscale_b = Buf()
        neglam = A.alloc([1], F32); nshift = A.alloc([1], F32); epsc = A.alloc([1], F32); misc_b = Buf()

        S.dma("sp", ident, ident_d, writes=[ident_b])
        S.dma("sp", bands.rearrange("p a b -> p (a b)"), bands_d, writes=[bands_b])
        S.dve(lambda e: e.memset(epsc, EPS), writes=[misc_b])

        try:
            A.mark()
            naw = A.alloc([16], F32); nmw = A.alloc([16], F32); bcol = A.alloc([6, 16], F32)
            c0 = A.alloc([16], F32); c1 = A.alloc([16], F32)
            scb = A.alloc([16, 2], BF16); rep = A.alloc([16, 128], BF16)
            lam4 = A.alloc([4, 64], F32); lamp = A.alloc([2, 64], F32); lams = A.alloc([2], F32); lame = A.alloc([2], F32)
            sbc = A.alloc([128], F32); wsq = A.alloc([2, 64], F32); wmx = A.alloc([2], F32)
            cols = A.alloc([4, 16, 2], F32)
            brow = A.alloc([2, D], F32)
            tmp16 = A.alloc([16], F32)
            p0_b = Buf()
            NCD = dict(allow_slow_non_contiguous=True)
            S.dma("sp", naw, naw_d.rearrange("(k p) -> p k", p=128), writes=[p0_b], accumulate=True, **NCD)
            S.dma("sp", nmw, nmw_d.rearrange("(k p) -> p k", p=128), writes=[p0_b], accumulate=True, **NCD)
            for j in range(6):
                S.dma("sp", bcol[:, j, :], b_mod[j * D:(j + 1) * D].rearrange("(k p) -> p k", p=128), writes=[p0_b], accumulate=True, **NCD)
            S.dma("sp", c0, cvec.rearrange("(k p) -> p k", p=128), writes=[p0_b], accumulate=True, **NCD)
            S.dma("sp", c1, cctx.rearrange("(k p) -> p k", p=128), writes=[p0_b], accumulate=True, **NCD)
            S.dma("sp", pscale, pool_scale.rearrange("(k p) -> p k", p=128), writes=[pscale_b], **NCD)
            S.dma("sp", wq_bc, qnw.partition_broadcast(128), writes=[wqk_b], accumulate=True)
            S.dma("sp", wk_bc, knw.partition_broadcast(128), writes=[wqk_b], accumulate=True)
            for i, lv in enumerate((lq1, lk1, lq2, lk2)):
                S.dma("sp", lam4[:, i, :], lv.partition_broadcast(128), writes=[p0_b], accumulate=True)
            S.dma("sp", sbc, subln.partition_broadcast(128), writes=[p0_b], accumulate=True)
            S.dma("sp", brow[:, 0, :], b_mod[2 * D:3 * D].partition_broadcast(128), writes=[p0_b], accumulate=True)
            S.dma("sp", brow[:, 1, :], b_mod[5 * D:6 * D].partition_broadcast(128), writes=[p0_b], accumulate=True)
            sc_b = Buf()
            S.act(lambda e: e.activation(out=scb[:, :, 0], in_=c0, func=AF.Silu), reads=[p0_b], writes=[sc_b])
            S.act(lambda e: e.activation(out=scb[:, :, 1], in_=c1, func=AF.Silu), reads=[p0_b, sc_b], writes=[sc_b])
            rep_b = Buf()
            S.dve(lambda e: e.tensor_copy(out=rep, in_=scb[:, :, 0:1].to_broadcast([128, 16, 128])), reads=[sc_b], writes=[rep_b])
            S.dve(lambda e: e.tensor_tensor(out=lamp[:, 0, :], in0=lam4[:, 0, :], in1=lam4[:, 1, :], op=ALU.mult), reads=[p0_b], writes=[misc_b])
            S.dve(lambda e: e.tensor_tensor(out=lamp[:, 1, :], in0=lam4[:, 2, :], in1=lam4[:, 3, :], op=ALU.mult), reads=[misc_b], writes=[misc_b])
            S.dve(lambda e: e.tensor_reduce(out=lams, in_=lamp, axis=AX.X, op=ALU.add), reads=[misc_b], writes=[misc_b])
            S.act(lambda e: e.activation(out=lame, in_=lams, func=AF.Exp), reads=[misc_b], writes=[misc_b])
            S.dve(lambda e: e.tensor_tensor(out=neglam, in0=lame[:, 1:2], in1=lame[:, 0:1], op=ALU.subtract), reads=[misc_b], writes=[misc_b])
            S.dve(lambda e: e.tensor_scalar(out=neglam, in0=neglam, scalar1=-0.2, scalar2=None, op0=ALU.add), reads=[misc_b], writes=[misc_b])
            S.dve(lambda e: e.tensor_tensor(out=wsq[:, 0, :], in0=wq_bc, in1=wq_bc, op=ALU.mult), reads=[wqk_b, misc_b], writes=[misc_b])
            S.dve(lambda e: e.tensor_tensor(out=wsq[:, 1, :], in0=wk_bc, in1=wk_bc, op=ALU.mult), reads=[wqk_b, misc_b], writes=[misc_b])
            S.dve(lambda e: e.tensor_reduce(out=wmx[:, 0:1], in_=wsq.rearrange("p a b -> p (a b)"), axis=AX.X, op=ALU.max), reads=[misc_b], writes=[misc_b])
            S.dve(lambda e: e.tensor_scalar(out=nshift, in0=wmx[:, 0:1], scalar1=-8.0, scalar2=None, op0=ALU.mult), reads=[misc_b], writes=[misc_b])
            S.dve(lambda e: e.tensor_scalar(out=swb, in0=sbc, scalar1=0.8, scalar2=None, op0=ALU.mult), reads=[p0_b], writes=[swb_b])

            NW0 = 6
            wst = [A.alloc([16, 512], BF16) for _ in range(NW0)]
            wst_b = [Buf() for _ in range(NW0)]
            colmap = {0: 0, 1: 1, 3: 2, 4: 3}
            n = 0
            for j in range(6):
                for tq in range(4):
                    wb = wst[n % NW0]; wbb = wst_b[n % NW0]; n += 1
                    c0_ = j * D + tq * 512
                    S.dma("pool", wb, w_mod[:, c0_:c0_ + 512].rearrange("(k p) n -> p k n", p=128), writes=[wbb])
                    if j in colmap:
                        jj = colmap[j]
                        for m in range(4):
                            mm_ = tq * 4 + m
                            o_ap = psB[:, (jj * 16 + mm_) * 2:(jj * 16 + mm_) * 2 + 2]
                            for kc in range(KD):
                                S.pe(lambda e, o_ap=o_ap, wb=wb, m=m, kc=kc: e.matmul(o_ap, lhsT=wb[:, kc, m * 128:(m + 1) * 128], rhs=scb[:, kc, :], start=(kc == 0), stop=(kc == KD - 1)),
                                     reads=[wbb, sc_b], writes=[PB[0]])
                    else:
                        bank = PA[tq]
                        for kc in range(KD):
                            S.pe(lambda e, tq=tq, wb=wb, kc=kc: e.matmul(pa(tq), lhsT=rep[:, kc, :], rhs=wb[:, kc, :], start=(kc == 0), stop=(kc == KD - 1)),
                                 reads=[wbb, rep_b], writes=[bank])
                        row = GArow if j == 2 else GMrow
                        bi = 0 if j == 2 else 1
                        S.dve(lambda e, row=row, tq=tq, bi=bi: e.tensor_tensor(out=row[:, tq * 512:(tq + 1) * 512], in0=pa(tq), in1=brow[:, bi, tq * 512:(tq + 1) * 512], op=ALU.add),
                              reads=[bank, p0_b], writes=[grow_b])
            S.dve(lambda e: e.tensor_copy(out=cols.rearrange("p a b c -> p (a b c)"), in_=psB[:, 0:128]), reads=[PB[0]], writes=[modv_b])
            def mk_scale(dst, jj, which, bj, nw_):
                S.dve(lambda e: e.tensor_tensor(out=tmp16, in0=cols[:, jj, :, which], in1=bcol[:, bj, :], op=ALU.add), reads=[modv_b, p0_b], writes=[modv_b])
                S.dve(lambda e: e.scalar_tensor_tensor(out=dst, in0=tmp16, scalar=1.0, in1=nw_, op0=ALU.add, op1=ALU.mult), reads=[modv_b], writes=[modv_b])

            def mk_shift(dst, jj, which, bj):
                S.dve(lambda e: e.tensor_tensor(out=dst, in0=cols[:, jj, :, which], in1=bcol[:, bj, :], op=ALU.add), reads=[modv_b, p0_b], writes=[modv_b])

            mk_scale(A1, 1, 0, 1, naw); mk_shift(B1, 0, 0, 0)
            mk_scale(A1c, 1, 1, 1, naw); mk_shift(B1c, 0, 1, 0)
            mk_scale(A2, 3, 0, 4, nmw); mk_shift(B2, 2, 0, 3)
            A.release()
            S.barrier()
            if stop == 0:
                raise _Stop()

            A.mark()
            xbuf = [A.alloc([D], F32) for _ in range(2)]; xbuf_b = [Buf() for _ in range(2)]
            sqj = A.alloc([D], BF16); sqj_b = Buf()
            xs = [A.alloc([D], BF16) for _ in range(2)]; xs_b = [Buf() for _ in range(2)]
            st8 = [A.alloc([8], F32) for _ in range(2)]; st8_b = [Buf() for _ in range(2)]
            prep_n = [0]

            def prep(src_ap, Av, Bv, hT, hT_b, xt=None, xt_b=None):
                i = prep_n[0] % 2; prep_n[0] += 1
                if xt is None:
                    xt, xt_b = xbuf[i], xbuf_b[i]
                    S.dma("sp", xt, src_ap, writes=[xt_b])
                s8, s8b = st8[i], st8_b[i]
                xsi, xsi_b, sqj_, sqj_b_ = xs[i], xs_b[i], sqj, sqj_b
                PL = int(os.environ.get("PL", "9"))
                S.dve(lambda e: e.memset(s8[:, 0:1], 0.0), writes=[s8b])
                if PL < 1: return
                S.act(lambda e: e.activation(out=sqj_, in_=xt, func=AF.Square, accum_out=s8[:, 0:1]), reads=[xt_b], writes=[sqj_b_, s8b])
                if PL < 2: return
                S.act(lambda e: e.activation(out=s8[:, 1:2], in_=s8[:, 0:1], func=AF.Ln, scale=1.0 / D, bias=epsc), reads=[s8b, misc_b], writes=[s8b])
                S.act(lambda e: e.activation(out=s8[:, 2:3], in_=s8[:, 1:2], func=AF.Exp, scale=-0.5), reads=[s8b], writes=[s8b])
                S.act(lambda e: e.activation(out=xsi, in_=xt, func=AF.Copy, scale=s8[:, 2:3]), reads=[xt_b, s8b], writes=[xsi_b])
                if PL < 3: return
                for r in range(4):
                    for c in range(4):
                        kc = r * 4 + c
                        S.pe(lambda e, r=r, c=c, kc=kc: e.matmul(pb(r)[:, c * 128:(c + 1) * 128], lhsT=xsi[:, kc * 128:(kc + 1) * 128], rhs=ident, start=True, stop=True),
                             reads=[xsi_b, ident_b], writes=[PB[r]])
                    if PL < 4: continue
                    for c in range(4):
                        kc = r * 4 + c
                        S.act(lambda e, r=r, c=c, kc=kc: e.activation(out=hT[:, kc, :], in_=pb(r)[:, c * 128:(c + 1) * 128], func=AF.Identity, scale=Av[:, kc:kc + 1], bias=Bv[:, kc:kc + 1]),
                              reads=[PB[r], modv_b], writes=[hT_b])

            sq = [A.alloc([512], F32) for _ in range(2)]; qn = [A.alloc([512], F32) for _ in range(2)]
            ta = [A.alloc([512], F32) for _ in range(2)]; tb = [A.alloc([512], F32) for _ in range(2)]
            rs16 = [A.alloc([16], F32) for _ in range(2)]
            pp_b = [Buf() for _ in range(2)]
            pp_n = [0]

            def qk_post(bank, bank_b, dst, dst_b, wbc, rope_t, rope_b):
                i = pp_n[0] % 2; pp_n[0] += 1
                b = pp_b[i]
                v3 = lambda a: a.rearrange("p (s d) -> p s d", s=8)
                v5 = lambda a: a.rearrange("p (s h x i) -> p s h x i", s=8, h=2, x=2, i=16)
                cos4 = rope_t[:, 0:64]; sin4 = rope_t[:, 64:128]
                sin5 = sin4.rearrange("p (h x i) -> p h x i", h=2, x=2, i=16)
                S.act(lambda e: e.activation(out=sq[i], in_=bank, func=AF.Square), reads=[bank_b], writes=[b])
                S.dve(lambda e: e.tensor_reduce(out=rs16[i][:, 0:8], in_=v3(sq[i]), axis=AX.X, op=ALU.add), reads=[b], writes=[b])
                S.act(lambda e: e.activation(out=rs16[i][:, 8:16], in_=rs16[i][:, 0:8], func=AF.Ln, scale=1.0 / 64, bias=epsc), reads=[b, misc_b], writes=[b])
                S.act(lambda e: e.activation(out=rs16[i][:, 0:8], in_=rs16[i][:, 8:16], func=AF.Exp, scale=-0.5), reads=[b], writes=[b])
                S.dve(lambda e: e.tensor_tensor(out=v3(qn[i]), in0=v3(bank), in1=rs16[i][:, 0:8].unsqueeze(2).to_broadcast([128, 8, 64]), op=ALU.mult), reads=[bank_b, b], writes=[b])
                S.dve(lambda e: e.tensor_tensor(out=v3(qn[i]), in0=v3(qn[i]), in1=wbc.unsqueeze(1).to_broadcast([128, 8, 64]), op=ALU.mult), reads=[b, wqk_b], writes=[b])
                S.dve(lambda e: e.tensor_tensor(out=v3(ta[i]), in0=v3(qn[i]), in1=cos4.unsqueeze(1).to_broadcast([128, 8, 64]), op=ALU.mult), reads=[b, rope_b], writes=[b])
                S.dve(lambda e: e.tensor_tensor(out=v5(tb[i])[:, :, :, 0, :], in0=v5(qn[i])[:, :, :, 1, :], in1=sin5[:, :, 0, :].unsqueeze(1).to_broadcast([128, 8, 2, 16]), op=ALU.mult), reads=[b, rope_b], writes=[b])
                S.dve(lambda e: e.tensor_tensor(out=v5(tb[i])[:, :, :, 1, :], in0=v5(qn[i])[:, :, :, 0, :], in1=sin5[:, :, 1, :].unsqueeze(1).to_broadcast([128, 8, 2, 16]), op=ALU.mult), reads=[b, rope_b], writes=[b])
                S.dve(lambda e: e.tensor_tensor(out=dst, in0=ta[i], in1=tb[i], op=ALU.add), reads=[b], writes=[dst_b])

            wres = A.alloc([16, 2048], BF16); wres_b = [Buf() for _ in range(4)]
            hTa = [A.alloc([16, 128], BF16) for _ in range(2)]; hTa_b = [Buf() for _ in range(2)]
            ropeb = [A.alloc([128], F32) for _ in range(2)]; ropeb_b = [Buf() for _ in range(2)]
            qkb = [A.alloc([1024], BF16) for _ in range(2)]; qkb_b = [Buf() for _ in range(2)]
            ktst = [A.alloc([8, 128], BF16) for _ in range(2)]; ktst_b = [Buf() for _ in range(2)]
            vst = [A.alloc([1024], BF16) for _ in range(2)]; vst_b = [Buf() for _ in range(2)]

            def load_wres(col0):
                for g in range(4):
                    S.dma("pool", wres[:, :, g * 512:(g + 1) * 512], w_in[:, col0 + g * 512:col0 + (g + 1) * 512].rearrange("(k p) n -> p k n", p=128), writes=[wres_b[g]])

            def project(hT, hT_b, g, bank_i):
                for kc in range(KD):
                    S.pe(lambda e, g=g, kc=kc, bank_i=bank_i: e.matmul(pa(bank_i), lhsT=hT[:, kc, :], rhs=wres[:, kc, g * 512:(g + 1) * 512], start=(kc == 0), stop=(kc == KD - 1)),
                         reads=[hT_b, wres_b[g]], writes=[PA[bank_i]])

            def qk_transpose_store(i, dstT, dstT_b, col0, q):
                for r in range(2):
                    for c in range(4):
                        hh = r * 4 + c
                        S.pe(lambda e, r=r, c=c, hh=hh: e.matmul(pb(2 + r)[:, c * 128:(c + 1) * 128], lhsT=qkb[i][:, hh * 128:(hh + 1) * 128], rhs=ident, start=True, stop=True),
                             reads=[qkb_b[i], ident_b], writes=[PB[2 + r]])
                    S.dve(lambda e, r=r: e.tensor_copy(out=ktst[i][:, r * 4:(r + 1) * 4, :].rearrange("p a b -> p (a b)"), in_=pb(2 + r)), reads=[PB[2 + r]], writes=[ktst_b[i]])
                S.dma("sp", dstT[:, :, col0:col0 + 128].rearrange("h p t -> p h t"), ktst[i], reads=[ktst_b[i]], writes=[dstT_b])

            load_wres(1024)
            def a1_prep(kt):
                i = kt % 2
                if kt < NT:
                    src = xo[kt * 128:(kt + 1) * 128, :]; Av, Bv = A1, B1
                elif kt < NT + NO:
                    src = xr[(kt - NT) * 128:(kt - NT + 1) * 128, :]; Av, Bv = A1, B1
                else:
                    src = ctx[(kt - NT - NO) * 128:(kt - NT - NO + 1) * 128, :]; Av, Bv = A1c, B1c
                S.dma("sp", ropeb[i], rope_d[kt], writes=[ropeb_b[i]])
                prep(src, Av, Bv, hTa[i], hTa_b[i])

            a1_prep(0)
            for kt in range(NKT):
                i = kt % 2
                if kt + 1 < NKT:
                    a1_prep(kt + 1)
                for g in range(4):
                    project(hTa[i], hTa_b[i], g, g)
                for g in range(2):
                    qk_post(pa(g), PA[g], qkb[i][:, g * 512:(g + 1) * 512], qkb_b[i], wk_bc, ropeb[i], ropeb_b[i])
                qk_transpose_store(i, KTs, KTs_b[kt], kt * 128, False)
                for g in range(2):
                    S.act(lambda e, g=g, i=i: e.activation(out=vst[i][:, g * 512:(g + 1) * 512], in_=pa(2 + g), func=AF.Copy), reads=[PA[2 + g]], writes=[vst_b[i]])
                S.dma("sp", Vs[kt * 128:(kt + 1) * 128, :], vst[i], reads=[vst_b[i]], writes=[Vs_b[kt]])

            for g in range(2):
                S.dma("pool", wres[:, :, g * 512:(g + 1) * 512], w_in[:, g * 512:(g + 1) * 512].rearrange("(k p) n -> p k n", p=128), writes=[wres_b[g]])
            for g in range(2):
                S.dma("pool", wres[:, :, (2 + g) * 512:(3 + g) * 512], w_in[:, 3072 + g * 512:3072 + (g + 1) * 512].rearrange("(k p) n -> p k n", p=128), writes=[wres_b[2 + g]])
            def a2_prep(tt):
                i = tt % 2
                if tt < NT:
                    src = xo[tt * 128:(tt + 1) * 128, :]
                    S.dma("sp", ropeb[i], rope_d[tt], writes=[ropeb_b[i]])
                elif tt == NT:
                    src = xh[0:128, :]
                else:
                    src = xh[128:256, :]
                prep(src, A1, B1, hTa[i], hTa_b[i])

            a2_prep(0)
            for tt in range(NT + 2):
                i = tt % 2
                urow = (1 + tt) * 128 if tt < NT else (0 if tt == NT else (NT + 1) * 128)
                if tt + 1 < NT + 2:
                    a2_prep(tt + 1)
                if tt < NT:
                    for g in range(2):
                        project(hTa[i], hTa_b[i], g, g)
                for g in range(2):
                    project(hTa[i], hTa_b[i], 2 + g, 2 + g)
                if tt < NT:
                    for g in range(2):
                        qk_post(pa(g), PA[g], qkb[i][:, g * 512:(g + 1) * 512], qkb_b[i], wq_bc, ropeb[i], ropeb_b[i])
                    qk_transpose_store(i, QTs, QTs_b[tt], tt * 128, True)
                for g in range(2):
                    S.act(lambda e, g=g, i=i: e.activation(out=vst[i][:, g * 512:(g + 1) * 512], in_=pa(2 + g), func=AF.Copy), reads=[PA[2 + g]], writes=[vst_b[i]])
                S.dma("sp", Us[urow:urow + 128, :], vst[i], reads=[vst_b[i]], writes=[Us_b[urow // 128]])
            A.release()
            S.barrier()
            if stop == 1:
                raise _Stop()

            A.mark()
            QT = [A.alloc([T], BF16) for _ in range(2)]; QT_b = [Buf() for _ in range(2)]
            KT = [A.alloc([LK], BF16) for _ in range(2)]; KT_b = [Buf() for _ in range(2)]
            V1 = [A.alloc([NKT, 130], BF16) for _ in range(2)]; V1_b = [Buf() for _ in range(2)]
            Pb = [A.alloc([1024], BF16) for _ in range(3)]; Pb_b = [Buf() for _ in range(3)]
            Osb = A.alloc([1536], F32); Osb_b = Buf()
            rz = A.alloc([16], F32); dd = A.alloc([128], F32); dj = A.alloc([128], F32); ss4 = A.alloc([8], F32)
            ep_b = Buf()
            hn = A.alloc([512], BF16); hn_b = Buf()
            hst = [A.alloc([512], BF16) for _ in range(2)]; hst_b = [Buf() for _ in range(2)]
            for i in range(2):
                S.dve(lambda e, i=i: e.memset(V1[i][:, :, 128:130], 1.0), writes=[V1_b[i]])

            def oreg(r):
                bank = r // 3
                off = bank * 512 + (r % 3) * 160
                return bank, off

            pcount = 0
            scount = 0
            def load_head(h):
                hb = h % 2
                S.dma("sp", QT[hb], QTs[h], reads=QTs_b, writes=[QT_b[hb]])
                S.dma("sp", KT[hb], KTs[h], reads=KTs_b, writes=[KT_b[hb]])
                S.dma("sp", V1[hb][:, :, 0:128], Vs[:, h * 128:(h + 1) * 128].rearrange("(k p) v -> p k v", p=128), reads=Vs_b, writes=[V1_b[hb]], accumulate=True)

            def emit_S(n, h, qi, kt):
                hb = h % 2; sb = n % 2; pi = n % 3
                S.pe(lambda e: e.matmul(psA[:, sb * 1024:sb * 1024 + 512], lhsT=KT[hb][0:64, kt * 128:(kt + 1) * 128], rhs=QT[hb][0:64, qi * 512:(qi + 1) * 512], start=True, stop=True),
                     reads=[KT_b[hb], QT_b[hb]], writes=[PA[2 * sb]])
                S.pe(lambda e: e.matmul(psA[:, sb * 1024 + 512:sb * 1024 + 1024], lhsT=KT[hb][64:128, kt * 128:(kt + 1) * 128], rhs=QT[hb][64:128, qi * 512:(qi + 1) * 512], start=True, stop=True),
                     reads=[KT_b[hb], QT_b[hb]], writes=[PA[2 * sb + 1]])
                S.act(lambda e: e.activation(out=Pb[pi], in_=psA[:, sb * 1024:(sb + 1) * 1024], func=AF.Exp, scale=0.125, bias=nshift),
                      reads=[PA[2 * sb], PA[2 * sb + 1], misc_b], writes=[Pb_b[pi]])

            def emit_PV(n, h, qi, kt):
                hb = h % 2; pi = n % 3
                for sub in range(2):
                    for qs in range(4):
                        bank, off = oreg(sub * 4 + qs)
                        S.pe(lambda e, sub=sub, qs=qs, off=off: e.matmul(psB[:, off:off + 129], lhsT=Pb[pi][:, sub * 512 + qs * 128:sub * 512 + (qs + 1) * 128], rhs=V1[hb][:, kt, 0:129], start=(kt == 0 and (sub * 4 + qs) % 3 == 0), stop=(kt == NKT - 1), skip_group_check=True),
                             reads=[Pb_b[pi], V1_b[hb]], writes=[PB[bank]])

            dd4 = A.alloc([4, 128], F32)

            def epilogue_a(h, qi):
                S.dve(lambda e: e.tensor_copy(out=Osb, in_=psB[:, 0:1536]), reads=PB[0:3], writes=[Osb_b])
                for r in range(8):
                    _, off = oreg(r)
                    S.dve(lambda e, r=r, off=off: e.reciprocal(out=rz[:, r:r + 1], in_=Osb[:, off + 128:off + 129]), reads=[Osb_b], writes=[ep_b])
                S.dve(lambda e: e.tensor_scalar(out=rz[:, 8:12], in0=rz[:, 4:8], scalar1=neglam, scalar2=None, op0=ALU.mult), reads=[ep_b, misc_b], writes=[ep_b])
                for qs in range(4):
                    _, off0 = oreg(qs)
                    _, off1 = oreg(4 + qs)
                    S.dve(lambda e, qs=qs, off0=off0: e.tensor_scalar(out=dd4[:, qs, :], in0=Osb[:, off0:off0 + 128], scalar1=rz[:, qs:qs + 1], scalar2=None, op0=ALU.mult), reads=[Osb_b, ep_b], writes=[ep_b])
                    S.dve(lambda e, qs=qs, off1=off1: e.scalar_tensor_tensor(out=dd4[:, qs, :], in0=Osb[:, off1:off1 + 128], scalar=rz[:, 8 + qs:9 + qs], in1=dd4[:, qs, :], op0=ALU.mult, op1=ALU.add), reads=[Osb_b, ep_b], writes=[ep_b])
                    S.dve(lambda e, qs=qs: e.scalar_tensor_tensor(out=dj, in0=dd4[:, qs, :], scalar=1.0, in1=dd4[:, qs, :], op0=ALU.mult, op1=ALU.mult, accum_out=ss4[:, qs:qs + 1]), reads=[ep_b], writes=[ep_b])
                S.act(lambda e: e.activation(out=rz[:, 12:16], in_=ss4[:, 0:4], func=AF.Ln, scale=1.0 / 128, bias=epsc), reads=[ep_b, misc_b], writes=[ep_b])
                S.act(lambda e: e.activation(out=ss4[:, 4:8], in_=rz[:, 12:16], func=AF.Exp, scale=-0.5), reads=[ep_b], writes=[ep_b])
                for qs in range(4):
                    S.dve(lambda e, qs=qs: e.scalar_tensor_tensor(out=hn[:, qs * 128:(qs + 1) * 128], in0=dd4[:, qs, :], scalar=ss4[:, 4 + qs:5 + qs], in1=swb, op0=ALU.mult, op1=ALU.mult), reads=[ep_b, swb_b], writes=[hn_b])

            def epilogue_b(h, qi):
                for qs in range(4):
                    S.pe(lambda e, qs=qs: e.matmul(pb(3)[:, qs * 128:(qs + 1) * 128], lhsT=hn[:, qs * 128:(qs + 1) * 128], rhs=ident, start=True, stop=True), reads=[hn_b, ident_b], writes=[PB[3]])
                hi = (h * NQ + qi) % 2
                S.dve(lambda e, hi=hi: e.tensor_copy(out=hst[hi], in_=pb(3)), reads=[PB[3]], writes=[hst_b[hi]])
                S.dma("sp", HTs[h, :, qi * 512:(qi + 1) * 512], hst[hi], reads=[hst_b[hi]], writes=[HTs_b[h][qi]])

            steps = [(h, qi, kt) for h in range(8) for qi in range(NQ) for kt in range(NKT)]
            load_head(0)
            pending = None
            for n, (h, qi, kt) in enumerate(steps):
                emit_S(n, h, qi, kt)
                if n >= 1:
                    ph, pqi, pkt = steps[n - 1]
                    emit_PV(n - 1, ph, pqi, pkt)
                    if pkt == NKT - 1:
                        epilogue_a(ph, pqi)
                        pending = (n + 6, ph, pqi)
                if pending is not None and n >= pending[0]:
                    epilogue_b(pending[1], pending[2])
                    pending = None
                if qi == 0 and kt == 0 and h + 1 < 8:
                    load_head(h + 1)
            ph, pqi, pkt = steps[-1]
            emit_PV(len(steps) - 1, ph, pqi, pkt)
            if pending is not None:
                epilogue_b(pending[1], pending[2])
            epilogue_a(ph, pqi)
            epilogue_b(ph, pqi)
            A.release()
            S.barrier()
            if stop == 2:
                raise _Stop()

            A.mark()
            x1 = A.alloc([4, D], F32); x1_b = [Buf() for _ in range(4)]
            sqj = A.alloc([D], BF16); sqj_b = Buf()
            xs = [A.alloc([D], BF16) for _ in range(2)]; xs_b = [Buf() for _ in range(2)]
            st8 = [A.alloc([8], F32) for _ in range(2)]; st8_b = [Buf() for _ in range(2)]
            hT = A.alloc([16, 512], BF16); hT_b = [Buf() for _ in range(4)]
            pw = A.alloc([4, 2, 256], BF16); pw_b = Buf()
            wst = [A.alloc([16, 512], BF16) for _ in range(3)]; wst_b = [Buf() for _ in range(3)]
            A.mark()
            ut = A.alloc([6, 1024], BF16); ut_b = [Buf() for _ in range(6)]
            dT = A.alloc([8, 512], BF16); dT_b = Buf()
            hdT = A.alloc([8, 512], BF16); hdT_b = Buf()
            yT = A.alloc([8, 512], BF16); yT_b = Buf()
            mixT = A.alloc([16, 512], BF16); mixT_b = Buf()
            gsg = [A.alloc([512], F32) for _ in range(4)]; gsg_b = [Buf() for _ in range(4)]
            mix_end = A.off
            A.release()
            A.mark()
            hidT = A.alloc([32, 512], BF16); hidT_b = Buf()
            rl = [A.alloc([512], F32) for _ in range(2)]; rl_b = [Buf() for _ in range(2)]
            ffn_end = A.off
            A.release()
            A.off = max(mix_end, ffn_end)
            A.peak = max(A.peak, A.off)
            region_all = ut_b + [dT_b, hdT_b, yT_b, mixT_b] + gsg_b + [hidT_b] + rl_b
            dummy = A.alloc([1], F32)

            def region_switch():
                S.dve(lambda e: e.memset(dummy, 0.0), writes=region_all)

            for g in range(4):
                S.dma("pool", pw[:, g, :, :], pool_w[g].rearrange("(k p) n -> p k n", p=128), writes=[pw_b], accumulate=True)

            wn = [0]

            def wload(src_ap, kcn, extra_reads=()):
                i = wn[0] % 3; wn[0] += 1
                S.dma("pool", wst[i][:, 0:kcn, :], src_ap.rearrange("(k p) n -> p k n", p=128), writes=[wst_b[i]])
                return wst[i], wst_b[i]

            evn = [0]

            for s in range(NS):
                t0 = s * 512
                region_switch()
                ffn_guard = []
                S.dma("sp", hdT, HTs[:, :, t0:t0 + 512].rearrange("h p t -> p h t"), reads=[HTs_b[h][s] for h in range(8)] + ffn_guard, writes=[hdT_b])
                for j in range(6):
                    S.dma("sp", ut[:, j, :], Us[(s * 4 + j) * 128:(s * 4 + j + 1) * 128, :], reads=[Us_b[s * 4 + j]] + ffn_guard, writes=[ut_b[j]])
                for j in range(4):
                    S.dma("sp", x1[:, j, :], xo[t0 + j * 128:t0 + (j + 1) * 128, :], writes=[x1_b[j]])
                for j in range(4):
                    prep(None, A1, B1, hT[:, :, j * 128:(j + 1) * 128], hT_b[j], xt=x1[:, j, :], xt_b=x1_b[j])
                for c in range(8):
                    g = c // 2
                    bank_i = c % 4
                    for j in range(4):
                        gt = s * 4 + j
                        cls = 0 if gt == 0 else (2 if gt == NT - 1 else 1)
                        for n in range(3):
                            bidx = (cls * 4 + g) * 3 + n
                            S.pe(lambda e, c=c, j=j, n=n, bidx=bidx, bank_i=bank_i: e.matmul(pa(bank_i)[:, j * 128:(j + 1) * 128], lhsT=ut[:, j + n, c * 128:(c + 1) * 128], rhs=bands[:, bidx, :], start=(n == 0), stop=(n == 2)),
                                 reads=[ut_b[j + n], bands_b], writes=[PA[bank_i]])
                    S.act(lambda e, c=c, bank_i=bank_i: e.activation(out=dT[:, c, :], in_=pa(bank_i), func=AF.Copy), reads=[PA[bank_i]] + ffn_guard, writes=[dT_b])
                for oc8 in range(8):
                    g = oc8 // 2; oc = oc8 % 2
                    bank_i = oc8 % 4
                    for ic in range(2):
                        S.pe(lambda e, g=g, oc=oc, ic=ic, bank_i=bank_i: e.matmul(pa(bank_i), lhsT=pw[:, g, ic, oc * 128:(oc + 1) * 128], rhs=dT[:, g * 2 + ic, :], start=(ic == 0), stop=(ic == 1)),
                             reads=[pw_b, dT_b], writes=[PA[bank_i]])
                    S.act(lambda e, oc8=oc8, bank_i=bank_i: e.activation(out=yT[:, oc8, :], in_=pa(bank_i), func=AF.Copy, scale=pscale[:, oc8:oc8 + 1]), reads=[PA[bank_i], pscale_b] + ffn_guard, writes=[yT_b])
                for mg in range(4):
                    iab = wn[0] % 3; wn[0] += 1
                    wa, wa_b = wst[iab], wst_b[iab]
                    S.dma("pool", wa[:, 0:8, :], w_a_up[:, mg * 512:(mg + 1) * 512].rearrange("(k p) n -> p k n", p=128), writes=[wa_b])
                    S.dma("pool", wa[:, 8:16, :], w_b_up[:, mg * 512:(mg + 1) * 512].rearrange("(k p) n -> p k n", p=128), writes=[wa_b], accumulate=True)
                    wbu, wbu_b = wa, wa_b
                    wga, wga_b = wload(w_in[:, 4096 + mg * 512:4096 + (mg + 1) * 512], 16)
                    wgb, wgb_b = wload(w_in[:, 6144 + mg * 512:6144 + (mg + 1) * 512], 16)
                    for mi in range(4):
                        m = mg * 4 + mi
                        cs = slice(mi * 128, (mi + 1) * 128)
                        for kc in range(8):
                            S.pe(lambda e, kc=kc, cs=cs, wa=wa: e.matmul(pa(0), lhsT=wa[:, kc, cs], rhs=hdT[:, kc, :], start=(kc == 0), stop=(kc == 7)), reads=[wa_b, hdT_b], writes=[PA[0]])
                        for kc in range(8):
                            S.pe(lambda e, kc=kc, cs=cs, wbu=wbu: e.matmul(pa(1), lhsT=wbu[:, 8 + kc, cs], rhs=yT[:, kc, :], start=(kc == 0), stop=(kc == 7)), reads=[wbu_b, yT_b], writes=[PA[1]])
                        for kc in range(KD):
                            S.pe(lambda e, kc=kc, cs=cs, wga=wga: e.matmul(pa(2), lhsT=wga[:, kc, cs], rhs=hT[:, kc, :], start=(kc == 0), stop=(kc == KD - 1)), reads=[wga_b] + hT_b, writes=[PA[2]])
                        for kc in range(KD):
                            S.pe(lambda e, kc=kc, cs=cs, wgb=wgb: e.matmul(pa(3), lhsT=wgb[:, kc, cs], rhs=hT[:, kc, :], start=(kc == 0), stop=(kc == KD - 1)), reads=[wgb_b] + hT_b, writes=[PA[3]])
                        gi = (evn[0] % 2) * 2; evn[0] += 1
                        S.act(lambda e, gi=gi: e.activation(out=gsg[gi], in_=pa(2), func=AF.Sigmoid), reads=[PA[2]] + ffn_guard, writes=[gsg_b[gi]])
                        S.act(lambda e, gi=gi: e.activation(out=gsg[gi + 1], in_=pa(3), func=AF.Sigmoid), reads=[PA[3]] + ffn_guard, writes=[gsg_b[gi + 1]])
                        S.dve(lambda e, gi=gi: e.tensor_tensor(out=gsg[gi], in0=gsg[gi], in1=pa(0), op=ALU.mult), reads=[PA[0], gsg_b[gi]], writes=[gsg_b[gi]])
                        S.dve(lambda e, gi=gi: e.tensor_tensor(out=gsg[gi + 1], in0=gsg[gi + 1], in1=pa(1), op=ALU.mult), reads=[PA[1], gsg_b[gi + 1]], writes=[gsg_b[gi + 1]])
                        S.dve(lambda e, gi=gi, m=m: e.tensor_tensor(out=mixT[:, m, :], in0=gsg[gi], in1=gsg[gi + 1], op=ALU.add), reads=[gsg_b[gi], gsg_b[gi + 1]] + ffn_guard, writes=[mixT_b])
                for ng in range(4):
                    wo, wo_b = wload(w_o[:, ng * 512:(ng + 1) * 512], 16)
                    for j in range(4):
                        for kc in range(KD):
                            S.pe(lambda e, kc=kc, j=j, wo=wo: e.matmul(pa(j), lhsT=mixT[:, kc, j * 128:(j + 1) * 128], rhs=wo[:, kc, :], start=(kc == 0), stop=(kc == KD - 1)), reads=[mixT_b, wo_b], writes=[PA[j]])
                    for j in range(4):
                        ri = evn[0] % 2; evn[0] += 1
                        S.dve(lambda e, j=j, ng=ng, ri=ri: e.tensor_tensor(out=gsg[ri], in0=pa(j), in1=GArow[:, ng * 512:(ng + 1) * 512], op=ALU.mult), reads=[PA[j], grow_b], writes=[gsg_b[ri]])
                        S.dve(lambda e, j=j, ng=ng, ri=ri: e.tensor_tensor(out=x1[:, j, ng * 512:(ng + 1) * 512], in0=x1[:, j, ng * 512:(ng + 1) * 512], in1=gsg[ri], op=ALU.add), reads=[gsg_b[ri], x1_b[j]], writes=[x1_b[j]])
                for j in range(4):
                    prep(None, A2, B2, hT[:, :, j * 128:(j + 1) * 128], hT_b[j], xt=x1[:, j, :], xt_b=x1_b[j])
                region_switch()
                mix_guard = []
                for half in range(2):
                    for fg in range(8):
                        col0 = half * 4096 + fg * 512
                        w1, w1_b = wload(w_ff1[:, col0:col0 + 512], 16)
                        for mi in range(4):
                            jh = fg * 4 + mi
                            bank_i = mi
                            for kc in range(KD):
                                S.pe(lambda e, kc=kc, mi=mi, w1=w1, bank_i=bank_i: e.matmul(pa(bank_i), lhsT=w1[:, kc, mi * 128:(mi + 1) * 128], rhs=hT[:, kc, :], start=(kc == 0), stop=(kc == KD - 1)), reads=[w1_b] + hT_b, writes=[PA[bank_i]])
                            ri = evn[0] % 2; evn[0] += 1
                            S.act(lambda e, ri=ri, bank_i=bank_i: e.activation(out=rl[ri], in_=pa(bank_i), func=AF.Relu), reads=[PA[bank_i]] + mix_guard, writes=[rl_b[ri]])
                            S.dve(lambda e, ri=ri, jh=jh: e.tensor_tensor(out=hidT[:, jh, :], in0=rl[ri], in1=rl[ri], op=ALU.mult), reads=[rl_b[ri]] + mix_guard, writes=[hidT_b])
                    for ng in range(4):
                        for kg in range(2):
                            r0 = half * 4096 + kg * 2048
                            w2, w2_b = wload(w_ff2[r0:r0 + 2048, ng * 512:(ng + 1) * 512], 16)
                            for j in range(4):
                                for kc in range(KD):
                                    kk = kg * 16 + kc
                                    S.pe(lambda e, kc=kc, kk=kk, j=j, w2=w2, kg=kg: e.matmul(pa(j), lhsT=hidT[:, kk, j * 128:(j + 1) * 128], rhs=w2[:, kc, :], start=(kk == 0), stop=(kk == 31)), reads=[hidT_b, w2_b], writes=[PA[j]])
                        for j in range(4):
                            ri = evn[0] % 2; evn[0] += 1
                            S.dve(lambda e, j=j, ng=ng, ri=ri: e.tensor_tensor(out=rl[ri], in0=pa(j), in1=GMrow[:, ng * 512:(ng + 1) * 512], op=ALU.mult), reads=[PA[j], grow_b], writes=[rl_b[ri]])
                            S.dve(lambda e, j=j, ng=ng, ri=ri: e.tensor_tensor(out=x1[:, j, ng * 512:(ng + 1) * 512], in0=x1[:, j, ng * 512:(ng + 1) * 512], in1=rl[ri], op=ALU.add), reads=[rl_b[ri], x1_b[j]], writes=[x1_b[j]])
                for j in range(4):
                    S.dma("sp", y[t0 + j * 128:t0 + (j + 1) * 128, :], x1[:, j, :], reads=[x1_b[j]], writes=[Buf()])
            A.release()
        except _Stop:
            pass
        finals = [o for o in S.dma_pool["sp"]["last"] if o is not None]
        S.op("sp", lambda e: e.nop(), deps=finals)
        block = st.enter_context(nc.Block())
        stats = S.emit(block)
        if debug:
            print("ops/waits:", stats, "sems:", S.nsem, "arena peak:", A.peak)
    return nc


def rope_tables(tok_idx, n_ctx):
    half = 32
    inv_freq = (10000.0 ** (-np.arange(0, half, 2, dtype=np.float32) / half)).astype(np.float32)
    row = (tok_idx // GRID_W).astype(np.float32)
    col = (tok_idx % GRID_W).astype(np.float32)
    out = np.zeros((len(tok_idx) + n_ctx, 128), np.float32)
    for hh, pos in enumerate((row, col)):
        ang = (pos[:, None] * inv_freq[None, :]).astype(np.float32)
        c, s_ = np.cos(ang), np.sin(ang)
        for xi in range(2):
            out[:len(tok_idx), hh * 32 + xi * 16: hh * 32 + xi * 16 + 16] = c
            out[:len(tok_idx), 64 + hh * 32 + xi * 16: 64 + hh * 32 + xi * 16 + 16] = (-s_ if xi == 0 else s_)
    out[len(tok_idx):, 0:64] = 1.0
    return out


def band_tables(L, s):
    T = L // 2
    out = np.zeros((128, 3, 4, 3, 128), np.float32)
    tiles = {0: s * (T // 128), 1: s * (T // 128) + 1, 2: (s + 1) * (T // 128) - 1}
    for cls, gt in tiles.items():
        for g, w in enumerate(POOL_WINDOWS):
            for tl in range(128):
                t = gt * 128 + tl
                lo = min(max(t - w // 2, 0), L)
                hi = min(max(t + w - w // 2, 0), L)
                for tp in range(lo, hi):
                    n = tp // 128 - gt + 1
                    out[tp % 128, cls, g, n, tl] += 1.0 / (hi - lo)
                out[tl, cls, g, 1, tl] -= 1.0
    return out.reshape(128, 36 * 128).astype(ml_dtypes.bfloat16)


_NC_CACHE = {}


def run(inputs, L, C, B, debug=False, stop=99):
    key = (L, C)
    if key not in _NC_CACHE:
        _NC_CACHE[key] = build(L, C, debug=debug, stop=stop)
    nc = _NC_CACHE[key]
    T = L // 2
    f = lambda a: np.ascontiguousarray(np.asarray(a, dtype=np.float32))
    x = f(inputs["x"]); c = f(inputs["c"]); ctx = f(inputs["ctx"])
    shared = {"cctx": f(inputs["c_ctx"]), "ident": np.eye(128, dtype=np.float32).astype(ml_dtypes.bfloat16)}
    for k in ("w_mod", "b_mod", "norm_attn_w", "w_in", "q_norm_w", "k_norm_w", "lambda_q1", "lambda_k1", "lambda_q2",
              "lambda_k2", "subln_w", "pool_w", "pool_scale", "w_a_up", "w_b_up", "w_o", "norm_mlp_w", "w_ff1", "w_ff2"):
        shared[k] = f(inputs[k])[0]
    in_maps = []
    zeros128 = np.zeros((128, D), np.float32)
    for b in range(B):
        for s in range(2):
            own = slice(s * T, (s + 1) * T)
            oth = slice((1 - s) * T, (2 - s) * T)
            m = dict(shared)
            m["xo"] = x[b, own]
            m["xr"] = x[b, oth]
            before = x[b, s * T - 128:s * T] if s == 1 else zeros128
            after = x[b, (s + 1) * T:(s + 1) * T + 128] if s == 0 else zeros128
            m["xh"] = np.ascontiguousarray(np.concatenate([before, after], 0))
            m["ctx"] = ctx[b]
            m["cvec"] = c[b]
            tok = np.concatenate([np.arange(s * T, (s + 1) * T), np.arange((1 - s) * T, (2 - s) * T)])
            m["rope"] = rope_tables(tok, C).reshape(-1, 128, 128)
            m["bands"] = band_tables(L, s)
            in_maps.append(m)
    res = run_bass_kernel_spmd(nc, in_maps, core_ids=list(range(2 * B)))
    out = np.zeros((B, L, D), np.float32)
    for b in range(B):
        for s in range(2):
            out[b, s * T:(s + 1) * T] = res.results[b * 2 + s]["y"]
    if debug:
        return out, res.results
    return out


def kernel(**inputs):
    return run(inputs, 8192, 256, 4)
```

```python
import os
import numpy as np
import ml_dtypes
from contextlib import ExitStack
import concourse.bass as bass
import concourse.mybir as mybir
from concourse.bass_utils import run_bass_kernel_spmd

F32 = mybir.dt.float32
BF16 = mybir.dt.bfloat16
U8 = mybir.dt.uint8
AF = mybir.ActivationFunctionType
ALU = mybir.AluOpType
AX = mybir.AxisListType

ENGS = ("pe", "act", "dve", "pool", "sp")
SEM_ROT = 30000
D = 2048
KD = 16
EPS = 1e-6
GRID_W = 64
POOL_WINDOWS = (2, 4, 8, 16)


class Op:
    __slots__ = ("eng", "fn", "deps", "sig", "sem", "val", "is_dma")

    def __init__(self, eng, fn, deps, is_dma=False):
        self.eng = eng
        self.fn = fn
        self.deps = deps
        self.sig = False
        self.sem = None
        self.val = None
        self.is_dma = is_dma


class Buf:
    __slots__ = ("name", "w", "r", "rd", "excl")

    def __init__(self, name="", excl=False):
        self.name = name
        self.excl = excl
        self.w = []
        self.r = {}
        self.rd = []


class Sched:
    def __init__(self, nc, stack, n_dma_sems=8):
        self.nc = nc
        self.stack = stack
        self.ops = {e: [] for e in ENGS}
        self.nsem = 0
        self.dma_pool = {q: {"sems": [], "cnt": [], "last": [], "i": 0, "n": n_dma_sems} for q in ("sp", "pool")}

    def new_sem(self, name):
        self.nsem += 1
        return self.stack.enter_context(self.nc.semaphore(f"{name}{self.nsem}"))

    def _deps(self, eng, is_dma, reads, writes, extra):
        deps = list(extra)
        for b in reads:
            deps += b.w
        for b in writes:
            deps += b.w
            deps += list(b.r.values())
            deps += b.rd
        return [d for d in deps if d is not None]

    def _update(self, o, reads, writes, accumulate=False):
        for b in reads:
            if o.is_dma:
                b.rd.append(o)
            else:
                b.r[o.eng] = o
        for b in writes:
            if accumulate:
                b.w = b.w + [o]
            else:
                b.w = [o]
            b.r = {}
            b.rd = []

    @staticmethod
    def _split(reads, writes):
        ex = [b for b in reads if b.excl]
        if ex:
            reads = [b for b in reads if not b.excl]
            writes = list(writes) + ex
        return reads, writes

    def op(self, eng, fn, reads=(), writes=(), deps=()):
        reads, writes = self._split(reads, writes)
        o = Op(eng, fn, self._deps(eng, False, reads, writes, deps))
        self.ops[eng].append(o)
        self._update(o, reads, writes)
        return o

    def pe(self, fn, reads=(), writes=(), deps=()):
        return self.op("pe", fn, reads, writes, deps)

    def act(self, fn, reads=(), writes=(), deps=()):
        return self.op("act", fn, reads, writes, deps)

    def dve(self, fn, reads=(), writes=(), deps=()):
        return self.op("dve", fn, reads, writes, deps)

    def dma(self, q, out, in_, reads=(), writes=(), deps=(), accumulate=False, **kw):
        P = self.dma_pool[q]
        if len(P["sems"]) < P["n"]:
            P["sems"].append(self.new_sem(f"d{q}"))
            P["cnt"].append(0)
            P["last"].append(None)
            i = len(P["sems"]) - 1
        else:
            i = P["i"] % P["n"]
        P["i"] += 1
        if P["cnt"][i] + 16 > SEM_ROT:
            P["sems"][i] = self.new_sem(f"d{q}")
            P["cnt"][i] = 0
        reads, writes = self._split(reads, writes)
        d = self._deps(q, True, reads, writes, deps)
        if accumulate:
            prevw = set(id(x) for b in writes for x in b.w if x.is_dma)
            d = [x for x in d if id(x) not in prevw]
        if P["last"][i] is not None:
            d.append(P["last"][i])
        o = Op(q, lambda e, out=out, in_=in_, kw=kw: e.dma_start(out=out, in_=in_, **kw), d, is_dma=True)
        P["cnt"][i] += 16
        o.sem = P["sems"][i]
        o.val = P["cnt"][i]
        o.sig = True
        P["last"][i] = o
        self.ops[q].append(o)
        self._update(o, reads, writes, accumulate)
        return o

    def barrier(self):
        toks = []
        for e in ENGS:
            for o in reversed(self.ops[e]):
                if not o.is_dma and o.fn is not None:
                    toks.append(o)
                    break
        for q in self.dma_pool:
            toks += [o for o in self.dma_pool[q]["last"] if o is not None]
        for e in ENGS:
            o = Op(e, None, list(toks))
            self.ops[e].append(o)

    def emit(self, block):
        for e in ENGS:
            for o in self.ops[e]:
                for d in o.deps:
                    if not d.is_dma and not (d.eng == "pe" and o.eng == "pe" and not o.is_dma):
                        d.sig = True
        for e in ENGS:
            sem = None
            cnt = 0
            for o in self.ops[e]:
                if o.is_dma or not o.sig:
                    continue
                if sem is None or cnt >= SEM_ROT:
                    sem = self.new_sem(f"c{e}")
                    cnt = 0
                cnt += 1
                o.sem = sem
                o.val = cnt
        stats = {}

        def run(eng_name, e):
            waited = {}
            nw = 0
            for o in self.ops[eng_name]:
                for d in o.deps:
                    if d.eng == "pe" and eng_name == "pe" and not d.is_dma and not o.is_dma:
                        continue
                    k = id(d.sem)
                    if waited.get(k, 0) >= d.val:
                        continue
                    waited[k] = d.val
                    e.wait_ge(d.sem, d.val)
                    nw += 1
                if o.fn is None:
                    continue
                ins = o.fn(e)
                if o.sig:
                    ins.then_inc(o.sem, 16 if o.is_dma else 1)
            stats[eng_name] = (len(self.ops[eng_name]), nw)

        @block.tensor
        def _(e):
            run("pe", e)

        @block.scalar
        def _(e):
            run("act", e)

        @block.vector
        def _(e):
            run("dve", e)

        @block.gpsimd
        def _(e):
            run("pool", e)

        @block.sync
        def _(e):
            run("sp", e)

        return stats


class Arena:
    def __init__(self, tensor, nbytes):
        self.t = tensor
        self.n = nbytes
        self.off = 0
        self.marks = []
        self.peak = 0

    def mark(self):
        self.marks.append(self.off)

    def release(self):
        self.off = self.marks.pop()

    def alloc(self, free_shape, dt):
        esz = {F32: 4, BF16: 2, U8: 1}[dt]
        n = int(np.prod(free_shape)) * esz
        assert self.off + n <= self.n, f"arena overflow {self.off}+{n}>{self.n}"
        a = self.t[:, self.off:self.off + n]
        if dt != U8:
            a = a.bitcast(dt)
        self.off += (n + 63) // 64 * 64
        self.peak = max(self.peak, self.off)
        if len(free_shape) > 1:
            names = " ".join(f"d{i}" for i in range(len(free_shape)))
            kw = {f"d{i}": int(free_shape[i]) for i in range(len(free_shape))}
            a = a.rearrange(f"p ({names}) -> p {names}", **kw)
        return a


class _Stop(Exception):
    pass


def build(L, C, debug=False, stop=99):
    T = L // 2
    NT = T // 128
    NO = (L - T) // 128
    NCT = C // 128
    NKT = NT + NO + NCT
    LK = NKT * 128
    NS = T // 512
    NQ = T // 512
    nc = bass.Bass("TRN2", target_bir_lowering=False)

    def din(name, shape, dt=F32):
        return nc.dram_tensor(name, list(shape), dt, kind="ExternalInput").ap()

    xo = din("xo", [T, D]); xr = din("xr", [L - T, D]); xh = din("xh", [256, D]); ctx = din("ctx", [C, D])
    cvec = din("cvec", [D]); cctx = din("cctx", [D])
    w_mod = din("w_mod", [D, 6 * D]); b_mod = din("b_mod", [6 * D]); naw_d = din("norm_attn_w", [D])
    w_in = din("w_in", [D, 8192]); qnw = din("q_norm_w", [64]); knw = din("k_norm_w", [64])
    lq1 = din("lambda_q1", [64]); lk1 = din("lambda_k1", [64]); lq2 = din("lambda_q2", [64]); lk2 = din("lambda_k2", [64])
    subln = din("subln_w", [128]); pool_w = din("pool_w", [4, 256, 256]); pool_scale = din("pool_scale", [1024])
    w_a_up = din("w_a_up", [1024, D]); w_b_up = din("w_b_up", [1024, D]); w_o = din("w_o", [D, D])
    nmw_d = din("norm_mlp_w", [D]); w_ff1 = din("w_ff1", [D, 4 * D]); w_ff2 = din("w_ff2", [4 * D, D])
    ident_d = din("ident", [128, 128], BF16); rope_d = din("rope", [NKT, 128, 128]); bands_d = din("bands", [128, 36 * 128], BF16)
    y = nc.dram_tensor("y", [T, D], F32, kind="ExternalOutput").ap()
    sk = dict(kind="ExternalOutput") if debug else {}
    KTs = nc.dram_tensor("KTs", [8, 128, LK], BF16, **sk).ap()
    Vs = nc.dram_tensor("Vs", [LK, 1024], BF16, **sk).ap()
    QTs = nc.dram_tensor("QTs", [8, 128, T], BF16, **sk).ap()
    Us = nc.dram_tensor("Us", [(NT + 2) * 128, 1024], BF16, **sk).ap()
    HTs = nc.dram_tensor("HTs", [8, 128, T], BF16, **sk).ap()
    Wg_s = nc.dram_tensor("Wg_s", [D, 4096], BF16).ap()
    Wa_s = nc.dram_tensor("Wa_s", [1024, D], BF16).ap()
    Wb_s = nc.dram_tensor("Wb_s", [1024, D], BF16).ap()
    Wo_s = nc.dram_tensor("Wo_s", [D, D], BF16).ap()
    W1_s = nc.dram_tensor("W1_s", [D, 4 * D], BF16).ap()
    W2_s = nc.dram_tensor("W2_s", [4 * D, D], BF16).ap()
    pre = {}
    KTs_b = [Buf() for _ in range(NKT)]; Vs_b = [Buf() for _ in range(NKT)]
    QTs_b = [Buf() for _ in range(NT)]; Us_b = [Buf() for _ in range(NT + 2)]
    HTs_b = [[Buf() for _ in range(NQ)] for _ in range(8)]
    dbg = {}

    with ExitStack() as st:
        ARENA_BYTES = 200 * 1024
        arena_t = st.enter_context(nc.sbuf_tensor("arena", [128, ARENA_BYTES], U8))
        psA = st.enter_context(nc.psum_tensor("psA", [128, 2048], F32))
        psB = st.enter_context(nc.psum_tensor("psB", [128, 2048], F32))
        S = Sched(nc, st)
        A = Arena(arena_t, ARENA_BYTES)
        PA = [Buf(f"PA{i}", excl=True) for i in range(4)]
        PB = [Buf(f"PB{i}", excl=True) for i in range(4)]

        def pa(i):
            return psA[:, i * 512:(i + 1) * 512]

        def pb(i):
            return psB[:, i * 512:(i + 1) * 512]


        ident = A.alloc([128], BF16); ident_b = Buf()
        bands = A.alloc([36, 128], BF16); bands_b = Buf()
        A1 = A.alloc([16], F32); B1 = A.alloc([16], F32); A1c = A.alloc([16], F32); B1c = A.alloc([16], F32)
        A2 = A.alloc([16], F32); B2 = A.alloc([16], F32)
        modv_b = Buf()
        GArow = A.alloc([D], F32); GMrow = A.alloc([D], F32); grow_b = Buf()
        wq_bc = A.alloc([64], F32); wk_bc = A.alloc([64], F32); wqk_b = Buf()
        swb = A.alloc([128], F32); swb_b = Buf()
        pscale = A.alloc([8], F32); pscale_b = Buf()
        neglam = A.alloc([1], F32); nshift = A.alloc([1], F32); epsc = A.alloc([1], F32); misc_b = Buf()

        S.dma("sp", ident, ident_d, writes=[ident_b])
        S.dma("sp", bands.rearrange("p a b -> p (a b)"), bands_d, writes=[bands_b])
        S.dve(lambda e: e.memset(epsc, EPS), writes=[misc_b])

        try:
            A.mark()
            naw = A.alloc([16], F32); nmw = A.alloc([16], F32); bcol = A.alloc([6, 16], F32)
            c0 = A.alloc([16], F32); c1 = A.alloc([16], F32)
            scb = A.alloc([16, 2], BF16); rep = A.alloc([16, 128], BF16)
            lam4 = A.alloc([4, 64], F32); lamp = A.alloc([2, 64], F32); lams = A.alloc([2], F32); lame = A.alloc([2], F32)
            sbc = A.alloc([128], F32); wsq = A.alloc([2, 64], F32); wmx = A.alloc([2], F32)
            cols = A.alloc([4, 16, 2], F32)
            brow = A.alloc([2, D], F32)
            tmp16 = A.alloc([16], F32)
            p0_b = Buf()
            NCD = dict(allow_slow_non_contiguous=True)
            S.dma("sp", naw, naw_d.rearrange("(k p) -> p k", p=128), writes=[p0_b], accumulate=True, **NCD)
            S.dma("sp", nmw, nmw_d.rearrange("(k p) -> p k", p=128), writes=[p0_b], accumulate=True, **NCD)
            for j in range(6):
                S.dma("sp", bcol[:, j, :], b_mod[j * D:(j + 1) * D].rearrange("(k p) -> p k", p=128), writes=[p0_b], accumulate=True, **NCD)
            S.dma("sp", c0, cvec.rearrange("(k p) -> p k", p=128), writes=[p0_b], accumulate=True, **NCD)
            S.dma("sp", c1, cctx.rearrange("(k p) -> p k", p=128), writes=[p0_b], accumulate=True, **NCD)
            S.dma("sp", pscale, pool_scale.rearrange("(k p) -> p k", p=128), writes=[pscale_b], **NCD)
            S.dma("sp", wq_bc, qnw.partition_broadcast(128), writes=[wqk_b], accumulate=True)
            S.dma("sp", wk_bc, knw.partition_broadcast(128), writes=[wqk_b], accumulate=True)
            for i, lv in enumerate((lq1, lk1, lq2, lk2)):
                S.dma("sp", lam4[:, i, :], lv.partition_broadcast(128), writes=[p0_b], accumulate=True)
            S.dma("sp", sbc, subln.partition_broadcast(128), writes=[p0_b], accumulate=True)
            S.dma("sp", brow[:, 0, :], b_mod[2 * D:3 * D].partition_broadcast(128), writes=[p0_b], accumulate=True)
            S.dma("sp", brow[:, 1, :], b_mod[5 * D:6 * D].partition_broadcast(128), writes=[p0_b], accumulate=True)
            sc_b = Buf()
            S.act(lambda e: e.activation(out=scb[:, :, 0], in_=c0, func=AF.Silu), reads=[p0_b], writes=[sc_b])
            S.act(lambda e: e.activation(out=scb[:, :, 1], in_=c1, func=AF.Silu), reads=[p0_b, sc_b], writes=[sc_b])
            rep_b = Buf()
            S.dve(lambda e: e.tensor_copy(out=rep, in_=scb[:, :, 0:1].to_broadcast([128, 16, 128])), reads=[sc_b], writes=[rep_b])
            S.dve(lambda e: e.tensor_tensor(out=lamp[:, 0, :], in0=lam4[:, 0, :], in1=lam4[:, 1, :], op=ALU.mult), reads=[p0_b], writes=[misc_b])
            S.dve(lambda e: e.tensor_tensor(out=lamp[:, 1, :], in0=lam4[:, 2, :], in1=lam4[:, 3, :], op=ALU.mult), reads=[misc_b], writes=[misc_b])
            S.dve(lambda e: e.tensor_reduce(out=lams, in_=lamp, axis=AX.X, op=ALU.add), reads=[misc_b], writes=[misc_b])
            S.act(lambda e: e.activation(out=lame, in_=lams, func=AF.Exp), reads=[misc_b], writes=[misc_b])
            S.dve(lambda e: e.tensor_tensor(out=neglam, in0=lame[:, 1:2], in1=lame[:, 0:1], op=ALU.subtract), reads=[misc_b], writes=[misc_b])
            S.dve(lambda e: e.tensor_scalar(out=neglam, in0=neglam, scalar1=-0.2, scalar2=None, op0=ALU.add), reads=[misc_b], writes=[misc_b])
            S.dve(lambda e: e.tensor_tensor(out=wsq[:, 0, :], in0=wq_bc, in1=wq_bc, op=ALU.mult), reads=[wqk_b, misc_b], writes=[misc_b])
            S.dve(lambda e: e.tensor_tensor(out=wsq[:, 1, :], in0=wk_bc, in1=wk_bc, op=ALU.mult), reads=[wqk_b, misc_b], writes=[misc_b])
            S.dve(lambda e: e.tensor_reduce(out=wmx[:, 0:1], in_=wsq.rearrange("p a b -> p (a b)"), axis=AX.X, op=ALU.max), reads=[misc_b], writes=[misc_b])
            S.dve(lambda e: e.tensor_scalar(out=nshift, in0=wmx[:, 0:1], scalar1=-8.0, scalar2=None, op0=ALU.mult), reads=[misc_b], writes=[misc_b])
            S.dve(lambda e: e.tensor_scalar(out=swb, in0=sbc, scalar1=0.8, scalar2=None, op0=ALU.mult), reads=[p0_b], writes=[swb_b])

            NW0 = 6
            wst = [A.alloc([16, 512], BF16) for _ in range(NW0)]
            wst_b = [Buf() for _ in range(NW0)]
            colmap = {0: 0, 1: 1, 3: 2, 4: 3}
            n = 0
            for j in range(6):
                for tq in range(4):
                    wb = wst[n % NW0]; wbb = wst_b[n % NW0]; n += 1
                    c0_ = j * D + tq * 512
                    S.dma("pool", wb, w_mod[:, c0_:c0_ + 512].rearrange("(k p) n -> p k n", p=128), writes=[wbb])
                    if j in colmap:
                        jj = colmap[j]
                        for m in range(4):
                            mm_ = tq * 4 + m
                            o_ap = psB[:, (jj * 16 + mm_) * 2:(jj * 16 + mm_) * 2 + 2]
                            for kc in range(KD):
                                S.pe(lambda e, o_ap=o_ap, wb=wb, m=m, kc=kc: e.matmul(o_ap, lhsT=wb[:, kc, m * 128:(m + 1) * 128], rhs=scb[:, kc, :], start=(kc == 0), stop=(kc == KD - 1)),
                                     reads=[wbb, sc_b], writes=[PB[0]])
                    else:
                        bank = PA[tq]
                        for kc in range(KD):
                            S.pe(lambda e, tq=tq, wb=wb, kc=kc: e.matmul(pa(tq), lhsT=rep[:, kc, :], rhs=wb[:, kc, :], start=(kc == 0), stop=(kc == KD - 1)),
                                 reads=[wbb, rep_b], writes=[bank])
                        row = GArow if j == 2 else GMrow
                        bi = 0 if j == 2 else 1
                        S.dve(lambda e, row=row, tq=tq, bi=bi: e.tensor_tensor(out=row[:, tq * 512:(tq + 1) * 512], in0=pa(tq), in1=brow[:, bi, tq * 512:(tq + 1) * 512], op=ALU.add),
                              reads=[bank, p0_b], writes=[grow_b])
            S.dve(lambda e: e.tensor_copy(out=cols.rearrange("p a b c -> p (a b c)"), in_=psB[:, 0:128]), reads=[PB[0]], writes=[modv_b])
            def mk_scale(dst, jj, which, bj, nw_):
                S.dve(lambda e: e.tensor_tensor(out=tmp16, in0=cols[:, jj, :, which], in1=bcol[:, bj, :], op=ALU.add), reads=[modv_b, p0_b], writes=[modv_b])
                S.dve(lambda e: e.scalar_tensor_tensor(out=dst, in0=tmp16, scalar=1.0, in1=nw_, op0=ALU.add, op1=ALU.mult), reads=[modv_b], writes=[modv_b])

            def mk_shift(dst, jj, which, bj):
                S.dve(lambda e: e.tensor_tensor(out=dst, in0=cols[:, jj, :, which], in1=bcol[:, bj, :], op=ALU.add), reads=[modv_b, p0_b], writes=[modv_b])

            mk_scale(A1, 1, 0, 1, naw); mk_shift(B1, 0, 0, 0)
            mk_scale(A1c, 1, 1, 1, naw); mk_shift(B1c, 0, 1, 0)
            mk_scale(A2, 3, 0, 4, nmw); mk_shift(B2, 2, 0, 3)
            A.release()
            S.barrier()
            if stop == 0:
                raise _Stop()

            A.mark()
            xbuf = [A.alloc([D], F32) for _ in range(2)]; xbuf_b = [Buf() for _ in range(2)]
            sqj = A.alloc([D], BF16); sqj_b = Buf()
            xs = [A.alloc([D], BF16) for _ in range(2)]; xs_b = [Buf() for _ in range(2)]
            st8 = [A.alloc([8], F32) for _ in range(2)]; st8_b = [Buf() for _ in range(2)]
            prep_n = [0]

            def prep(src_ap, Av, Bv, hT, hT_b, xt=None, xt_b=None):
                i = prep_n[0] % 2; prep_n[0] += 1
                if xt is None:
                    xt, xt_b = xbuf[i], xbuf_b[i]
                    S.dma("sp", xt, src_ap, writes=[xt_b])
                s8, s8b = st8[i], st8_b[i]
                xsi, xsi_b, sqj_, sqj_b_ = xs[i], xs_b[i], sqj, sqj_b
                PL = int(os.environ.get("PL", "9"))
                S.dve(lambda e: e.memset(s8[:, 0:1], 0.0), writes=[s8b])
                if PL < 1: return
                S.act(lambda e: e.activation(out=sqj_, in_=xt, func=AF.Square, accum_out=s8[:, 0:1]), reads=[xt_b], writes=[sqj_b_, s8b])
                if PL < 2: return
                S.act(lambda e: e.activation(out=s8[:, 1:2], in_=s8[:, 0:1], func=AF.Ln, scale=1.0 / D, bias=epsc), reads=[s8b, misc_b], writes=[s8b])
                S.act(lambda e: e.activation(out=s8[:, 2:3], in_=s8[:, 1:2], func=AF.Exp, scale=-0.5), reads=[s8b], writes=[s8b])
                S.act(lambda e: e.activation(out=xsi, in_=xt, func=AF.Copy, scale=s8[:, 2:3]), reads=[xt_b, s8b], writes=[xsi_b])
                if PL < 3: return
                for r in range(4):
                    for c in range(4):
                        kc = r * 4 + c
                        S.pe(lambda e, r=r, c=c, kc=kc: e.matmul(pb(r)[:, c * 128:(c + 1) * 128], lhsT=xsi[:, kc * 128:(kc + 1) * 128], rhs=ident, start=True, stop=True),
                             reads=[xsi_b, ident_b], writes=[PB[r]])
                    if PL < 4: continue
                    for c in range(4):
                        kc = r * 4 + c
                        S.act(lambda e, r=r, c=c, kc=kc: e.activation(out=hT[:, kc, :], in_=pb(r)[:, c * 128:(c + 1) * 128], func=AF.Identity, scale=Av[:, kc:kc + 1], bias=Bv[:, kc:kc + 1]),
                              reads=[PB[r], modv_b], writes=[hT_b])

            sq = [A.alloc([512], F32) for _ in range(2)]; qn = [A.alloc([512], F32) for _ in range(2)]
            ta = [A.alloc([512], F32) for _ in range(2)]; tb = [A.alloc([512], F32) for _ in range(2)]
            rs16 = [A.alloc([16], F32) for _ in range(2)]
            pp_b = [Buf() for _ in range(2)]
            pp_n = [0]

            def qk_post(bank, bank_b, dst, dst_b, wbc, rope_t, rope_b):
                i = pp_n[0] % 2; pp_n[0] += 1
                b = pp_b[i]
                v3 = lambda a: a.rearrange("p (s d) -> p s d", s=8)
                v5 = lambda a: a.rearrange("p (s h x i) -> p s h x i", s=8, h=2, x=2, i=16)
                cos4 = rope_t[:, 0:64]; sin4 = rope_t[:, 64:128]
                sin5 = sin4.rearrange("p (h x i) -> p h x i", h=2, x=2, i=16)
                S.act(lambda e: e.activation(out=sq[i], in_=bank, func=AF.Square), reads=[bank_b], writes=[b])
                S.dve(lambda e: e.tensor_reduce(out=rs16[i][:, 0:8], in_=v3(sq[i]), axis=AX.X, op=ALU.add), reads=[b], writes=[b])
                S.act(lambda e: e.activation(out=rs16[i][:, 8:16], in_=rs16[i][:, 0:8], func=AF.Ln, scale=1.0 / 64, bias=epsc), reads=[b, misc_b], writes=[b])
                S.act(lambda e: e.activation(out=rs16[i][:, 0:8], in_=rs16[i][:, 8:16], func=AF.Exp, scale=-0.5), reads=[b], writes=[b])
                S.dve(lambda e: e.tensor_tensor(out=v3(qn[i]), in0=v3(bank), in1=rs16[i][:, 0:8].unsqueeze(2).to_broadcast([128, 8, 64]), op=ALU.mult), reads=[bank_b, b], writes=[b])
                S.dve(lambda e: e.tensor_tensor(out=v3(qn[i]), in0=v3(qn[i]), in1=wbc.unsqueeze(1).to_broadcast([128, 8, 64]), op=ALU.mult), reads=[b, wqk_b], writes=[b])
                S.dve(lambda e: e.tensor_tensor(out=v3(ta[i]), in0=v3(qn[i]), in1=cos4.unsqueeze(1).to_broadcast([128, 8, 64]), op=ALU.mult), reads=[b, rope_b], writes=[b])
                S.dve(lambda e: e.tensor_tensor(out=v5(tb[i])[:, :, :, 0, :], in0=v5(qn[i])[:, :, :, 1, :], in1=sin5[:, :, 0, :].unsqueeze(1).to_broadcast([128, 8, 2, 16]), op=ALU.mult), reads=[b, rope_b], writes=[b])
                S.dve(lambda e: e.tensor_tensor(out=v5(tb[i])[:, :, :, 1, :], in0=v5(qn[i])[:, :, :, 0, :], in1=sin5[:, :, 1, :].unsqueeze(1).to_broadcast([128, 8, 2, 16]), op=ALU.mult), reads=[b, rope_b], writes=[b])
                S.dve(lambda e: e.tensor_tensor(out=dst, in0=ta[i], in1=tb[i], op=ALU.add), reads=[b], writes=[dst_b])

            wres = A.alloc([16, 2048], BF16); wres_b = [Buf() for _ in range(4)]
            hTa = [A.alloc([16, 128], BF16) for _ in range(2)]; hTa_b = [Buf() for _ in range(2)]
            ropeb = [A.alloc([128], F32) for _ in range(2)]; ropeb_b = [Buf() for _ in range(2)]
            qkb = [A.alloc([1024], BF16) for _ in range(2)]; qkb_b = [Buf() for _ in range(2)]
            ktst = [A.alloc([8, 128], BF16) for _ in range(2)]; ktst_b = [Buf() for _ in range(2)]
            vst = [A.alloc([1024], BF16) for _ in range(2)]; vst_b = [Buf() for _ in range(2)]

            def load_wres(col0):
                for g in range(4):
                    S.dma("pool", wres[:, :, g * 512:(g + 1) * 512], w_in[:, col0 + g * 512:col0 + (g + 1) * 512].rearrange("(k p) n -> p k n", p=128), writes=[wres_b[g]])

            def project(hT, hT_b, g, bank_i):
                for kc in range(KD):
                    S.pe(lambda e, g=g, kc=kc, bank_i=bank_i: e.matmul(pa(bank_i), lhsT=hT[:, kc, :], rhs=wres[:, kc, g * 512:(g + 1) * 512], start=(kc == 0), stop=(kc == KD - 1)),
                         reads=[hT_b, wres_b[g]], writes=[PA[bank_i]])

            def qk_transpose_store(i, dstT, dstT_b, col0, q):
                for r in range(2):
                    for c in range(4):
                        hh = r * 4 + c
                        S.pe(lambda e, r=r, c=c, hh=hh: e.matmul(pb(2 + r)[:, c * 128:(c + 1) * 128], lhsT=qkb[i][:, hh * 128:(hh + 1) * 128], rhs=ident, start=True, stop=True),
                             reads=[qkb_b[i], ident_b], writes=[PB[2 + r]])
                    S.dve(lambda e, r=r: e.tensor_copy(out=ktst[i][:, r * 4:(r + 1) * 4, :].rearrange("p a b -> p (a b)"), in_=pb(2 + r)), reads=[PB[2 + r]], writes=[ktst_b[i]])
                S.dma("sp", dstT[:, :, col0:col0 + 128].rearrange("h p t -> p h t"), ktst[i], reads=[ktst_b[i]], writes=[dstT_b])

            load_wres(1024)

            def precast(key, dst, src):
                pre[key] = Buf()
                S.dma("pool", dst, src, writes=[pre[key]])

            for mg in range(4):
                precast(("a", mg), Wa_s[:, mg * 512:(mg + 1) * 512], w_a_up[:, mg * 512:(mg + 1) * 512])
                precast(("b", mg), Wb_s[:, mg * 512:(mg + 1) * 512], w_b_up[:, mg * 512:(mg + 1) * 512])
                precast(("g", mg), Wg_s[:, mg * 512:(mg + 1) * 512], w_in[:, 4096 + mg * 512:4096 + (mg + 1) * 512])
                precast(("g", 4 + mg), Wg_s[:, (4 + mg) * 512:(5 + mg) * 512], w_in[:, 6144 + mg * 512:6144 + (mg + 1) * 512])
            for ng in range(4):
                precast(("o", ng), Wo_s[:, ng * 512:(ng + 1) * 512], w_o[:, ng * 512:(ng + 1) * 512])
            for half in range(2):
                for fg in range(8):
                    cb = half * 8 + fg
                    precast(("1", cb), W1_s[:, cb * 512:(cb + 1) * 512], w_ff1[:, cb * 512:(cb + 1) * 512])
                for ng in range(4):
                    for kg in range(2):
                        rg = half * 2 + kg
                        precast(("2", rg, ng), W2_s[rg * 2048:(rg + 1) * 2048, ng * 512:(ng + 1) * 512], w_ff2[rg * 2048:(rg + 1) * 2048, ng * 512:(ng + 1) * 512])
            def a1_prep(kt):
                i = kt % 2
                if kt < NT:
                    src = xo[kt * 128:(kt + 1) * 128, :]; Av, Bv = A1, B1
                elif kt < NT + NO:
                    src = xr[(kt - NT) * 128:(kt - NT + 1) * 128, :]; Av, Bv = A1, B1
                else:
                    src = ctx[(kt - NT - NO) * 128:(kt - NT - NO + 1) * 128, :]; Av, Bv = A1c, B1c
                S.dma("sp", ropeb[i], rope_d[kt], writes=[ropeb_b[i]])
                prep(src, Av, Bv, hTa[i], hTa_b[i])

            a1_prep(0)
            for kt in range(NKT):
                i = kt % 2
                if kt + 1 < NKT:
                    a1_prep(kt + 1)
                for g in range(4):
                    project(hTa[i], hTa_b[i], g, g)
                for g in range(2):
                    qk_post(pa(g), PA[g], qkb[i][:, g * 512:(g + 1) * 512], qkb_b[i], wk_bc, ropeb[i], ropeb_b[i])
                qk_transpose_store(i, KTs, KTs_b[kt], kt * 128, False)
                for g in range(2):
                    S.act(lambda e, g=g, i=i: e.activation(out=vst[i][:, g * 512:(g + 1) * 512], in_=pa(2 + g), func=AF.Copy), reads=[PA[2 + g]], writes=[vst_b[i]])
                S.dma("sp", Vs[kt * 128:(kt + 1) * 128, :], vst[i], reads=[vst_b[i]], writes=[Vs_b[kt]])

            for g in range(2):
                S.dma("pool", wres[:, :, g * 512:(g + 1) * 512], w_in[:, g * 512:(g + 1) * 512].rearrange("(k p) n -> p k n", p=128), writes=[wres_b[g]])
            for g in range(2):
                S.dma("pool", wres[:, :, (2 + g) * 512:(3 + g) * 512], w_in[:, 3072 + g * 512:3072 + (g + 1) * 512].rearrange("(k p) n -> p k n", p=128), writes=[wres_b[2 + g]])
            def a2_prep(tt):
                i = tt % 2
                if tt < NT:
                    src = xo[tt * 128:(tt + 1) * 128, :]
                    S.dma("sp", ropeb[i], rope_d[tt], writes=[ropeb_b[i]])
                elif tt == NT:
                    src = xh[0:128, :]
                else:
                    src = xh[128:256, :]
                prep(src, A1, B1, hTa[i], hTa_b[i])

            a2_prep(0)
            for tt in range(NT + 2):
                i = tt % 2
                urow = (1 + tt) * 128 if tt < NT else (0 if tt == NT else (NT + 1) * 128)
                if tt + 1 < NT + 2:
                    a2_prep(tt + 1)
                if tt < NT:
                    for g in range(2):
                        project(hTa[i], hTa_b[i], g, g)
                for g in range(2):
                    project(hTa[i], hTa_b[i], 2 + g, 2 + g)
                if tt < NT:
                    for g in range(2):
                        qk_post(pa(g), PA[g], qkb[i][:, g * 512:(g + 1) * 512], qkb_b[i], wq_bc, ropeb[i], ropeb_b[i])
                    qk_transpose_store(i, QTs, QTs_b[tt], tt * 128, True)
                for g in range(2):
                    S.act(lambda e, g=g, i=i: e.activation(out=vst[i][:, g * 512:(g + 1) * 512], in_=pa(2 + g), func=AF.Copy), reads=[PA[2 + g]], writes=[vst_b[i]])
                S.dma("sp", Us[urow:urow + 128, :], vst[i], reads=[vst_b[i]], writes=[Us_b[urow // 128]])
            A.release()
            S.barrier()
            if stop == 1:
                raise _Stop()

            A.mark()
            QT = [A.alloc([T], BF16) for _ in range(2)]; QT_b = [Buf() for _ in range(2)]
            KT = [A.alloc([LK], BF16) for _ in range(2)]; KT_b = [Buf() for _ in range(2)]
            V1 = [A.alloc([NKT, 130], BF16) for _ in range(2)]; V1_b = [Buf() for _ in range(2)]
            Pb = [A.alloc([1024], BF16) for _ in range(3)]; Pb_b = [Buf() for _ in range(3)]
            Osb = A.alloc([1536], F32); Osb_b = Buf()
            rz = A.alloc([16], F32); dd = A.alloc([128], F32); dj = A.alloc([128], F32); ss4 = A.alloc([8], F32)
            ep_b = Buf()
            hn = A.alloc([512], BF16); hn_b = Buf()
            hst = [A.alloc([512], BF16) for _ in range(2)]; hst_b = [Buf() for _ in range(2)]
            for i in range(2):
                S.dve(lambda e, i=i: e.memset(V1[i][:, :, 128:130], 1.0), writes=[V1_b[i]])

            def oreg(r):
                bank = r // 3
                off = bank * 512 + (r % 3) * 160
                return bank, off

            pcount = 0
            scount = 0
            def load_head(h):
                hb = h % 2
                S.dma("sp", QT[hb], QTs[h], reads=QTs_b, writes=[QT_b[hb]])
                S.dma("sp", KT[hb], KTs[h], reads=KTs_b, writes=[KT_b[hb]])
                S.dma("sp", V1[hb][:, :, 0:128], Vs[:, h * 128:(h + 1) * 128].rearrange("(k p) v -> p k v", p=128), reads=Vs_b, writes=[V1_b[hb]], accumulate=True)

            def emit_S(n, h, qi, kt):
                hb = h % 2; sb = n % 2; pi = n % 3
                S.pe(lambda e: e.matmul(psA[:, sb * 1024:sb * 1024 + 512], lhsT=KT[hb][0:64, kt * 128:(kt + 1) * 128], rhs=QT[hb][0:64, qi * 512:(qi + 1) * 512], start=True, stop=True),
                     reads=[KT_b[hb], QT_b[hb]], writes=[PA[2 * sb]])
                S.pe(lambda e: e.matmul(psA[:, sb * 1024 + 512:sb * 1024 + 1024], lhsT=KT[hb][64:128, kt * 128:(kt + 1) * 128], rhs=QT[hb][64:128, qi * 512:(qi + 1) * 512], start=True, stop=True),
                     reads=[KT_b[hb], QT_b[hb]], writes=[PA[2 * sb + 1]])
                S.act(lambda e: e.activation(out=Pb[pi], in_=psA[:, sb * 1024:(sb + 1) * 1024], func=AF.Exp, scale=0.125, bias=nshift),
                      reads=[PA[2 * sb], PA[2 * sb + 1], misc_b], writes=[Pb_b[pi]])

            def emit_PV(n, h, qi, kt):
                hb = h % 2; pi = n % 3
                for sub in range(2):
                    for qs in range(4):
                        bank, off = oreg(sub * 4 + qs)
                        S.pe(lambda e, sub=sub, qs=qs, off=off: e.matmul(psB[:, off:off + 129], lhsT=Pb[pi][:, sub * 512 + qs * 128:sub * 512 + (qs + 1) * 128], rhs=V1[hb][:, kt, 0:129], start=(kt == 0 and (sub * 4 + qs) % 3 == 0), stop=(kt == NKT - 1), skip_group_check=True),
                             reads=[Pb_b[pi], V1_b[hb]], writes=[PB[bank]])

            dd4 = A.alloc([4, 128], F32)

            def epilogue_a(h, qi):
                S.dve(lambda e: e.tensor_copy(out=Osb, in_=psB[:, 0:1536]), reads=PB[0:3], writes=[Osb_b])
                for r in range(8):
                    _, off = oreg(r)
                    S.dve(lambda e, r=r, off=off: e.reciprocal(out=rz[:, r:r + 1], in_=Osb[:, off + 128:off + 129]), reads=[Osb_b], writes=[ep_b])
                S.dve(lambda e: e.tensor_scalar(out=rz[:, 8:12], in0=rz[:, 4:8], scalar1=neglam, scalar2=None, op0=ALU.mult), reads=[ep_b, misc_b], writes=[ep_b])
                for qs in range(4):
                    _, off0 = oreg(qs)
                    _, off1 = oreg(4 + qs)
                    S.dve(lambda e, qs=qs, off0=off0: e.tensor_scalar(out=dd4[:, qs, :], in0=Osb[:, off0:off0 + 128], scalar1=rz[:, qs:qs + 1], scalar2=None, op0=ALU.mult), reads=[Osb_b, ep_b], writes=[ep_b])
                    S.dve(lambda e, qs=qs, off1=off1: e.scalar_tensor_tensor(out=dd4[:, qs, :], in0=Osb[:, off1:off1 + 128], scalar=rz[:, 8 + qs:9 + qs], in1=dd4[:, qs, :], op0=ALU.mult, op1=ALU.add), reads=[Osb_b, ep_b], writes=[ep_b])
                    S.dve(lambda e, qs=qs: e.scalar_tensor_tensor(out=dj, in0=dd4[:, qs, :], scalar=1.0, in1=dd4[:, qs, :], op0=ALU.mult, op1=ALU.mult, accum_out=ss4[:, qs:qs + 1]), reads=[ep_b], writes=[ep_b])
                S.act(lambda e: e.activation(out=rz[:, 12:16], in_=ss4[:, 0:4], func=AF.Ln, scale=1.0 / 128, bias=epsc), reads=[ep_b, misc_b], writes=[ep_b])
                S.act(lambda e: e.activation(out=ss4[:, 4:8], in_=rz[:, 12:16], func=AF.Exp, scale=-0.5), reads=[ep_b], writes=[ep_b])
                for qs in range(4):
                    S.dve(lambda e, qs=qs: e.scalar_tensor_tensor(out=hn[:, qs * 128:(qs + 1) * 128], in0=dd4[:, qs, :], scalar=ss4[:, 4 + qs:5 + qs], in1=swb, op0=ALU.mult, op1=ALU.mult), reads=[ep_b, swb_b], writes=[hn_b])

            def epilogue_b(h, qi):
                for qs in range(4):
                    S.pe(lambda e, qs=qs: e.matmul(pb(3)[:, qs * 128:(qs + 1) * 128], lhsT=hn[:, qs * 128:(qs + 1) * 128], rhs=ident, start=True, stop=True), reads=[hn_b, ident_b], writes=[PB[3]])
                hi = (h * NQ + qi) % 2
                S.dve(lambda e, hi=hi: e.tensor_copy(out=hst[hi], in_=pb(3)), reads=[PB[3]], writes=[hst_b[hi]])
                S.dma("sp", HTs[h, :, qi * 512:(qi + 1) * 512], hst[hi], reads=[hst_b[hi]], writes=[HTs_b[h][qi]])

            steps = [(h, qi, kt) for h in range(8) for qi in range(NQ) for kt in range(NKT)]
            load_head(0)
            pending = None
            for n, (h, qi, kt) in enumerate(steps):
                emit_S(n, h, qi, kt)
                if n >= 1:
                    ph, pqi, pkt = steps[n - 1]
                    emit_PV(n - 1, ph, pqi, pkt)
                    if pkt == NKT - 1:
                        epilogue_a(ph, pqi)
                        pending = (n + 6, ph, pqi)
                if pending is not None and n >= pending[0]:
                    epilogue_b(pending[1], pending[2])
                    pending = None
                if qi == 0 and kt == 0 and h + 1 < 8:
                    load_head(h + 1)
            ph, pqi, pkt = steps[-1]
            emit_PV(len(steps) - 1, ph, pqi, pkt)
            if pending is not None:
                epilogue_b(pending[1], pending[2])
            epilogue_a(ph, pqi)
            epilogue_b(ph, pqi)
            A.release()
            S.barrier()
            if stop == 2:
                raise _Stop()

            A.mark()
            x1 = A.alloc([4, D], F32); x1_b = [Buf() for _ in range(4)]
            sqj = A.alloc([D], BF16); sqj_b = Buf()
            xs = [A.alloc([D], BF16) for _ in range(2)]; xs_b = [Buf() for _ in range(2)]
            st8 = [A.alloc([8], F32) for _ in range(2)]; st8_b = [Buf() for _ in range(2)]
            hT = A.alloc([16, 512], BF16); hT_b = [Buf() for _ in range(4)]
            pw = A.alloc([4, 2, 256], BF16); pw_b = Buf()
            wst = [A.alloc([16, 512], BF16) for _ in range(3)]; wst_b = [Buf() for _ in range(3)]
            A.mark()
            ut = A.alloc([6, 1024], BF16); ut_b = [Buf() for _ in range(6)]
            dT = A.alloc([8, 512], BF16); dT_b = Buf()
            hdT = A.alloc([8, 512], BF16); hdT_b = Buf()
            yT = A.alloc([8, 512], BF16); yT_b = Buf()
            mixT = A.alloc([16, 512], BF16); mixT_b = Buf()
            gsg = [A.alloc([512], F32) for _ in range(4)]; gsg_b = [Buf() for _ in range(4)]
            mix_end = A.off
            A.release()
            A.mark()
            hidT = A.alloc([32, 512], BF16); hidT_b = Buf()
            rl = [A.alloc([512], F32) for _ in range(2)]; rl_b = [Buf() for _ in range(2)]
            ffn_end = A.off
            A.release()
            A.off = max(mix_end, ffn_end)
            A.peak = max(A.peak, A.off)
            region_all = ut_b + [dT_b, hdT_b, yT_b, mixT_b] + gsg_b + [hidT_b] + rl_b
            dummy = A.alloc([1], F32)

            def region_switch():
                S.dve(lambda e: e.memset(dummy, 0.0), writes=region_all)

            for g in range(4):
                S.dma("pool", pw[:, g, :, :], pool_w[g].rearrange("(k p) n -> p k n", p=128), writes=[pw_b], accumulate=True)

            wn = [0]

            def wload(src_ap, kcn, key):
                i = wn[0] % 3; wn[0] += 1
                S.dma("sp", wst[i][:, 0:kcn, :], src_ap.rearrange("(k p) n -> p k n", p=128), reads=[pre[key]], writes=[wst_b[i]])
                return wst[i], wst_b[i]

            evn = [0]

            for s in range(NS):
                t0 = s * 512
                region_switch()
                ffn_guard = []
                S.dma("sp", hdT, HTs[:, :, t0:t0 + 512].rearrange("h p t -> p h t"), reads=[HTs_b[h][s] for h in range(8)] + ffn_guard, writes=[hdT_b])
                for j in range(6):
                    S.dma("sp", ut[:, j, :], Us[(s * 4 + j) * 128:(s * 4 + j + 1) * 128, :], reads=[Us_b[s * 4 + j]] + ffn_guard, writes=[ut_b[j]])
                for j in range(4):
                    S.dma("sp", x1[:, j, :], xo[t0 + j * 128:t0 + (j + 1) * 128, :], writes=[x1_b[j]])
                for j in range(4):
                    prep(None, A1, B1, hT[:, :, j * 128:(j + 1) * 128], hT_b[j], xt=x1[:, j, :], xt_b=x1_b[j])
                for c in range(8):
                    g = c // 2
                    bank_i = c % 4
                    for j in range(4):
                        gt = s * 4 + j
                        cls = 0 if gt == 0 else (2 if gt == NT - 1 else 1)
                        for n in range(3):
                            bidx = (cls * 4 + g) * 3 + n
                            S.pe(lambda e, c=c, j=j, n=n, bidx=bidx, bank_i=bank_i: e.matmul(pa(bank_i)[:, j * 128:(j + 1) * 128], lhsT=ut[:, j + n, c * 128:(c + 1) * 128], rhs=bands[:, bidx, :], start=(n == 0), stop=(n == 2)),
                                 reads=[ut_b[j + n], bands_b], writes=[PA[bank_i]])
                    S.act(lambda e, c=c, bank_i=bank_i: e.activation(out=dT[:, c, :], in_=pa(bank_i), func=AF.Copy), reads=[PA[bank_i]] + ffn_guard, writes=[dT_b])
                for oc8 in range(8):
                    g = oc8 // 2; oc = oc8 % 2
                    bank_i = oc8 % 4
                    for ic in range(2):
                        S.pe(lambda e, g=g, oc=oc, ic=ic, bank_i=bank_i: e.matmul(pa(bank_i), lhsT=pw[:, g, ic, oc * 128:(oc + 1) * 128], rhs=dT[:, g * 2 + ic, :], start=(ic == 0), stop=(ic == 1)),
                             reads=[pw_b, dT_b], writes=[PA[bank_i]])
                    S.act(lambda e, oc8=oc8, bank_i=bank_i: e.activation(out=yT[:, oc8, :], in_=pa(bank_i), func=AF.Copy, scale=pscale[:, oc8:oc8 + 1]), reads=[PA[bank_i], pscale_b] + ffn_guard, writes=[yT_b])
                for mg in range(4):
                    iab = wn[0] % 3; wn[0] += 1
                    wa, wa_b = wst[iab], wst_b[iab]
                    S.dma("sp", wa[:, 0:8, :], Wa_s[:, mg * 512:(mg + 1) * 512].rearrange("(k p) n -> p k n", p=128), reads=[pre[("a", mg)]], writes=[wa_b])
                    S.dma("sp", wa[:, 8:16, :], Wb_s[:, mg * 512:(mg + 1) * 512].rearrange("(k p) n -> p k n", p=128), reads=[pre[("b", mg)]], writes=[wa_b], accumulate=True)
                    wbu, wbu_b = wa, wa_b
                    wga, wga_b = wload(Wg_s[:, mg * 512:(mg + 1) * 512], 16, ("g", mg))
                    wgb, wgb_b = wload(Wg_s[:, (4 + mg) * 512:(5 + mg) * 512], 16, ("g", 4 + mg))
                    for mi in range(4):
                        m = mg * 4 + mi
                        cs = slice(mi * 128, (mi + 1) * 128)
                        for kc in range(8):
                            S.pe(lambda e, kc=kc, cs=cs, wa=wa: e.matmul(pa(0), lhsT=wa[:, kc, cs], rhs=hdT[:, kc, :], start=(kc == 0), stop=(kc == 7)), reads=[wa_b, hdT_b], writes=[PA[0]])
                        for kc in range(8):
                            S.pe(lambda e, kc=kc, cs=cs, wbu=wbu: e.matmul(pa(1), lhsT=wbu[:, 8 + kc, cs], rhs=yT[:, kc, :], start=(kc == 0), stop=(kc == 7)), reads=[wbu_b, yT_b], writes=[PA[1]])
                        for kc in range(KD):
                            S.pe(lambda e, kc=kc, cs=cs, wga=wga: e.matmul(pa(2), lhsT=wga[:, kc, cs], rhs=hT[:, kc, :], start=(kc == 0), stop=(kc == KD - 1)), reads=[wga_b] + hT_b, writes=[PA[2]])
                        for kc in range(KD):
                            S.pe(lambda e, kc=kc, cs=cs, wgb=wgb: e.matmul(pa(3), lhsT=wgb[:, kc, cs], rhs=hT[:, kc, :], start=(kc == 0), stop=(kc == KD - 1)), reads=[wgb_b] + hT_b, writes=[PA[3]])
                        gi = (evn[0] % 2) * 2; evn[0] += 1
                        S.act(lambda e, gi=gi: e.activation(out=gsg[gi], in_=pa(2), func=AF.Sigmoid), reads=[PA[2]] + ffn_guard, writes=[gsg_b[gi]])
                        S.act(lambda e, gi=gi: e.activation(out=gsg[gi + 1], in_=pa(3), func=AF.Sigmoid), reads=[PA[3]] + ffn_guard, writes=[gsg_b[gi + 1]])
                        S.dve(lambda e, gi=gi: e.tensor_tensor(out=gsg[gi], in0=gsg[gi], in1=pa(0), op=ALU.mult), reads=[PA[0], gsg_b[gi]], writes=[gsg_b[gi]])
                        S.dve(lambda e, gi=gi: e.tensor_tensor(out=gsg[gi + 1], in0=gsg[gi + 1], in1=pa(1), op=ALU.mult), reads=[PA[1], gsg_b[gi + 1]], writes=[gsg_b[gi + 1]])
                        S.dve(lambda e, gi=gi, m=m: e.tensor_tensor(out=mixT[:, m, :], in0=gsg[gi], in1=gsg[gi + 1], op=ALU.add), reads=[gsg_b[gi], gsg_b[gi + 1]] + ffn_guard, writes=[mixT_b])
                for ng in range(4):
                    wo, wo_b = wload(Wo_s[:, ng * 512:(ng + 1) * 512], 16, ("o", ng))
                    for j in range(4):
                        for kc in range(KD):
                            S.pe(lambda e, kc=kc, j=j, wo=wo: e.matmul(pa(j), lhsT=mixT[:, kc, j * 128:(j + 1) * 128], rhs=wo[:, kc, :], start=(kc == 0), stop=(kc == KD - 1)), reads=[mixT_b, wo_b], writes=[PA[j]])
                    for j in range(4):
                        ri = evn[0] % 2; evn[0] += 1
                        S.dve(lambda e, j=j, ng=ng, ri=ri: e.tensor_tensor(out=gsg[ri], in0=pa(j), in1=GArow[:, ng * 512:(ng + 1) * 512], op=ALU.mult), reads=[PA[j], grow_b], writes=[gsg_b[ri]])
                        S.dve(lambda e, j=j, ng=ng, ri=ri: e.tensor_tensor(out=x1[:, j, ng * 512:(ng + 1) * 512], in0=x1[:, j, ng * 512:(ng + 1) * 512], in1=gsg[ri], op=ALU.add), reads=[gsg_b[ri], x1_b[j]], writes=[x1_b[j]])
                for j in range(4):
                    prep(None, A2, B2, hT[:, :, j * 128:(j + 1) * 128], hT_b[j], xt=x1[:, j, :], xt_b=x1_b[j])
                region_switch()
                mix_guard = []
                for half in range(2):
                    for fg in range(8):
                        col0 = half * 4096 + fg * 512
                        w1, w1_b = wload(W1_s[:, col0:col0 + 512], 16, ("1", half * 8 + fg))
                        for mi in range(4):
                            jh = fg * 4 + mi
                            bank_i = mi
                            for kc in range(KD):
                                S.pe(lambda e, kc=kc, mi=mi, w1=w1, bank_i=bank_i: e.matmul(pa(bank_i), lhsT=w1[:, kc, mi * 128:(mi + 1) * 128], rhs=hT[:, kc, :], start=(kc == 0), stop=(kc == KD - 1)), reads=[w1_b] + hT_b, writes=[PA[bank_i]])
                            ri = evn[0] % 2; evn[0] += 1
                            S.act(lambda e, ri=ri, bank_i=bank_i: e.activation(out=rl[ri], in_=pa(bank_i), func=AF.Relu), reads=[PA[bank_i]] + mix_guard, writes=[rl_b[ri]])
                            S.dve(lambda e, ri=ri, jh=jh: e.tensor_tensor(out=hidT[:, jh, :], in0=rl[ri], in1=rl[ri], op=ALU.mult), reads=[rl_b[ri]] + mix_guard, writes=[hidT_b])
                    for ng in range(4):
                        for kg in range(2):
                            r0 = half * 4096 + kg * 2048
                            w2, w2_b = wload(W2_s[r0:r0 + 2048, ng * 512:(ng + 1) * 512], 16, ("2", half * 2 + kg, ng))
                            for j in range(4):
                                for kc in range(KD):
                                    kk = kg * 16 + kc
                                    S.pe(lambda e, kc=kc, kk=kk, j=j, w2=w2, kg=kg: e.matmul(pa(j), lhsT=hidT[:, kk, j * 128:(j + 1) * 128], rhs=w2[:, kc, :], start=(kk == 0), stop=(kk == 31)), reads=[hidT_b, w2_b], writes=[PA[j]])
                        for j in range(4):
                            ri = evn[0] % 2; evn[0] += 1
                            S.dve(lambda e, j=j, ng=ng, ri=ri: e.tensor_tensor(out=rl[ri], in0=pa(j), in1=GMrow[:, ng * 512:(ng + 1) * 512], op=ALU.mult), reads=[PA[j], grow_b], writes=[rl_b[ri]])
                            S.dve(lambda e, j=j, ng=ng, ri=ri: e.tensor_tensor(out=x1[:, j, ng * 512:(ng + 1) * 512], in0=x1[:, j, ng * 512:(ng + 1) * 512], in1=rl[ri], op=ALU.add), reads=[rl_b[ri], x1_b[j]], writes=[x1_b[j]])
                for j in range(4):
                    S.dma("sp", y[t0 + j * 128:t0 + (j + 1) * 128, :], x1[:, j, :], reads=[x1_b[j]], writes=[Buf()])
            A.release()
        except _Stop:
            pass
        finals = [o for o in S.dma_pool["sp"]["last"] if o is not None]
        S.op("sp", lambda e: e.nop(), deps=finals)
        block = st.enter_context(nc.Block())
        stats = S.emit(block)
        if debug:
            print("ops/waits:", stats, "sems:", S.nsem, "arena peak:", A.peak)
    return nc


def rope_tables(tok_idx, n_ctx):
    half = 32
    inv_freq = (10000.0 ** (-np.arange(0, half, 2, dtype=np.float32) / half)).astype(np.float32)
    row = (tok_idx // GRID_W).astype(np.float32)
    col = (tok_idx % GRID_W).astype(np.float32)
    out = np.zeros((len(tok_idx) + n_ctx, 128), np.float32)
    for hh, pos in enumerate((row, col)):
        ang = (pos[:, None] * inv_freq[None, :]).astype(np.float32)
        c, s_ = np.cos(ang), np.sin(ang)
        for xi in range(2):
            out[:len(tok_idx), hh * 32 + xi * 16: hh * 32 + xi * 16 + 16] = c
            out[:len(tok_idx), 64 + hh * 32 + xi * 16: 64 + hh * 32 + xi * 16 + 16] = (-s_ if xi == 0 else s_)
    out[len(tok_idx):, 0:64] = 1.0
    return out


def band_tables(L, s):
    T = L // 2
    out = np.zeros((128, 3, 4, 3, 128), np.float32)
    tiles = {0: s * (T // 128), 1: s * (T // 128) + 1, 2: (s + 1) * (T // 128) - 1}
    for cls, gt in tiles.items():
        for g, w in enumerate(POOL_WINDOWS):
            for tl in range(128):
                t = gt * 128 + tl
                lo = min(max(t - w // 2, 0), L)
                hi = min(max(t + w - w // 2, 0), L)
                for tp in range(lo, hi):
                    n = tp // 128 - gt + 1
                    out[tp % 128, cls, g, n, tl] += 1.0 / (hi - lo)
                out[tl, cls, g, 1, tl] -= 1.0
    return out.reshape(128, 36 * 128).astype(ml_dtypes.bfloat16)


_NC_CACHE = {}


def run(inputs, L, C, B, debug=False, stop=99):
    key = (L, C)
    if key not in _NC_CACHE:
        _NC_CACHE[key] = build(L, C, debug=debug, stop=stop)
    nc = _NC_CACHE[key]
    T = L // 2
    f = lambda a: np.ascontiguousarray(np.asarray(a, dtype=np.float32))
    x = f(inputs["x"]); c = f(inputs["c"]); ctx = f(inputs["ctx"])
    shared = {"cctx": f(inputs["c_ctx"]), "ident": np.eye(128, dtype=np.float32).astype(ml_dtypes.bfloat16)}
    for k in ("w_mod", "b_mod", "norm_attn_w", "w_in", "q_norm_w", "k_norm_w", "lambda_q1", "lambda_k1", "lambda_q2",
              "lambda_k2", "subln_w", "pool_w", "pool_scale", "w_a_up", "w_b_up", "w_o", "norm_mlp_w", "w_ff1", "w_ff2"):
        shared[k] = f(inputs[k])[0]
    in_maps = []
    zeros128 = np.zeros((128, D), np.float32)
    for b in range(B):
        for s in range(2):
            own = slice(s * T, (s + 1) * T)
            oth = slice((1 - s) * T, (2 - s) * T)
            m = dict(shared)
            m["xo"] = x[b, own]
            m["xr"] = x[b, oth]
            before = x[b, s * T - 128:s * T] if s == 1 else zeros128
            after = x[b, (s + 1) * T:(s + 1) * T + 128] if s == 0 else zeros128
            m["xh"] = np.ascontiguousarray(np.concatenate([before, after], 0))
            m["ctx"] = ctx[b]
            m["cvec"] = c[b]
            tok = np.concatenate([np.arange(s * T, (s + 1) * T), np.arange((1 - s) * T, (2 - s) * T)])
            m["rope"] = rope_tables(tok, C).reshape(-1, 128, 128)
            m["bands"] = band_tables(L, s)
            in_maps.append(m)
    res = run_bass_kernel_spmd(nc, in_maps, core_ids=list(range(2 * B)))
    out = np.zeros((B, L, D), np.float32)
    for b in range(B):
        for s in range(2):
            out[b, s * T:(s + 1) * T] = res.results[b * 2 + s]["y"]
    if debug:
        return out, res.results
    return out


def kernel(**inputs):
    return run(inputs, 8192, 256, 4)
```

```python
import os
import numpy as np
import ml_dtypes
from contextlib import ExitStack
import concourse.bass as bass
import concourse.mybir as mybir
from concourse.bass_utils import run_bass_kernel_spmd

F32 = mybir.dt.float32
BF16 = mybir.dt.bfloat16
U8 = mybir.dt.uint8
AF = mybir.ActivationFunctionType
ALU = mybir.AluOpType
AX = mybir.AxisListType

ENGS = ("pe", "act", "dve", "pool", "sp")
SEM_ROT = 30000
D = 2048
KD = 16
EPS = 1e-6
GRID_W = 64
POOL_WINDOWS = (2, 4, 8, 16)


class Op:
    __slots__ = ("eng", "fn", "deps", "sig", "sem", "val", "is_dma")

    def __init__(self, eng, fn, deps, is_dma=False):
        self.eng = eng
        self.fn = fn
        self.deps = deps
        self.sig = False
        self.sem = None
        self.val = None
        self.is_dma = is_dma


class Buf:
    __slots__ = ("name", "w", "r", "rd", "excl")

    def __init__(self, name="", excl=False):
        self.name = name
        self.excl = excl
        self.w = []
        self.r = {}
        self.rd = []


class Sched:
    def __init__(self, nc, stack, n_dma_sems=8):
        self.nc = nc
        self.stack = stack
        self.ops = {e: [] for e in ENGS}
        self.nsem = 0
        self.dma_pool = {q: {"sems": [], "cnt": [], "last": [], "i": 0, "n": n_dma_sems} for q in ("sp", "pool")}

    def new_sem(self, name):
        self.nsem += 1
        return self.stack.enter_context(self.nc.semaphore(f"{name}{self.nsem}"))

    def _deps(self, eng, is_dma, reads, writes, extra):
        deps = list(extra)
        for b in reads:
            deps += b.w
        for b in writes:
            deps += b.w
            deps += list(b.r.values())
            deps += b.rd
        return [d for d in deps if d is not None]

    def _update(self, o, reads, writes, accumulate=False):
        for b in reads:
            if o.is_dma:
                b.rd.append(o)
            else:
                b.r[o.eng] = o
        for b in writes:
            if accumulate:
                b.w = b.w + [o]
            else:
                b.w = [o]
            b.r = {}
            b.rd = []

    @staticmethod
    def _split(reads, writes):
        ex = [b for b in reads if b.excl]
        if ex:
            reads = [b for b in reads if not b.excl]
            writes = list(writes) + ex
        return reads, writes

    def op(self, eng, fn, reads=(), writes=(), deps=()):
        reads, writes = self._split(reads, writes)
        o = Op(eng, fn, self._deps(eng, False, reads, writes, deps))
        self.ops[eng].append(o)
        self._update(o, reads, writes)
        return o

    def pe(self, fn, reads=(), writes=(), deps=()):
        return self.op("pe", fn, reads, writes, deps)

    def act(self, fn, reads=(), writes=(), deps=()):
        return self.op("act", fn, reads, writes, deps)

    def dve(self, fn, reads=(), writes=(), deps=()):
        return self.op("dve", fn, reads, writes, deps)

    def dma(self, q, out, in_, reads=(), writes=(), deps=(), accumulate=False, **kw):
        P = self.dma_pool[q]
        if len(P["sems"]) < P["n"]:
            P["sems"].append(self.new_sem(f"d{q}"))
            P["cnt"].append(0)
            P["last"].append(None)
            i = len(P["sems"]) - 1
        else:
            i = P["i"] % P["n"]
        P["i"] += 1
        if P["cnt"][i] + 16 > SEM_ROT:
            P["sems"][i] = self.new_sem(f"d{q}")
            P["cnt"][i] = 0
        reads, writes = self._split(reads, writes)
        d = self._deps(q, True, reads, writes, deps)
        if accumulate:
            prevw = set(id(x) for b in writes for x in b.w if x.is_dma)
            d = [x for x in d if id(x) not in prevw]
        if P["last"][i] is not None:
            d.append(P["last"][i])
        o = Op(q, lambda e, out=out, in_=in_, kw=kw: e.dma_start(out=out, in_=in_, **kw), d, is_dma=True)
        P["cnt"][i] += 16
        o.sem = P["sems"][i]
        o.val = P["cnt"][i]
        o.sig = True
        P["last"][i] = o
        self.ops[q].append(o)
        self._update(o, reads, writes, accumulate)
        return o

    def barrier(self):
        toks = []
        for e in ENGS:
            for o in reversed(self.ops[e]):
                if not o.is_dma and o.fn is not None:
                    toks.append(o)
                    break
        for q in self.dma_pool:
            toks += [o for o in self.dma_pool[q]["last"] if o is not None]
        for e in ENGS:
            o = Op(e, None, list(toks))
            self.ops[e].append(o)

    def emit(self, block):
        for e in ENGS:
            for o in self.ops[e]:
                for d in o.deps:
                    if not d.is_dma and not (d.eng == "pe" and o.eng == "pe" and not o.is_dma):
                        d.sig = True
        for e in ENGS:
            sem = None
            cnt = 0
            for o in self.ops[e]:
                if o.is_dma or not o.sig:
                    continue
                if sem is None or cnt >= SEM_ROT:
                    sem = self.new_sem(f"c{e}")
                    cnt = 0
                cnt += 1
                o.sem = sem
                o.val = cnt
        stats = {}

        def run(eng_name, e):
            waited = {}
            nw = 0
            for o in self.ops[eng_name]:
                for d in o.deps:
                    if d.eng == "pe" and eng_name == "pe" and not d.is_dma and not o.is_dma:
                        continue
                    k = id(d.sem)
                    if waited.get(k, 0) >= d.val:
                        continue
                    waited[k] = d.val
                    e.wait_ge(d.sem, d.val)
                    nw += 1
                if o.fn is None:
                    continue
                ins = o.fn(e)
                if o.sig:
                    ins.then_inc(o.sem, 16 if o.is_dma else 1)
            stats[eng_name] = (len(self.ops[eng_name]), nw)

        @block.tensor
        def _(e):
            run("pe", e)

        @block.scalar
        def _(e):
            run("act", e)

        @block.vector
        def _(e):
            run("dve", e)

        @block.gpsimd
        def _(e):
            run("pool", e)

        @block.sync
        def _(e):
            run("sp", e)

        return stats


class Arena:
    def __init__(self, tensor, nbytes):
        self.t = tensor
        self.n = nbytes
        self.off = 0
        self.marks = []
        self.peak = 0

    def mark(self):
        self.marks.append(self.off)

    def release(self):
        self.off = self.marks.pop()

    def alloc(self, free_shape, dt):
        esz = {F32: 4, BF16: 2, U8: 1}[dt]
        n = int(np.prod(free_shape)) * esz
        assert self.off + n <= self.n, f"arena overflow {self.off}+{n}>{self.n}"
        a = self.t[:, self.off:self.off + n]
        if dt != U8:
            a = a.bitcast(dt)
        self.off += (n + 63) // 64 * 64
        self.peak = max(self.peak, self.off)
        if len(free_shape) > 1:
            names = " ".join(f"d{i}" for i in range(len(free_shape)))
            kw = {f"d{i}": int(free_shape[i]) for i in range(len(free_shape))}
            a = a.rearrange(f"p ({names}) -> p {names}", **kw)
        return a


class _Stop(Exception):
    pass


def build(L, C, debug=False, stop=99):
    T = L // 2
    NT = T // 128
    NO = (L - T) // 128
    NCT = C // 128
    NKT = NT + NO + NCT
    LK = NKT * 128
    NS = T // 512
    NQ = T // 512
    nc = bass.Bass("TRN2", target_bir_lowering=False)

    def din(name, shape, dt=F32):
        return nc.dram_tensor(name, list(shape), dt, kind="ExternalInput").ap()

    xo = din("xo", [T, D]); xr = din("xr", [L - T, D]); xh = din("xh", [256, D]); ctx = din("ctx", [C, D])
    cvec = din("cvec", [D]); cctx = din("cctx", [D])
    w_mod = din("w_mod", [D, 6 * D]); b_mod = din("b_mod", [6 * D]); naw_d = din("norm_attn_w", [D])
    w_in = din("w_in", [D, 8192]); qnw = din("q_norm_w", [64]); knw = din("k_norm_w", [64])
    lq1 = din("lambda_q1", [64]); lk1 = din("lambda_k1", [64]); lq2 = din("lambda_q2", [64]); lk2 = din("lambda_k2", [64])
    subln = din("subln_w", [128]); pool_w = din("pool_w", [4, 256, 256]); pool_scale = din("pool_scale", [1024])
    w_a_up = din("w_a_up", [1024, D]); w_b_up = din("w_b_up", [1024, D]); w_o = din("w_o", [D, D])
    nmw_d = din("norm_mlp_w", [D]); w_ff1 = din("w_ff1", [D, 4 * D]); w_ff2 = din("w_ff2", [4 * D, D])
    ident_d = din("ident", [128, 128], BF16); rope_d = din("rope", [NKT, 128, 128]); bands_d = din("bands", [128, 36 * 128], BF16)
    y = nc.dram_tensor("y", [T, D], F32, kind="ExternalOutput").ap()
    sk = dict(kind="ExternalOutput") if debug else {}
    KTs = nc.dram_tensor("KTs", [8, 128, LK], BF16, **sk).ap()
    Vs = nc.dram_tensor("Vs", [LK, 1024], BF16, **sk).ap()
    QTs = nc.dram_tensor("QTs", [8, 128, T], BF16, **sk).ap()
    Us = nc.dram_tensor("Us", [(NT + 2) * 128, 1024], BF16, **sk).ap()
    HTs = nc.dram_tensor("HTs", [8, 128, T], BF16, **sk).ap()
    Wg_s = nc.dram_tensor("Wg_s", [D, 4096], BF16).ap()
    Wa_s = nc.dram_tensor("Wa_s", [1024, D], BF16).ap()
    Wb_s = nc.dram_tensor("Wb_s", [1024, D], BF16).ap()
    Wo_s = nc.dram_tensor("Wo_s", [D, D], BF16).ap()
    W1_s = nc.dram_tensor("W1_s", [D, 4 * D], BF16).ap()
    W2_s = nc.dram_tensor("W2_s", [4 * D, D], BF16).ap()
    pre = {}
    KTs_b = [Buf() for _ in range(NKT)]; Vs_b = [Buf() for _ in range(NKT)]
    QTs_b = [Buf() for _ in range(NT)]; Us_b = [Buf() for _ in range(NT + 2)]
    HTs_b = [[Buf() for _ in range(NQ)] for _ in range(8)]
    dbg = {}

    with ExitStack() as st:
        ARENA_BYTES = 200 * 1024
        arena_t = st.enter_context(nc.sbuf_tensor("arena", [128, ARENA_BYTES], U8))
        psA = st.enter_context(nc.psum_tensor("psA", [128, 2048], F32))
        psB = st.enter_context(nc.psum_tensor("psB", [128, 2048], F32))
        S = Sched(nc, st)
        A = Arena(arena_t, ARENA_BYTES)
        PA = [Buf(f"PA{i}", excl=True) for i in range(4)]
        PB = [Buf(f"PB{i}", excl=True) for i in range(4)]

        def pa(i):
            return psA[:, i * 512:(i + 1) * 512]

        def pb(i):
            return psB[:, i * 512:(i + 1) * 512]


        ident = A.alloc([128], BF16); ident_b = Buf()
        bands = A.alloc([36, 128], BF16); bands_b = Buf()
        A1 = A.alloc([16], F32); B1 = A.alloc([16], F32); A1c = A.alloc([16], F32); B1c = A.alloc([16], F32)
        A2 = A.alloc([16], F32); B2 = A.alloc([16], F32)
        modv_b = Buf()
        GArow = A.alloc([D], F32); GMrow = A.alloc([D], F32); grow_b = Buf()
        wq_bc = A.alloc([64], F32); wk_bc = A.alloc([64], F32); wqk_b = Buf()
        swb = A.alloc([128], F32); swb_b = Buf()
        pscale = A.alloc([8], F32); pscale_b = Buf()
        neglam = A.alloc([1], F32); nshift = A.alloc([1], F32); epsc = A.alloc([1], F32); misc_b = Buf()

        S.dma("sp", ident, ident_d, writes=[ident_b])
        S.dma("sp", bands.rearrange("p a b -> p (a b)"), bands_d, writes=[bands_b])
        S.dve(lambda e: e.memset(epsc, EPS), writes=[misc_b])

        try:
            A.mark()
            naw = A.alloc([16], F32); nmw = A.alloc([16], F32); bcol = A.alloc([6, 16], F32)
            c0 = A.alloc([16], F32); c1 = A.alloc([16], F32)
            scb = A.alloc([16, 2], BF16); rep = A.alloc([16, 128], BF16)
            lam4 = A.alloc([4, 64], F32); lamp = A.alloc([2, 64], F32); lams = A.alloc([2], F32); lame = A.alloc([2], F32)
            sbc = A.alloc([128], F32); wsq = A.alloc([2, 64], F32); wmx = A.alloc([2], F32)
            cols = A.alloc([4, 16, 2], F32)
            brow = A.alloc([2, D], F32)
            tmp16 = A.alloc([16], F32)
            p0_b = Buf()
            NCD = dict(allow_slow_non_contiguous=True)
            S.dma("sp", naw, naw_d.rearrange("(k p) -> p k", p=128), writes=[p0_b], accumulate=True, **NCD)
            S.dma("sp", nmw, nmw_d.rearrange("(k p) -> p k", p=128), writes=[p0_b], accumulate=True, **NCD)
            for j in range(6):
                S.dma("sp", bcol[:, j, :], b_mod[j * D:(j + 1) * D].rearrange("(k p) -> p k", p=128), writes=[p0_b], accumulate=True, **NCD)
            S.dma("sp", c0, cvec.rearrange("(k p) -> p k", p=128), writes=[p0_b], accumulate=True, **NCD)
            S.dma("sp", c1, cctx.rearrange("(k p) -> p k", p=128), writes=[p0_b], accumulate=True, **NCD)
            S.dma("sp", pscale, pool_scale.rearrange("(k p) -> p k", p=128), writes=[pscale_b], **NCD)
            S.dma("sp", wq_bc, qnw.partition_broadcast(128), writes=[wqk_b], accumulate=True)
            S.dma("sp", wk_bc, knw.partition_broadcast(128), writes=[wqk_b], accumulate=True)
            for i, lv in enumerate((lq1, lk1, lq2, lk2)):
                S.dma("sp", lam4[:, i, :], lv.partition_broadcast(128), writes=[p0_b], accumulate=True)
            S.dma("sp", sbc, subln.partition_broadcast(128), writes=[p0_b], accumulate=True)
            S.dma("sp", brow[:, 0, :], b_mod[2 * D:3 * D].partition_broadcast(128), writes=[p0_b], accumulate=True)
            S.dma("sp", brow[:, 1, :], b_mod[5 * D:6 * D].partition_broadcast(128), writes=[p0_b], accumulate=True)
            sc_b = Buf()
            S.act(lambda e: e.activation(out=scb[:, :, 0], in_=c0, func=AF.Silu), reads=[p0_b], writes=[sc_b])
            S.act(lambda e: e.activation(out=scb[:, :, 1], in_=c1, func=AF.Silu), reads=[p0_b, sc_b], writes=[sc_b])
            rep_b = Buf()
            S.dve(lambda e: e.tensor_copy(out=rep, in_=scb[:, :, 0:1].to_broadcast([128, 16, 128])), reads=[sc_b], writes=[rep_b])
            S.dve(lambda e: e.tensor_tensor(out=lamp[:, 0, :], in0=lam4[:, 0, :], in1=lam4[:, 1, :], op=ALU.mult), reads=[p0_b], writes=[misc_b])
            S.dve(lambda e: e.tensor_tensor(out=lamp[:, 1, :], in0=lam4[:, 2, :], in1=lam4[:, 3, :], op=ALU.mult), reads=[misc_b], writes=[misc_b])
            S.dve(lambda e: e.tensor_reduce(out=lams, in_=lamp, axis=AX.X, op=ALU.add), reads=[misc_b], writes=[misc_b])
            S.act(lambda e: e.activation(out=lame, in_=lams, func=AF.Exp), reads=[misc_b], writes=[misc_b])
            S.dve(lambda e: e.tensor_tensor(out=neglam, in0=lame[:, 1:2], in1=lame[:, 0:1], op=ALU.subtract), reads=[misc_b], writes=[misc_b])
            S.dve(lambda e: e.tensor_scalar(out=neglam, in0=neglam, scalar1=-0.2, scalar2=None, op0=ALU.add), reads=[misc_b], writes=[misc_b])
            S.dve(lambda e: e.tensor_tensor(out=wsq[:, 0, :], in0=wq_bc, in1=wq_bc, op=ALU.mult), reads=[wqk_b, misc_b], writes=[misc_b])
            S.dve(lambda e: e.tensor_tensor(out=wsq[:, 1, :], in0=wk_bc, in1=wk_bc, op=ALU.mult), reads=[wqk_b, misc_b], writes=[misc_b])
            S.dve(lambda e: e.tensor_reduce(out=wmx[:, 0:1], in_=wsq.rearrange("p a b -> p (a b)"), axis=AX.X, op=ALU.max), reads=[misc_b], writes=[misc_b])
            S.dve(lambda e: e.tensor_scalar(out=nshift, in0=wmx[:, 0:1], scalar1=-8.0, scalar2=None, op0=ALU.mult), reads=[misc_b], writes=[misc_b])
            S.dve(lambda e: e.tensor_scalar(out=swb, in0=sbc, scalar1=0.8, scalar2=None, op0=ALU.mult), reads=[p0_b], writes=[swb_b])

            NW0 = 6
            wst = [A.alloc([16, 512], BF16) for _ in range(NW0)]
            wst_b = [Buf() for _ in range(NW0)]
            colmap = {0: 0, 1: 1, 3: 2, 4: 3}
            n = 0
            for j in range(6):
                for tq in range(4):
                    wb = wst[n % NW0]; wbb = wst_b[n % NW0]; n += 1
                    c0_ = j * D + tq * 512
                    S.dma("pool", wb, w_mod[:, c0_:c0_ + 512].rearrange("(k p) n -> p k n", p=128), writes=[wbb])
                    if j in colmap:
                        jj = colmap[j]
                        for m in range(4):
                            mm_ = tq * 4 + m
                            o_ap = psB[:, (jj * 16 + mm_) * 2:(jj * 16 + mm_) * 2 + 2]
                            for kc in range(KD):
                                S.pe(lambda e, o_ap=o_ap, wb=wb, m=m, kc=kc: e.matmul(o_ap, lhsT=wb[:, kc, m * 128:(m + 1) * 128], rhs=scb[:, kc, :], start=(kc == 0), stop=(kc == KD - 1)),
                                     reads=[wbb, sc_b], writes=[PB[0]])
                    else:
                        bank = PA[tq]
                        for kc in range(KD):
                            S.pe(lambda e, tq=tq, wb=wb, kc=kc: e.matmul(pa(tq), lhsT=rep[:, kc, :], rhs=wb[:, kc, :], start=(kc == 0), stop=(kc == KD - 1)),
                                 reads=[wbb, rep_b], writes=[bank])
                        row = GArow if j == 2 else GMrow
                        bi = 0 if j == 2 else 1
                        S.dve(lambda e, row=row, tq=tq, bi=bi: e.tensor_tensor(out=row[:, tq * 512:(tq + 1) * 512], in0=pa(tq), in1=brow[:, bi, tq * 512:(tq + 1) * 512], op=ALU.add),
                              reads=[bank, p0_b], writes=[grow_b])
            S.dve(lambda e: e.tensor_copy(out=cols.rearrange("p a b c -> p (a b c)"), in_=psB[:, 0:128]), reads=[PB[0]], writes=[modv_b])
            def mk_scale(dst, jj, which, bj, nw_):
                S.dve(lambda e: e.tensor_tensor(out=tmp16, in0=cols[:, jj, :, which], in1=bcol[:, bj, :], op=ALU.add), reads=[modv_b, p0_b], writes=[modv_b])
                S.dve(lambda e: e.scalar_tensor_tensor(out=dst, in0=tmp16, scalar=1.0, in1=nw_, op0=ALU.add, op1=ALU.mult), reads=[modv_b], writes=[modv_b])

            def mk_shift(dst, jj, which, bj):
                S.dve(lambda e: e.tensor_tensor(out=dst, in0=cols[:, jj, :, which], in1=bcol[:, bj, :], op=ALU.add), reads=[modv_b, p0_b], writes=[modv_b])

            mk_scale(A1, 1, 0, 1, naw); mk_shift(B1, 0, 0, 0)
            mk_scale(A1c, 1, 1, 1, naw); mk_shift(B1c, 0, 1, 0)
            mk_scale(A2, 3, 0, 4, nmw); mk_shift(B2, 2, 0, 3)
            A.release()
            S.barrier()
            if stop == 0:
                raise _Stop()

            A.mark()
            NPA = 3
            xbuf = [A.alloc([D], F32) for _ in range(NPA)]; xbuf_b = [Buf() for _ in range(NPA)]
            sqj = A.alloc([D], BF16); sqj_b = Buf()
            xs = [A.alloc([D], BF16) for _ in range(NPA)]; xs_b = [Buf() for _ in range(NPA)]
            st8 = [A.alloc([8], F32) for _ in range(NPA)]; st8_b = [Buf() for _ in range(NPA)]
            prep_n = [0]

            def prep(src_ap, Av, Bv, hT, hT_b, xt=None, xt_b=None):
                i = prep_n[0] % len(xs); prep_n[0] += 1
                if xt is None:
                    xt, xt_b = xbuf[i], xbuf_b[i]
                    S.dma("sp", xt, src_ap, writes=[xt_b])
                s8, s8b = st8[i], st8_b[i]
                xsi, xsi_b, sqj_, sqj_b_ = xs[i], xs_b[i], sqj, sqj_b
                PL = int(os.environ.get("PL", "9"))
                S.dve(lambda e: e.memset(s8[:, 0:1], 0.0), writes=[s8b])
                if PL < 1: return
                S.act(lambda e: e.activation(out=sqj_, in_=xt, func=AF.Square, accum_out=s8[:, 0:1]), reads=[xt_b], writes=[sqj_b_, s8b])
                if PL < 2: return
                S.act(lambda e: e.activation(out=s8[:, 1:2], in_=s8[:, 0:1], func=AF.Ln, scale=1.0 / D, bias=epsc), reads=[s8b, misc_b], writes=[s8b])
                S.act(lambda e: e.activation(out=s8[:, 2:3], in_=s8[:, 1:2], func=AF.Exp, scale=-0.5), reads=[s8b], writes=[s8b])
                S.act(lambda e: e.activation(out=xsi, in_=xt, func=AF.Copy, scale=s8[:, 2:3]), reads=[xt_b, s8b], writes=[xsi_b])
                if PL < 3: return
                for r in range(4):
                    for c in range(4):
                        kc = r * 4 + c
                        S.pe(lambda e, r=r, c=c, kc=kc: e.matmul(pb(r)[:, c * 128:(c + 1) * 128], lhsT=xsi[:, kc * 128:(kc + 1) * 128], rhs=ident, start=True, stop=True),
                             reads=[xsi_b, ident_b], writes=[PB[r]])
                    if PL < 4: continue
                    for c in range(4):
                        kc = r * 4 + c
                        S.act(lambda e, r=r, c=c, kc=kc: e.activation(out=hT[:, kc, :], in_=pb(r)[:, c * 128:(c + 1) * 128], func=AF.Identity, scale=Av[:, kc:kc + 1], bias=Bv[:, kc:kc + 1]),
                              reads=[PB[r], modv_b], writes=[hT_b])

            sq = [A.alloc([512], F32) for _ in range(2)]; qn = [A.alloc([512], F32) for _ in range(2)]
            ta = [A.alloc([512], F32) for _ in range(2)]; tb = [A.alloc([512], F32) for _ in range(2)]
            rs16 = [A.alloc([16], F32) for _ in range(2)]
            pp_b = [Buf() for _ in range(2)]
            pp_n = [0]

            def qk_post(bank, bank_b, dst, dst_b, wbc, rope_t, rope_b):
                i = pp_n[0] % 2; pp_n[0] += 1
                b = pp_b[i]
                v3 = lambda a: a.rearrange("p (s d) -> p s d", s=8)
                v5 = lambda a: a.rearrange("p (s h x i) -> p s h x i", s=8, h=2, x=2, i=16)
                cos4 = rope_t[:, 0:64]; sin4 = rope_t[:, 64:128]
                sin5 = sin4.rearrange("p (h x i) -> p h x i", h=2, x=2, i=16)
                S.act(lambda e: e.activation(out=sq[i], in_=bank, func=AF.Square), reads=[bank_b], writes=[b])
                S.dve(lambda e: e.tensor_reduce(out=rs16[i][:, 0:8], in_=v3(sq[i]), axis=AX.X, op=ALU.add), reads=[b], writes=[b])
                S.act(lambda e: e.activation(out=rs16[i][:, 8:16], in_=rs16[i][:, 0:8], func=AF.Ln, scale=1.0 / 64, bias=epsc), reads=[b, misc_b], writes=[b])
                S.act(lambda e: e.activation(out=rs16[i][:, 0:8], in_=rs16[i][:, 8:16], func=AF.Exp, scale=-0.5), reads=[b], writes=[b])
                S.dve(lambda e: e.tensor_tensor(out=v3(qn[i]), in0=v3(bank), in1=rs16[i][:, 0:8].unsqueeze(2).to_broadcast([128, 8, 64]), op=ALU.mult), reads=[bank_b, b], writes=[b])
                S.dve(lambda e: e.tensor_tensor(out=v3(qn[i]), in0=v3(qn[i]), in1=wbc.unsqueeze(1).to_broadcast([128, 8, 64]), op=ALU.mult), reads=[b, wqk_b], writes=[b])
                S.dve(lambda e: e.tensor_tensor(out=v3(ta[i]), in0=v3(qn[i]), in1=cos4.unsqueeze(1).to_broadcast([128, 8, 64]), op=ALU.mult), reads=[b, rope_b], writes=[b])
                S.dve(lambda e: e.tensor_tensor(out=v5(tb[i])[:, :, :, 0, :], in0=v5(qn[i])[:, :, :, 1, :], in1=sin5[:, :, 0, :].unsqueeze(1).to_broadcast([128, 8, 2, 16]), op=ALU.mult), reads=[b, rope_b], writes=[b])
                S.dve(lambda e: e.tensor_tensor(out=v5(tb[i])[:, :, :, 1, :], in0=v5(qn[i])[:, :, :, 0, :], in1=sin5[:, :, 1, :].unsqueeze(1).to_broadcast([128, 8, 2, 16]), op=ALU.mult), reads=[b, rope_b], writes=[b])
                S.dve(lambda e: e.tensor_tensor(out=dst, in0=ta[i], in1=tb[i], op=ALU.add), reads=[b], writes=[dst_b])

            wres = A.alloc([16, 2048], BF16); wres_b = [Buf() for _ in range(4)]
            hTa = [A.alloc([16, 128], BF16) for _ in range(NPA)]; hTa_b = [Buf() for _ in range(NPA)]
            ropeb = [A.alloc([128], F32) for _ in range(NPA)]; ropeb_b = [Buf() for _ in range(NPA)]
            qkb = [A.alloc([1024], BF16) for _ in range(2)]; qkb_b = [Buf() for _ in range(2)]
            ktst = [A.alloc([8, 128], BF16) for _ in range(2)]; ktst_b = [Buf() for _ in range(2)]
            vst = [A.alloc([1024], BF16) for _ in range(2)]; vst_b = [Buf() for _ in range(2)]

            def load_wres(col0):
                for g in range(4):
                    S.dma("pool", wres[:, :, g * 512:(g + 1) * 512], w_in[:, col0 + g * 512:col0 + (g + 1) * 512].rearrange("(k p) n -> p k n", p=128), writes=[wres_b[g]])

            def project(hT, hT_b, g, bank_i):
                for kc in range(KD):
                    S.pe(lambda e, g=g, kc=kc, bank_i=bank_i: e.matmul(pa(bank_i), lhsT=hT[:, kc, :], rhs=wres[:, kc, g * 512:(g + 1) * 512], start=(kc == 0), stop=(kc == KD - 1)),
                         reads=[hT_b, wres_b[g]], writes=[PA[bank_i]])

            def qk_transpose_store(i, dstT, dstT_b, col0, q):
                for r in range(2):
                    for c in range(4):
                        hh = r * 4 + c
                        S.pe(lambda e, r=r, c=c, hh=hh: e.matmul(pb(2 + r)[:, c * 128:(c + 1) * 128], lhsT=qkb[i][:, hh * 128:(hh + 1) * 128], rhs=ident, start=True, stop=True),
                             reads=[qkb_b[i], ident_b], writes=[PB[2 + r]])
                    S.dve(lambda e, r=r: e.tensor_copy(out=ktst[i][:, r * 4:(r + 1) * 4, :].rearrange("p a b -> p (a b)"), in_=pb(2 + r)), reads=[PB[2 + r]], writes=[ktst_b[i]])
                S.dma("sp", dstT[:, :, col0:col0 + 128].rearrange("h p t -> p h t"), ktst[i], reads=[ktst_b[i]], writes=[dstT_b])

            load_wres(1024)

            def precast(key, dst, src):
                pre[key] = Buf()
                S.dma("pool", dst, src, writes=[pre[key]])

            for mg in range(4):
                precast(("a", mg), Wa_s[:, mg * 512:(mg + 1) * 512], w_a_up[:, mg * 512:(mg + 1) * 512])
                precast(("b", mg), Wb_s[:, mg * 512:(mg + 1) * 512], w_b_up[:, mg * 512:(mg + 1) * 512])
                precast(("g", mg), Wg_s[:, mg * 512:(mg + 1) * 512], w_in[:, 4096 + mg * 512:4096 + (mg + 1) * 512])
                precast(("g", 4 + mg), Wg_s[:, (4 + mg) * 512:(5 + mg) * 512], w_in[:, 6144 + mg * 512:6144 + (mg + 1) * 512])
            for ng in range(4):
                precast(("o", ng), Wo_s[:, ng * 512:(ng + 1) * 512], w_o[:, ng * 512:(ng + 1) * 512])
            for half in range(2):
                for fg in range(8):
                    cb = half * 8 + fg
                    precast(("1", cb), W1_s[:, cb * 512:(cb + 1) * 512], w_ff1[:, cb * 512:(cb + 1) * 512])
                for ng in range(4):
                    for kg in range(2):
                        rg = half * 2 + kg
                        precast(("2", rg, ng), W2_s[rg * 2048:(rg + 1) * 2048, ng * 512:(ng + 1) * 512], w_ff2[rg * 2048:(rg + 1) * 2048, ng * 512:(ng + 1) * 512])
            def a1_prep(kt):
                i = kt % NPA
                if kt < NT:
                    src = xo[kt * 128:(kt + 1) * 128, :]; Av, Bv = A1, B1
                elif kt < NT + NO:
                    src = xr[(kt - NT) * 128:(kt - NT + 1) * 128, :]; Av, Bv = A1, B1
                else:
                    src = ctx[(kt - NT - NO) * 128:(kt - NT - NO + 1) * 128, :]; Av, Bv = A1c, B1c
                S.dma("sp", ropeb[i], rope_d[kt], writes=[ropeb_b[i]])
                prep(src, Av, Bv, hTa[i], hTa_b[i])

            a1_prep(0)
            a1_prep(1)
            for kt in range(NKT):
                i = kt % NPA; i2 = kt % 2
                if kt + 2 < NKT:
                    a1_prep(kt + 2)
                for g in range(4):
                    project(hTa[i], hTa_b[i], g, g)
                for g in range(2):
                    qk_post(pa(g), PA[g], qkb[i2][:, g * 512:(g + 1) * 512], qkb_b[i2], wk_bc, ropeb[i], ropeb_b[i])
                qk_transpose_store(i2, KTs, KTs_b[kt], kt * 128, False)
                for g in range(2):
                    S.act(lambda e, g=g, i2=i2: e.activation(out=vst[i2][:, g * 512:(g + 1) * 512], in_=pa(2 + g), func=AF.Copy), reads=[PA[2 + g]], writes=[vst_b[i2]])
                S.dma("sp", Vs[kt * 128:(kt + 1) * 128, :], vst[i2], reads=[vst_b[i2]], writes=[Vs_b[kt]])

            for g in range(2):
                S.dma("pool", wres[:, :, g * 512:(g + 1) * 512], w_in[:, g * 512:(g + 1) * 512].rearrange("(k p) n -> p k n", p=128), writes=[wres_b[g]])
            for g in range(2):
                S.dma("pool", wres[:, :, (2 + g) * 512:(3 + g) * 512], w_in[:, 3072 + g * 512:3072 + (g + 1) * 512].rearrange("(k p) n -> p k n", p=128), writes=[wres_b[2 + g]])
            def a2_prep(tt):
                i = tt % NPA
                if tt < NT:
                    src = xo[tt * 128:(tt + 1) * 128, :]
                    S.dma("sp", ropeb[i], rope_d[tt], writes=[ropeb_b[i]])
                elif tt == NT:
                    src = xh[0:128, :]
                else:
                    src = xh[128:256, :]
                prep(src, A1, B1, hTa[i], hTa_b[i])

            a2_prep(0)
            a2_prep(1)
            for tt in range(NT + 2):
                i = tt % NPA; i2 = tt % 2
                urow = (1 + tt) * 128 if tt < NT else (0 if tt == NT else (NT + 1) * 128)
                if tt + 2 < NT + 2:
                    a2_prep(tt + 2)
                if tt < NT:
                    for g in range(2):
                        project(hTa[i], hTa_b[i], g, g)
                for g in range(2):
                    project(hTa[i], hTa_b[i], 2 + g, 2 + g)
                if tt < NT:
                    for g in range(2):
                        qk_post(pa(g), PA[g], qkb[i2][:, g * 512:(g + 1) * 512], qkb_b[i2], wq_bc, ropeb[i], ropeb_b[i])
                    qk_transpose_store(i2, QTs, QTs_b[tt], tt * 128, True)
                for g in range(2):
                    S.act(lambda e, g=g, i2=i2: e.activation(out=vst[i2][:, g * 512:(g + 1) * 512], in_=pa(2 + g), func=AF.Copy), reads=[PA[2 + g]], writes=[vst_b[i2]])
                S.dma("sp", Us[urow:urow + 128, :], vst[i2], reads=[vst_b[i2]], writes=[Us_b[urow // 128]])
            A.release()
            S.barrier()
            if stop == 1:
                raise _Stop()

            A.mark()
            QT = [A.alloc([T], BF16) for _ in range(2)]; QT_b = [Buf() for _ in range(2)]
            KT = [A.alloc([LK], BF16) for _ in range(2)]; KT_b = [Buf() for _ in range(2)]
            V1 = [A.alloc([NKT, 130], BF16) for _ in range(2)]; V1_b = [Buf() for _ in range(2)]
            Pb = [A.alloc([1024], BF16) for _ in range(3)]; Pb_b = [Buf() for _ in range(3)]
            Osb = A.alloc([1536], F32); Osb_b = Buf()
            rz = A.alloc([16], F32); dd = A.alloc([128], F32); dj = A.alloc([128], F32); ss4 = A.alloc([8], F32)
            ep_b = Buf()
            hn = A.alloc([512], BF16); hn_b = Buf()
            hst = [A.alloc([512], BF16) for _ in range(2)]; hst_b = [Buf() for _ in range(2)]
            for i in range(2):
                S.dve(lambda e, i=i: e.memset(V1[i][:, :, 128:130], 1.0), writes=[V1_b[i]])

            def oreg(r):
                bank = r // 3
                off = bank * 512 + (r % 3) * 160
                return bank, off

            pcount = 0
            scount = 0
            def load_head(h):
                hb = h % 2
                S.dma("sp", QT[hb], QTs[h], reads=QTs_b, writes=[QT_b[hb]])
                S.dma("sp", KT[hb], KTs[h], reads=KTs_b, writes=[KT_b[hb]])
                S.dma("sp", V1[hb][:, :, 0:128], Vs[:, h * 128:(h + 1) * 128].rearrange("(k p) v -> p k v", p=128), reads=Vs_b, writes=[V1_b[hb]], accumulate=True)

            def emit_S(n, h, qi, kt):
                hb = h % 2; sb = n % 2; pi = n % 3
                S.pe(lambda e: e.matmul(psA[:, sb * 1024:sb * 1024 + 512], lhsT=KT[hb][0:64, kt * 128:(kt + 1) * 128], rhs=QT[hb][0:64, qi * 512:(qi + 1) * 512], start=True, stop=True),
                     reads=[KT_b[hb], QT_b[hb]], writes=[PA[2 * sb]])
                S.pe(lambda e: e.matmul(psA[:, sb * 1024 + 512:sb * 1024 + 1024], lhsT=KT[hb][64:128, kt * 128:(kt + 1) * 128], rhs=QT[hb][64:128, qi * 512:(qi + 1) * 512], start=True, stop=True),
                     reads=[KT_b[hb], QT_b[hb]], writes=[PA[2 * sb + 1]])
                S.act(lambda e: e.activation(out=Pb[pi], in_=psA[:, sb * 1024:(sb + 1) * 1024], func=AF.Exp, scale=0.125, bias=nshift),
                      reads=[PA[2 * sb], PA[2 * sb + 1], misc_b], writes=[Pb_b[pi]])

            def emit_PV(n, h, qi, kt):
                hb = h % 2; pi = n % 3
                for sub in range(2):
                    for qs in range(4):
                        bank, off = oreg(sub * 4 + qs)
                        S.pe(lambda e, sub=sub, qs=qs, off=off: e.matmul(psB[:, off:off + 129], lhsT=Pb[pi][:, sub * 512 + qs * 128:sub * 512 + (qs + 1) * 128], rhs=V1[hb][:, kt, 0:129], start=(kt == 0 and (sub * 4 + qs) % 3 == 0), stop=(kt == NKT - 1), skip_group_check=True),
                             reads=[Pb_b[pi], V1_b[hb]], writes=[PB[bank]])

            dd4 = A.alloc([4, 128], F32)

            def epilogue_a(h, qi):
                S.dve(lambda e: e.tensor_copy(out=Osb, in_=psB[:, 0:1536]), reads=PB[0:3], writes=[Osb_b])
                for r in range(8):
                    _, off = oreg(r)
                    S.dve(lambda e, r=r, off=off: e.reciprocal(out=rz[:, r:r + 1], in_=Osb[:, off + 128:off + 129]), reads=[Osb_b], writes=[ep_b])
                S.dve(lambda e: e.tensor_scalar(out=rz[:, 8:12], in0=rz[:, 4:8], scalar1=neglam, scalar2=None, op0=ALU.mult), reads=[ep_b, misc_b], writes=[ep_b])
                for qs in range(4):
                    _, off0 = oreg(qs)
                    _, off1 = oreg(4 + qs)
                    S.dve(lambda e, qs=qs, off0=off0: e.tensor_scalar(out=dd4[:, qs, :], in0=Osb[:, off0:off0 + 128], scalar1=rz[:, qs:qs + 1], scalar2=None, op0=ALU.mult), reads=[Osb_b, ep_b], writes=[ep_b])
                    S.dve(lambda e, qs=qs, off1=off1: e.scalar_tensor_tensor(out=dd4[:, qs, :], in0=Osb[:, off1:off1 + 128], scalar=rz[:, 8 + qs:9 + qs], in1=dd4[:, qs, :], op0=ALU.mult, op1=ALU.add), reads=[Osb_b, ep_b], writes=[ep_b])
                    S.dve(lambda e, qs=qs: e.scalar_tensor_tensor(out=dj, in0=dd4[:, qs, :], scalar=1.0, in1=dd4[:, qs, :], op0=ALU.mult, op1=ALU.mult, accum_out=ss4[:, qs:qs + 1]), reads=[ep_b], writes=[ep_b])
                S.act(lambda e: e.activation(out=rz[:, 12:16], in_=ss4[:, 0:4], func=AF.Ln, scale=1.0 / 128, bias=epsc), reads=[ep_b, misc_b], writes=[ep_b])
                S.act(lambda e: e.activation(out=ss4[:, 4:8], in_=rz[:, 12:16], func=AF.Exp, scale=-0.5), reads=[ep_b], writes=[ep_b])
                for qs in range(4):
                    S.dve(lambda e, qs=qs: e.scalar_tensor_tensor(out=hn[:, qs * 128:(qs + 1) * 128], in0=dd4[:, qs, :], scalar=ss4[:, 4 + qs:5 + qs], in1=swb, op0=ALU.mult, op1=ALU.mult), reads=[ep_b, swb_b], writes=[hn_b])

            def epilogue_b(h, qi):
                for qs in range(4):
                    S.pe(lambda e, qs=qs: e.matmul(pb(3)[:, qs * 128:(qs + 1) * 128], lhsT=hn[:, qs * 128:(qs + 1) * 128], rhs=ident, start=True, stop=True), reads=[hn_b, ident_b], writes=[PB[3]])
                hi = (h * NQ + qi) % 2
                S.dve(lambda e, hi=hi: e.tensor_copy(out=hst[hi], in_=pb(3)), reads=[PB[3]], writes=[hst_b[hi]])
                S.dma("sp", HTs[h, :, qi * 512:(qi + 1) * 512], hst[hi], reads=[hst_b[hi]], writes=[HTs_b[h][qi]])

            steps = [(h, qi, kt) for h in range(8) for qi in range(NQ) for kt in range(NKT)]
            load_head(0)
            pending = None
            for n, (h, qi, kt) in enumerate(steps):
                emit_S(n, h, qi, kt)
                if n >= 1:
                    ph, pqi, pkt = steps[n - 1]
                    emit_PV(n - 1, ph, pqi, pkt)
                    if pkt == NKT - 1:
                        epilogue_a(ph, pqi)
                        pending = (n + 6, ph, pqi)
                if pending is not None and n >= pending[0]:
                    epilogue_b(pending[1], pending[2])
                    pending = None
                if qi == 0 and kt == 0 and h + 1 < 8:
                    load_head(h + 1)
            ph, pqi, pkt = steps[-1]
            emit_PV(len(steps) - 1, ph, pqi, pkt)
            if pending is not None:
                epilogue_b(pending[1], pending[2])
            epilogue_a(ph, pqi)
            epilogue_b(ph, pqi)
            A.release()
            S.barrier()
            if stop == 2:
                raise _Stop()

            A.mark()
            x1 = A.alloc([4, D], F32); x1_b = [Buf() for _ in range(4)]
            sqj = A.alloc([D], BF16); sqj_b = Buf()
            xs = [A.alloc([D], BF16) for _ in range(2)]; xs_b = [Buf() for _ in range(2)]
            st8 = [A.alloc([8], F32) for _ in range(2)]; st8_b = [Buf() for _ in range(2)]
            hT = A.alloc([16, 512], BF16); hT_b = [Buf() for _ in range(4)]
            pw = A.alloc([4, 2, 256], BF16); pw_b = Buf()
            wst = [A.alloc([16, 512], BF16) for _ in range(3)]; wst_b = [Buf() for _ in range(3)]
            A.mark()
            ut = A.alloc([6, 1024], BF16); ut_b = [Buf() for _ in range(6)]
            dT = A.alloc([8, 512], BF16); dT_b = Buf()
            hdT = A.alloc([8, 512], BF16); hdT_b = Buf()
            yT = A.alloc([8, 512], BF16); yT_b = Buf()
            mixT = A.alloc([16, 512], BF16); mixT_b = Buf()
            gsg = [A.alloc([512], F32) for _ in range(4)]; gsg_b = [Buf() for _ in range(4)]
            mix_end = A.off
            A.release()
            A.mark()
            hidT = A.alloc([32, 512], BF16); hidT_b = Buf()
            rl = [A.alloc([512], F32) for _ in range(2)]; rl_b = [Buf() for _ in range(2)]
            ffn_end = A.off
            A.release()
            A.off = max(mix_end, ffn_end)
            A.peak = max(A.peak, A.off)
            region_all = ut_b + [dT_b, hdT_b, yT_b, mixT_b] + gsg_b + [hidT_b] + rl_b
            dummy = A.alloc([1], F32)

            def region_switch():
                S.dve(lambda e: e.memset(dummy, 0.0), writes=region_all)

            for g in range(4):
                S.dma("pool", pw[:, g, :, :], pool_w[g].rearrange("(k p) n -> p k n", p=128), writes=[pw_b], accumulate=True)

            wn = [0]

            def wload(src_ap, kcn, key):
                i = wn[0] % 3; wn[0] += 1
                S.dma("sp", wst[i][:, 0:kcn, :], src_ap.rearrange("(k p) n -> p k n", p=128), reads=[pre[key]], writes=[wst_b[i]])
                return wst[i], wst_b[i]

            evn = [0]

            for s in range(NS):
                t0 = s * 512
                region_switch()
                ffn_guard = []
                S.dma("sp", hdT, HTs[:, :, t0:t0 + 512].rearrange("h p t -> p h t"), reads=[HTs_b[h][s] for h in range(8)] + ffn_guard, writes=[hdT_b])
                for j in range(6):
                    S.dma("sp", ut[:, j, :], Us[(s * 4 + j) * 128:(s * 4 + j + 1) * 128, :], reads=[Us_b[s * 4 + j]] + ffn_guard, writes=[ut_b[j]])
                for j in range(4):
                    S.dma("sp", x1[:, j, :], xo[t0 + j * 128:t0 + (j + 1) * 128, :], writes=[x1_b[j]])
                for j in range(4):
                    prep(None, A1, B1, hT[:, :, j * 128:(j + 1) * 128], hT_b[j], xt=x1[:, j, :], xt_b=x1_b[j])
                for c in range(8):
                    g = c // 2
                    bank_i = c % 4
                    for j in range(4):
                        gt = s * 4 + j
                        cls = 0 if gt == 0 else (2 if gt == NT - 1 else 1)
                        for n in range(3):
                            bidx = (cls * 4 + g) * 3 + n
                            S.pe(lambda e, c=c, j=j, n=n, bidx=bidx, bank_i=bank_i: e.matmul(pa(bank_i)[:, j * 128:(j + 1) * 128], lhsT=ut[:, j + n, c * 128:(c + 1) * 128], rhs=bands[:, bidx, :], start=(n == 0), stop=(n == 2)),
                                 reads=[ut_b[j + n], bands_b], writes=[PA[bank_i]])
                    S.act(lambda e, c=c, bank_i=bank_i: e.activation(out=dT[:, c, :], in_=pa(bank_i), func=AF.Copy), reads=[PA[bank_i]] + ffn_guard, writes=[dT_b])
                for oc8 in range(8):
                    g = oc8 // 2; oc = oc8 % 2
                    bank_i = oc8 % 4
                    for ic in range(2):
                        S.pe(lambda e, g=g, oc=oc, ic=ic, bank_i=bank_i: e.matmul(pa(bank_i), lhsT=pw[:, g, ic, oc * 128:(oc + 1) * 128], rhs=dT[:, g * 2 + ic, :], start=(ic == 0), stop=(ic == 1)),
                             reads=[pw_b, dT_b], writes=[PA[bank_i]])
                    S.act(lambda e, oc8=oc8, bank_i=bank_i: e.activation(out=yT[:, oc8, :], in_=pa(bank_i), func=AF.Copy, scale=pscale[:, oc8:oc8 + 1]), reads=[PA[bank_i], pscale_b] + ffn_guard, writes=[yT_b])
                for mg in range(4):
                    iab = wn[0] % 3; wn[0] += 1
                    wa, wa_b = wst[iab], wst_b[iab]
                    S.dma("sp", wa[:, 0:8, :], Wa_s[:, mg * 512:(mg + 1) * 512].rearrange("(k p) n -> p k n", p=128), reads=[pre[("a", mg)]], writes=[wa_b])
                    S.dma("sp", wa[:, 8:16, :], Wb_s[:, mg * 512:(mg + 1) * 512].rearrange("(k p) n -> p k n", p=128), reads=[pre[("b", mg)]], writes=[wa_b], accumulate=True)
                    wbu, wbu_b = wa, wa_b
                    wga, wga_b = wload(Wg_s[:, mg * 512:(mg + 1) * 512], 16, ("g", mg))
                    wgb, wgb_b = wload(Wg_s[:, (4 + mg) * 512:(5 + mg) * 512], 16, ("g", 4 + mg))
                    for mi in range(4):
                        m = mg * 4 + mi
                        cs = slice(mi * 128, (mi + 1) * 128)
                        for kc in range(8):
                            S.pe(lambda e, kc=kc, cs=cs, wa=wa: e.matmul(pa(0), lhsT=wa[:, kc, cs], rhs=hdT[:, kc, :], start=(kc == 0), stop=(kc == 7)), reads=[wa_b, hdT_b], writes=[PA[0]])
                        for kc in range(8):
                            S.pe(lambda e, kc=kc, cs=cs, wbu=wbu: e.matmul(pa(1), lhsT=wbu[:, 8 + kc, cs], rhs=yT[:, kc, :], start=(kc == 0), stop=(kc == 7)), reads=[wbu_b, yT_b], writes=[PA[1]])
                        for kc in range(KD):
                            S.pe(lambda e, kc=kc, cs=cs, wga=wga: e.matmul(pa(2), lhsT=wga[:, kc, cs], rhs=hT[:, kc, :], start=(kc == 0), stop=(kc == KD - 1)), reads=[wga_b] + hT_b, writes=[PA[2]])
                        for kc in range(KD):
                            S.pe(lambda e, kc=kc, cs=cs, wgb=wgb: e.matmul(pa(3), lhsT=wgb[:, kc, cs], rhs=hT[:, kc, :], start=(kc == 0), stop=(kc == KD - 1)), reads=[wgb_b] + hT_b, writes=[PA[3]])
                        gi = (evn[0] % 2) * 2; evn[0] += 1
                        S.act(lambda e, gi=gi: e.activation(out=gsg[gi], in_=pa(2), func=AF.Sigmoid), reads=[PA[2]] + ffn_guard, writes=[gsg_b[gi]])
                        S.act(lambda e, gi=gi: e.activation(out=gsg[gi + 1], in_=pa(3), func=AF.Sigmoid), reads=[PA[3]] + ffn_guard, writes=[gsg_b[gi + 1]])
                        S.dve(lambda e, gi=gi: e.tensor_tensor(out=gsg[gi], in0=gsg[gi], in1=pa(0), op=ALU.mult), reads=[PA[0], gsg_b[gi]], writes=[gsg_b[gi]])
                        S.dve(lambda e, gi=gi: e.tensor_tensor(out=gsg[gi + 1], in0=gsg[gi + 1], in1=pa(1), op=ALU.mult), reads=[PA[1], gsg_b[gi + 1]], writes=[gsg_b[gi + 1]])
                        S.dve(lambda e, gi=gi, m=m: e.tensor_tensor(out=mixT[:, m, :], in0=gsg[gi], in1=gsg[gi + 1], op=ALU.add), reads=[gsg_b[gi], gsg_b[gi + 1]] + ffn_guard, writes=[mixT_b])
                for ng in range(4):
                    wo, wo_b = wload(Wo_s[:, ng * 512:(ng + 1) * 512], 16, ("o", ng))
                    for j in range(4):
                        for kc in range(KD):
                            S.pe(lambda e, kc=kc, j=j, wo=wo: e.matmul(pa(j), lhsT=mixT[:, kc, j * 128:(j + 1) * 128], rhs=wo[:, kc, :], start=(kc == 0), stop=(kc == KD - 1)), reads=[mixT_b, wo_b], writes=[PA[j]])
                    for j in range(4):
                        ri = evn[0] % 2; evn[0] += 1
                        S.dve(lambda e, j=j, ng=ng, ri=ri: e.tensor_tensor(out=gsg[ri], in0=pa(j), in1=GArow[:, ng * 512:(ng + 1) * 512], op=ALU.mult), reads=[PA[j], grow_b], writes=[gsg_b[ri]])
                        S.dve(lambda e, j=j, ng=ng, ri=ri: e.tensor_tensor(out=x1[:, j, ng * 512:(ng + 1) * 512], in0=x1[:, j, ng * 512:(ng + 1) * 512], in1=gsg[ri], op=ALU.add), reads=[gsg_b[ri], x1_b[j]], writes=[x1_b[j]])
                for j in range(4):
                    prep(None, A2, B2, hT[:, :, j * 128:(j + 1) * 128], hT_b[j], xt=x1[:, j, :], xt_b=x1_b[j])
                region_switch()
                mix_guard = []
                for half in range(2):
                    for fg in range(8):
                        col0 = half * 4096 + fg * 512
                        w1, w1_b = wload(W1_s[:, col0:col0 + 512], 16, ("1", half * 8 + fg))
                        for mi in range(4):
                            jh = fg * 4 + mi
                            bank_i = mi
                            for kc in range(KD):
                                S.pe(lambda e, kc=kc, mi=mi, w1=w1, bank_i=bank_i: e.matmul(pa(bank_i), lhsT=w1[:, kc, mi * 128:(mi + 1) * 128], rhs=hT[:, kc, :], start=(kc == 0), stop=(kc == KD - 1)), reads=[w1_b] + hT_b, writes=[PA[bank_i]])
                            ri = evn[0] % 2; evn[0] += 1
                            S.act(lambda e, ri=ri, bank_i=bank_i: e.activation(out=rl[ri], in_=pa(bank_i), func=AF.Relu), reads=[PA[bank_i]] + mix_guard, writes=[rl_b[ri]])
                            S.dve(lambda e, ri=ri, jh=jh: e.tensor_tensor(out=hidT[:, jh, :], in0=rl[ri], in1=rl[ri], op=ALU.mult), reads=[rl_b[ri]] + mix_guard, writes=[hidT_b])
                    for ng in range(4):
                        for kg in range(2):
                            r0 = half * 4096 + kg * 2048
                            w2, w2_b = wload(W2_s[r0:r0 + 2048, ng * 512:(ng + 1) * 512], 16, ("2", half * 2 + kg, ng))
                            for j in range(4):
                                for kc in range(KD):
                                    kk = kg * 16 + kc
                                    S.pe(lambda e, kc=kc, kk=kk, j=j, w2=w2, kg=kg: e.matmul(pa(j), lhsT=hidT[:, kk, j * 128:(j + 1) * 128], rhs=w2[:, kc, :], start=(kk == 0), stop=(kk == 31)), reads=[hidT_b, w2_b], writes=[PA[j]])
                        for j in range(4):
                            ri = evn[0] % 2; evn[0] += 1
                            S.dve(lambda e, j=j, ng=ng, ri=ri: e.tensor_tensor(out=rl[ri], in0=pa(j), in1=GMrow[:, ng * 512:(ng + 1) * 512], op=ALU.mult), reads=[PA[j], grow_b], writes=[rl_b[ri]])
                            S.dve(lambda e, j=j, ng=ng, ri=ri: e.tensor_tensor(out=x1[:, j, ng * 512:(ng + 1) * 512], in0=x1[:, j, ng * 512:(ng + 1) * 512], in1=rl[ri], op=ALU.add), reads=[rl_b[ri], x1_b[j]], writes=[x1_b[j]])
                for j in range(4):
                    S.dma("sp", y[t0 + j * 128:t0 + (j + 1) * 128, :], x1[:, j, :], reads=[x1_b[j]], writes=[Buf()])
            A.release()
        except _Stop:
            pass
        finals = [o for o in S.dma_pool["sp"]["last"] if o is not None]
        S.op("sp", lambda e: e.nop(), deps=finals)
        block = st.enter_context(nc.Block())
        stats = S.emit(block)
        if debug:
            print("ops/waits:", stats, "sems:", S.nsem, "arena peak:", A.peak)
    return nc


def rope_tables(tok_idx, n_ctx):
    half = 32
    inv_freq = (10000.0 ** (-np.arange(0, half, 2, dtype=np.float32) / half)).astype(np.float32)
    row = (tok_idx // GRID_W).astype(np.float32)
    col = (tok_idx % GRID_W).astype(np.float32)
    out = np.zeros((len(tok_idx) + n_ctx, 128), np.float32)
    for hh, pos in enumerate((row, col)):
        ang = (pos[:, None] * inv_freq[None, :]).astype(np.float32)
        c, s_ = np.cos(ang), np.sin(ang)
        for xi in range(2):
            out[:len(tok_idx), hh * 32 + xi * 16: hh * 32 + xi * 16 + 16] = c
            out[:len(tok_idx), 64 + hh * 32 + xi * 16: 64 + hh * 32 + xi * 16 + 16] = (-s_ if xi == 0 else s_)
    out[len(tok_idx):, 0:64] = 1.0
    return out


def band_tables(L, s):
    T = L // 2
    out = np.zeros((128, 3, 4, 3, 128), np.float32)
    tiles = {0: s * (T // 128), 1: s * (T // 128) + 1, 2: (s + 1) * (T // 128) - 1}
    for cls, gt in tiles.items():
        for g, w in enumerate(POOL_WINDOWS):
            for tl in range(128):
                t = gt * 128 + tl
                lo = min(max(t - w // 2, 0), L)
                hi = min(max(t + w - w // 2, 0), L)
                for tp in range(lo, hi):
                    n = tp // 128 - gt + 1
                    out[tp % 128, cls, g, n, tl] += 1.0 / (hi - lo)
                out[tl, cls, g, 1, tl] -= 1.0
    return out.reshape(128, 36 * 128).astype(ml_dtypes.bfloat16)


_NC_CACHE = {}


def run(inputs, L, C, B, debug=False, stop=99):
    key = (L, C)
    if key not in _NC_CACHE:
        _NC_CACHE[key] = build(L, C, debug=debug, stop=stop)
    nc = _NC_CACHE[key]
    T = L // 2
    f = lambda a: np.ascontiguousarray(np.asarray(a, dtype=np.float32))
    x = f(inputs["x"]); c = f(inputs["c"]); ctx = f(inputs["ctx"])
    shared = {"cctx": f(inputs["c_ctx"]), "ident": np.eye(128, dtype=np.float32).astype(ml_dtypes.bfloat16)}
    for k in ("w_mod", "b_mod", "norm_attn_w", "w_in", "q_norm_w", "k_norm_w", "lambda_q1", "lambda_k1", "lambda_q2",
              "lambda_k2", "subln_w", "pool_w", "pool_scale", "w_a_up", "w_b_up", "w_o", "norm_mlp_w", "w_ff1", "w_ff2"):
        shared[k] = f(inputs[k])[0]
    in_maps = []
    zeros128 = np.zeros((128, D), np.float32)
    for b in range(B):
        for s in range(2):
            own = slice(s * T, (s + 1) * T)
            oth = slice((1 - s) * T, (2 - s) * T)
            m = dict(shared)
            m["xo"] = x[b, own]
            m["xr"] = x[b, oth]
            before = x[b, s * T - 128:s * T] if s == 1 else zeros128
            after = x[b, (s + 1) * T:(s + 1) * T + 128] if s == 0 else zeros128
            m["xh"] = np.ascontiguousarray(np.concatenate([before, after], 0))
            m["ctx"] = ctx[b]
            m["cvec"] = c[b]
            tok = np.concatenate([np.arange(s * T, (s + 1) * T), np.arange((1 - s) * T, (2 - s) * T)])
            m["rope"] = rope_tables(tok, C).reshape(-1, 128, 128)
            m["bands"] = band_tables(L, s)
            in_maps.append(m)
    res = run_bass_kernel_spmd(nc, in_maps, core_ids=list(range(2 * B)))
    out = np.zeros((B, L, D), np.float32)
    for b in range(B):
        for s in range(2):
            out[b, s * T:(s + 1) * T] = res.results[b * 2 + s]["y"]
    if debug:
        return out, res.results
    return out


def kernel(**inputs):
    return run(inputs, 8192, 256, 4)
```
